# Optimizing a Trainium2 kernel written in Bass

```python
import math
import jax, jax.numpy as jnp
from jax import lax
import numpy as np

D_MODEL = 1024
BATCH = 8
SEQ = 2048
DEPTH = 1
DEC_BATCH = 32
DEC_SEQ = 4
PAST_LEN = 16384
PAGE_SIZE = 128

HEAD_DIM = 64
N_HEADS_A = 8
N_KV_A = 2
N_IDX_HEADS = 8
D_IDX = 64
TOPK_MAX = 256
N_HEADS_B = 8
D_A = N_HEADS_A * HEAD_DIM
D_B = N_HEADS_B * HEAD_DIM
D_MIX = D_A + D_B
D_PLE = 256
N_BUCKETS = 32
MAX_DISTANCE = 128
ROPE_BASE = 10000.0
RET_CHUNK = 128
Q_BLOCK = 128
EPS = 1e-6
SPLIT_SIZES = (D_A, N_KV_A * HEAD_DIM, N_KV_A * HEAD_DIM, N_IDX_HEADS * D_IDX, D_IDX, N_IDX_HEADS, D_A, D_B, D_B, D_B, D_B)
D_IN = D_A + 2 * N_KV_A * HEAD_DIM + N_IDX_HEADS * D_IDX + D_IDX + N_IDX_HEADS + D_A + 4 * D_B

kernel_name = 'hymba_dsa_retention_step'


def _rmsnorm(x, g):
    xf = x.astype(jnp.float32)
    y = xf * lax.rsqrt(jnp.mean(xf * xf, axis=-1, keepdims=True) + EPS) * g.astype(jnp.float32)
    return y.astype(x.dtype)


def _head_norm(o):
    mu = jnp.mean(o, axis=-1, keepdims=True)
    var = jnp.mean(jnp.square(o - mu), axis=-1, keepdims=True)
    return (o - mu) * lax.rsqrt(var + EPS)


def _rope(x, pos):
    half = x.shape[-1] // 2
    inv = ROPE_BASE ** (-jnp.arange(half, dtype=jnp.float32) / half)
    ang = pos.astype(jnp.float32)[:, None] * inv[None, :]
    cos = jnp.cos(ang)[:, None, :]
    sin = jnp.sin(ang)[:, None, :]
    xf = x.astype(jnp.float32)
    x1, x2 = xf[..., :half], xf[..., half:]
    return jnp.concatenate([x1 * cos - x2 * sin, x1 * sin + x2 * cos], axis=-1).astype(x.dtype)


def _t5_bucket(n):
    max_exact = N_BUCKETS // 2
    n = jnp.maximum(n, 0)
    nf = jnp.maximum(n, 1).astype(jnp.float32)
    large = max_exact + (jnp.log(nf / max_exact) / math.log(MAX_DISTANCE / max_exact)
                         * (N_BUCKETS - max_exact)).astype(jnp.int32)
    large = jnp.minimum(large, N_BUCKETS - 1)
    return jnp.where(n < max_exact, n, large)


def _project(h, w_in, pos):
    B, T, _ = h.shape
    z = jnp.einsum('btd,de->bte', h, w_in)
    offsets = np.cumsum(SPLIT_SIZES)[:-1].tolist()
    qa, ka, va, qi, ki, wi, ga, qb, kb, vb, gb = jnp.split(z, offsets, axis=-1)
    qa = qa.reshape(B, T, N_HEADS_A, HEAD_DIM)
    ka = ka.reshape(B, T, N_KV_A, HEAD_DIM)
    va = va.reshape(B, T, N_KV_A, HEAD_DIM)
    qi = qi.reshape(B, T, N_IDX_HEADS, D_IDX)
    wi = wi * (N_IDX_HEADS ** -0.5)
    qb = _rope(qb.reshape(B, T, N_HEADS_B, HEAD_DIM), pos)
    kb = _rope(kb.reshape(B, T, N_HEADS_B, HEAD_DIM), pos) * (HEAD_DIM ** -0.5)
    vb = vb.reshape(B, T, N_HEADS_B, HEAD_DIM)
    return qa, ka, va, qi, ki, wi, ga, qb, kb, vb, gb


def _index_select(qi, wi, ki, qpos, kpos, topk):
    s = jnp.einsum('bqhd,bsd->bqhs', qi.astype(jnp.float32), ki.astype(jnp.float32)) * (D_IDX ** -0.5)
    score = jnp.einsum('bqh,bqhs->bqs', wi.astype(jnp.float32), jax.nn.relu(s))
    valid = kpos[None, None, :] <= qpos[None, :, None]
    score = jnp.where(valid, score, -jnp.inf)
    top_val, top_idx = lax.top_k(score, topk)
    return top_idx, jnp.isfinite(top_val)


def _sparse_attend(q, kg, vg, qpos, sel_pos, sel_ok, rel_bias):
    B, Q = q.shape[:2]
    K = sel_pos.shape[-1]
    G = N_HEADS_A // N_KV_A
    qg = q.reshape(B, Q, N_KV_A, G, HEAD_DIM).astype(jnp.float32)
    logits = jnp.einsum('bqngd,bqknd->bqngk', qg, kg.astype(jnp.float32)) * (HEAD_DIM ** -0.5)
    bucket = _t5_bucket(qpos[None, :, None] - sel_pos)
    bias = rel_bias.astype(jnp.float32)[bucket]
    bias = bias.reshape(B, Q, K, N_KV_A, G).transpose(0, 1, 3, 4, 2)
    logits = jnp.where(sel_ok[:, :, None, None, :], logits + bias, -jnp.inf)
    probs = jax.nn.softmax(logits, axis=-1)
    o = jnp.einsum('bqngk,bqknd->bqngd', probs, vg.astype(jnp.float32))
    return o.reshape(B, Q, N_HEADS_A, HEAD_DIM)


def _gather_rows(a, idx):
    return jax.vmap(lambda ab, ib: ab[ib])(a, idx)


def _dsa_prompt(qa, ka, va, qi, ki, wi, rel_bias):
    B, T = qa.shape[:2]
    topk = min(TOPK_MAX, T // 4)
    blk = math.gcd(T, Q_BLOCK)
    kpos = jnp.arange(T)

    def one_block(t0):
        sl = lambda a: lax.dynamic_slice_in_dim(a, t0, blk, axis=1)
        qpos = t0 + jnp.arange(blk)
        idx, ok = _index_select(sl(qi), sl(wi), ki, qpos, kpos, topk)
        kg = _gather_rows(ka, idx)
        vg = _gather_rows(va, idx)
        return _sparse_attend(sl(qa), kg, vg, qpos, idx, ok, rel_bias)

    out = lax.map(one_block, jnp.arange(T // blk) * blk)
    return out.transpose(1, 0, 2, 3, 4).reshape(B, T, N_HEADS_A, HEAD_DIM)


def _dsa_sample(qa, ka, va, qi, ki, wi, cache_k, cache_v, cache_kidx, page_table, rel_bias):
    B, T = qa.shape[:2]
    n_pages = page_table.shape[1]
    past = n_pages * PAGE_SIZE
    L = past + T
    topk = min(TOPK_MAX, L // 4)
    ki_past = cache_kidx[page_table].reshape(B, past, D_IDX).astype(ki.dtype)
    ki_all = jnp.concatenate([ki_past, ki], axis=1)
    qpos = past + jnp.arange(T)
    kpos = jnp.arange(L)
    idx, ok = _index_select(qi, wi, ki_all, qpos, kpos, topk)
    is_past = idx < past
    pidx = jnp.minimum(idx, past - 1)
    phys_page = jnp.take_along_axis(page_table, (pidx // PAGE_SIZE).reshape(B, -1), axis=1).reshape(idx.shape)
    phys_row = phys_page * PAGE_SIZE + pidx % PAGE_SIZE
    nidx = jnp.clip(idx - past, 0, T - 1)
    flat_k = cache_k.reshape(-1, N_KV_A, HEAD_DIM)
    flat_v = cache_v.reshape(-1, N_KV_A, HEAD_DIM)
    kg = jnp.where(is_past[..., None, None], flat_k[phys_row].astype(ka.dtype), _gather_rows(ka, nidx))
    vg = jnp.where(is_past[..., None, None], flat_v[phys_row].astype(va.dtype), _gather_rows(va, nidx))
    return _sparse_attend(qa, kg, vg, qpos, idx, ok, rel_bias)


def _retention(q, k, v, state0):
    B, T, H, dk = q.shape
    dv = v.shape[-1]
    C = math.gcd(T, RET_CHUNK)
    n = T // C
    log_g = jnp.log1p(-jnp.exp2(-5.0 - jnp.arange(H, dtype=jnp.float32)))
    i = jnp.arange(C, dtype=jnp.float32)
    diff = i[:, None] - i[None, :]
    dmat = jnp.where(diff >= 0, jnp.exp(jnp.maximum(diff, 0.0)[None] * log_g[:, None, None]), 0.0)
    dec_in = jnp.exp((i + 1.0)[:, None] * log_g[None, :])
    dec_out = jnp.exp((C - 1.0 - i)[:, None] * log_g[None, :])
    dec_chunk = jnp.exp(C * log_g)

    def to_chunks(a):
        return a.astype(jnp.float32).reshape(B, n, C, H, a.shape[-1]).transpose(1, 0, 2, 3, 4)

    def step(S, inp):
        qc, kc, vc = inp
        inter = jnp.einsum('bchk,bhkv->bchv', qc * dec_in[None, :, :, None], S)
        scores = jnp.einsum('bchk,bshk->bhcs', qc, kc) * dmat[None]
        intra = jnp.einsum('bhcs,bshv->bchv', scores, vc)
        S = S * dec_chunk[None, :, None, None] + jnp.einsum('bchk,bchv->bhkv', kc * dec_out[None, :, :, None], vc)
        return S, inter + intra

    S, ys = lax.scan(step, state0.astype(jnp.float32), (to_chunks(q), to_chunks(k), to_chunks(v)))
    o = ys.transpose(1, 0, 2, 3, 4).reshape(B, T, H, dv)
    return o, S


def _merge(x, oa, ob, ga, gb, p, w_out, g_post, w_ple_up, w_ple_gate):
    B, T, _ = x.shape
    ya = jax.nn.silu(ga) * oa.reshape(B, T, D_A).astype(x.dtype)
    yb = jax.nn.silu(gb) * _head_norm(ob).reshape(B, T, D_B).astype(x.dtype)
    y = jnp.einsum('bte,ed->btd', jnp.concatenate([ya, yb], axis=-1), w_out)
    x = x + _rmsnorm(y, g_post)
    ple = jnp.einsum('btp,pd->btd', p.astype(x.dtype), w_ple_up)
    gate = jax.nn.sigmoid(jnp.einsum('btd,de->bte', x, w_ple_gate))
    return x + ple * gate


def setup_inputs(seed: int = 0) -> dict:
    key = jax.random.key(seed)
    ks = jax.random.split(key, 20)
    f32 = jnp.float32
    n_pages = PAST_LEN // PAGE_SIZE
    n_used = DEC_BATCH * n_pages
    n_pool = n_used + max(1, n_used // 4)
    perm = jax.random.permutation(ks[0], n_pool)[:n_used]
    page_table = perm.reshape(DEC_BATCH, n_pages).astype(jnp.int32)
    nrm = lambda k, shape, s: jax.random.normal(k, shape, f32) * s
    return {
        'x_prompt': nrm(ks[1], (BATCH, SEQ, D_MODEL), 1.0),
        'x_sample': nrm(ks[2], (DEC_BATCH, DEC_SEQ, D_MODEL), 1.0),
        'cache_k': nrm(ks[3], (DEPTH, n_pool, PAGE_SIZE, N_KV_A, HEAD_DIM), 1.0),
        'cache_v': nrm(ks[4], (DEPTH, n_pool, PAGE_SIZE, N_KV_A, HEAD_DIM), 1.0),
        'cache_kidx': nrm(ks[5], (DEPTH, n_pool, PAGE_SIZE, D_IDX), 1.0),
        'state_ret': nrm(ks[6], (DEPTH, DEC_BATCH, N_HEADS_B, HEAD_DIM, HEAD_DIM), 0.5),
        'page_table': page_table,
        'p_prompt': nrm(ks[7], (DEPTH, BATCH, SEQ, D_PLE), 1.0),
        'p_sample': nrm(ks[8], (DEPTH, DEC_BATCH, DEC_SEQ, D_PLE), 1.0),
        'rel_bias': nrm(ks[9], (N_BUCKETS, N_HEADS_A), 0.1),
        'w_in': nrm(ks[10], (DEPTH, D_MODEL, D_IN), D_MODEL ** -0.5),
        'w_out': nrm(ks[11], (DEPTH, D_MIX, D_MODEL), D_MIX ** -0.5),
        'g_pre': 1.0 + nrm(ks[12], (DEPTH, D_MODEL), 0.01),
        'g_post': 1.0 + nrm(ks[13], (DEPTH, D_MODEL), 0.01),
        'w_ple_up': nrm(ks[14], (DEPTH, D_PLE, D_MODEL), D_PLE ** -0.5),
        'w_ple_gate': nrm(ks[15], (DEPTH, D_MODEL, D_MODEL), D_MODEL ** -0.5),
    }


def reference(x_prompt, x_sample, cache_k, cache_v, cache_kidx, state_ret, page_table, p_prompt, p_sample,
              rel_bias, w_in, w_out, g_pre, g_post, w_ple_up, w_ple_gate):
    B, T = x_prompt.shape[:2]
    Ts = x_sample.shape[1]
    pos_p = jnp.arange(T)
    pos_s = PAST_LEN + jnp.arange(Ts)
    xp, xs = x_prompt, x_sample
    kp_l, vp_l, ip_l, sp_l = [], [], [], []
    ks_l, vs_l, is_l, ss_l = [], [], [], []
    for i in range(DEPTH):
        hp = _rmsnorm(xp, g_pre[i])
        qa, ka, va, qi, ki, wi, ga, qb, kb, vb, gb = _project(hp, w_in[i], pos_p)
        oa = _dsa_prompt(qa, ka, va, qi, ki, wi, rel_bias)
        ob, sp = _retention(qb, kb, vb, jnp.zeros((B, N_HEADS_B, HEAD_DIM, HEAD_DIM), jnp.float32))
        xp = _merge(xp, oa, ob, ga, gb, p_prompt[i], w_out[i], g_post[i], w_ple_up[i], w_ple_gate[i])
        kp_l.append(ka)
        vp_l.append(va)
        ip_l.append(ki)
        sp_l.append(sp.astype(state_ret.dtype))
        hs = _rmsnorm(xs, g_pre[i])
        qa, ka, va, qi, ki, wi, ga, qb, kb, vb, gb = _project(hs, w_in[i], pos_s)
        oa = _dsa_sample(qa, ka, va, qi, ki, wi, cache_k[i], cache_v[i], cache_kidx[i], page_table, rel_bias)
        ob, ss = _retention(qb, kb, vb, state_ret[i])
        xs = _merge(xs, oa, ob, ga, gb, p_sample[i], w_out[i], g_post[i], w_ple_up[i], w_ple_gate[i])
        ks_l.append(ka)
        vs_l.append(va)
        is_l.append(ki)
        ss_l.append(ss.astype(state_ret.dtype))
    k_prompt = jnp.stack(kp_l)
    v_prompt = jnp.stack(vp_l)
    kidx_prompt = jnp.stack(ip_l)
    ret_state_prompt = jnp.stack(sp_l)
    k_sample = jnp.stack(ks_l)
    v_sample = jnp.stack(vs_l)
    kidx_sample = jnp.stack(is_l)
    ret_state_sample = jnp.stack(ss_l)
    return (xp, xs, k_prompt, v_prompt, kidx_prompt, ret_state_prompt, k_sample, v_sample, kidx_sample, ret_state_sample)
```

```python
import math
import contextlib
import numpy as np
import concourse.bass as bass
import concourse.mybir as mybir
from concourse.bass_utils import run_bass_kernel_spmd

F32 = mybir.dt.float32
BF16 = mybir.dt.bfloat16
I32 = mybir.dt.int32
ALU = mybir.AluOpType
AF = mybir.ActivationFunctionType
AX = mybir.AxisListType

D = 1024
T = 2048
NT = 16
DIN = 3912
EPS = 1e-6
NIT = 16
TOPK = 256
NPAGE = 128
NPOOL = 5120
PAST = 16384
O_QA, O_KA, O_VA, O_QI, O_KI, O_WI, O_GA, O_QB, O_KB, O_VB, O_GB = (
    0, 512, 640, 768, 1280, 1344, 1352, 1864, 2376, 2888, 3400)
NEG = -1.0e30
MB = 240000.0


class Prog:
    def __init__(self, nc, es):
        self.nc = nc
        self.es = es
        self.eng = {'pe': nc.tensor, 'act': nc.scalar, 'dve': nc.vector, 'pool': nc.gpsimd, 'sp': nc.sync}
        self.sem = {e: es.enter_context(nc.semaphore('s_' + e)) for e in ('pe', 'act', 'dve', 'pool')}
        self.cnt = {e: 0 for e in self.sem}
        self.waited = {e: {} for e in self.eng}
        self.last_w = {}
        self.readers = {}
        nd = {'sp': 8, 'pool': 6, 'act': 2}
        self.dsem = {}
        self.dcnt = {}
        self.drr = {q: 0 for q in nd}
        self.nd = nd
        for q, n in nd.items():
            for j in range(n):
                self.dsem[(q, j)] = es.enter_context(nc.semaphore('d_%s%d' % (q, j)))
                self.dcnt[(q, j)] = 0
        self.nbank = 6
        self.ring = None
        self.brr = 0
        self.banks = [es.enter_context(nc.psum_tensor('psb%d' % b, [128, 512], F32)) for b in range(8)]

    def bank(self):
        ring = self.ring if self.ring is not None else list(range(self.nbank))
        b = ring[self.brr % len(ring)]
        self.brr = (self.brr + 1) % len(ring)
        return self.banks[b], 'ps%d' % b

    def _wait(self, eng, tok):
        kind, s, v = tok
        if kind == 'E' and s == eng and eng == 'pe':
            return
        key = (kind, s)
        if self.waited[eng].get(key, 0) >= v:
            return
        self.waited[eng][key] = v
        semobj = self.sem[s] if kind == 'E' else self.dsem[s]
        self.eng[eng].wait_ge(semobj, v)

    def _deps(self, eng, reads, writes):
        deps = []
        for k in reads:
            if k in self.last_w:
                deps.append(self.last_w[k])
            if k.startswith('ps'):
                deps.extend(t for t in self.readers.get(k, {}).values() if not (t[0] == 'E' and t[1] == eng))
        for k in writes:
            if k in self.last_w:
                deps.append(self.last_w[k])
            deps.extend(self.readers.get(k, {}).values())
        for tok in deps:
            self._wait(eng, tok)

    def _reg(self, tok, reads, writes):
        rk = (tok[0], tok[1])
        for k in reads:
            self.readers.setdefault(k, {})[rk] = tok
        for k in writes:
            self.last_w[k] = tok
            self.readers[k] = {}

    def op(self, eng, fn, reads=(), writes=()):
        self._deps(eng, reads, writes)
        ins = fn(self.eng[eng])
        self.cnt[eng] += 1
        ins.then_inc(self.sem[eng], 1)
        tok = ('E', eng, self.cnt[eng])
        self._reg(tok, reads, writes)
        return tok

    def dma(self, q, out, in_, reads=(), writes=(), gather_idx=None):
        self._deps(q, reads, writes)
        j = self.drr[q]
        self.drr[q] = (j + 1) % self.nd[q]
        prev = self.dcnt[(q, j)]
        if prev > 0:
            self._wait(q, ('D', (q, j), prev))
        if gather_idx is None:
            ins = self.eng[q].dma_start(out=out, in_=in_)
        else:
            ins = self.eng[q].indirect_dma_start(
                out=out, out_offset=None, in_=in_,
                in_offset=bass.IndirectOffsetOnAxis(ap=gather_idx, axis=0))
        self.dcnt[(q, j)] = prev + 16
        ins.then_inc(self.dsem[(q, j)], 16)
        tok = ('D', (q, j), prev + 16)
        self._reg(tok, reads, writes)
        return tok

    def alias(self, new, old):
        if old in self.last_w:
            self.last_w[new] = self.last_w[old]
        self.readers[new] = dict(self.readers.get(old, {}))

    def finish(self):
        for (q, j), v in self.dcnt.items():
            if v > 0:
                self._wait(q, ('D', (q, j), v))
        for e in self.eng:
            for s in self.sem:
                if s != e and self.cnt[s] > 0:
                    self._wait(e, ('E', s, self.cnt[s]))


def _consts():
    c = {}
    half = 32
    inv = np.power(np.float32(10000.0), -np.arange(half, dtype=np.float32) / np.float32(half)).astype(np.float32)

    def cs(pos):
        ang = pos.astype(np.float32)[:, None] * inv[None, :]
        return np.cos(ang).astype(np.float32), np.sin(ang).astype(np.float32)
    cp, sp_ = cs(np.arange(T))
    c['cosp'] = cp
    c['sinp'] = sp_
    pos_s = np.tile(PAST + np.arange(4), 4)
    c_s, s_s = cs(pos_s)
    c['coss'] = c_s
    c['sins'] = s_s
    h = np.arange(8, dtype=np.float32)
    log_g = np.log1p(-np.exp2(-5.0 - h)).astype(np.float64)
    i = np.arange(128, dtype=np.float64)
    c['dinp'] = np.exp((i + 1.0)[:, None] * log_g[None, :]).astype(np.float32)
    c['doutp'] = (np.exp((127.0 - i)[:, None] * log_g[None, :]) * 0.125).astype(np.float32)
    dt = np.exp(-(i + 1.0)[:, None, None] * log_g[None, :, None]) * (i[None, None, :] >= i[:, None, None])
    c['dtp'] = dt.astype(np.float32).reshape(128, 1024)
    dc = np.zeros((128, 4, 64), np.float64)
    for hp in range(2):
        for hh in range(4):
            dc[hp * 64:(hp + 1) * 64, hh, :] = np.exp(128.0 * log_g[2 * hh + hp])
    c['dcp'] = dc.astype(np.float32).reshape(128, 256)
    tt = np.tile(np.arange(4, dtype=np.float64), 4)
    bb = np.repeat(np.arange(4), 4)
    c['dins'] = np.exp((tt + 1.0)[:, None] * log_g[None, :]).astype(np.float32)
    c['douts'] = (np.exp((3.0 - tt)[:, None] * log_g[None, :]) * 0.125).astype(np.float32)
    dts = np.exp(-(tt + 1.0)[:, None, None] * log_g[None, :, None]) * \
        ((tt[None, None, :] >= tt[:, None, None]) & (bb[None, None, :] == bb[:, None, None]))
    c['dts'] = dts.astype(np.float32).reshape(16, 128)
    dcs = np.zeros((128, 4, 64), np.float64)
    for hp in range(2):
        for hh in range(4):
            dcs[hp * 64:(hp + 1) * 64, hh, :] = np.exp(4.0 * log_g[2 * hh + hp])
    c['dcs'] = dcs.astype(np.float32).reshape(128, 256)
    bm = np.zeros((16, 4), np.float32)
    bm[np.arange(16), bb] = 1.0
    c['bmask'] = bm
    q = np.arange(128)
    c['negtri'] = np.where(q[None, :] <= q[:, None], 0.0, NEG).astype(np.float32)
    c['identf'] = np.eye(128, dtype=np.float32)
    c['jrev'] = np.eye(128, dtype=np.float32)[::-1].copy()
    c['onesf'] = np.ones((128, 128), np.float32)
    n = np.arange(-128, 256)
    nn = np.maximum(n, 0)
    nf = np.maximum(nn, 1).astype(np.float32)
    large = 16 + (np.log(nf / np.float32(16)) / np.float32(math.log(128 / 16)) * np.float32(16)).astype(np.int32)
    large = np.minimum(large, 31)
    bucket = np.where(nn < 16, nn, large)
    e = np.zeros((32, 384), np.float32)
    e[bucket, np.arange(384)] = 1.0
    c['ebucket'] = e
    c['iotap'] = np.arange(128, dtype=np.float32)[:, None].copy()
    c['rc4'] = np.tile(np.arange(4, dtype=np.float32)[None, :], (128, 1))
    c['rc8'] = np.tile(np.arange(8, dtype=np.float32)[None, :], (128, 1))
    bmf = np.zeros((128, 4, 16), np.float32)
    for b_ in range(4):
        bmf[:, b_, 4 * b_:4 * b_ + 4] = 1.0
    c['bmaskf'] = bmf.reshape(128, 64)
    nn_ = np.full((16, 4, 4), NEG, np.float32)
    for j_ in range(16):
        for q_ in range(4):
            if (j_ % 4) <= q_:
                nn_[j_, j_ // 4, q_] = 0.0
    c['negnew'] = nn_.reshape(16, 16)
    negp = np.zeros((128, 1), np.float32)
    negp[127, 0] = NEG
    c['negp'] = negp
    sel = np.zeros((16, 4, 4, 16), np.float32)
    for b_ in range(4):
        for h4_ in range(4):
            for q_ in range(4):
                sel[h4_ * 4 + q_, b_, h4_, 4 * b_ + q_] = 1.0
    selp = np.zeros((128, 256), np.float32)
    selp[0:16] = sel.reshape(16, 256)
    c['sel'] = selp
    c['pw2'] = np.tile((0.5 ** (np.arange(NIT + 1, dtype=np.float64) + 1.0))[None, :], (128, 1)).astype(np.float32)
    return c


CONST_SHAPES = None


def build(with_sample=True, ntiles=NT, stage=9, cut=99, WARM=0.0):
    nc = bass.Bass("TRN2", target_bir_lowering=False)
    es = contextlib.ExitStack()
    consts = _consts()

    def din(name, shape, dt=F32):
        return nc.dram_tensor(name, list(shape), dt, kind="ExternalInput")

    def dout(name, shape, dt=F32):
        return nc.dram_tensor(name, list(shape), dt, kind="ExternalOutput")

    x_p = din('x_p', [T, D])
    p_p = din('p_p', [T, 256])
    w_in = din('w_in', [D, DIN])
    w_out = din('w_out', [D, D])
    w_gate = din('w_gate', [D, D])
    w_up = din('w_up', [256, D])
    g_pre = din('g_pre', [1, D])
    g_post = din('g_post', [1, D])
    rel_bias = din('rel_bias', [32, 8])
    cdr = {k: din('c_' + k, v.shape) for k, v in consts.items()}
    y_p = dout('y_p', [T, D])
    k_p = dout('k_p', [T, 128])
    v_p = dout('v_p', [T, 128])
    ki_p = dout('ki_p', [T, 64])
    st_p = dout('st_p', [8, 64, 64])
    scr = nc.dram_tensor('scr', [8, 384], F32, kind="Internal")
    if with_sample:
        x_s = din('x_s', [16, D])
        p_s = din('p_s', [16, 256])
        st_in = din('st_in', [4, 8, 64, 64])
        pt_s = din('pt_s', [4, 128], I32)
        c_k = din('c_k', [NPOOL, 16384])
        c_v = din('c_v', [NPOOL, 16384])
        c_ki = din('c_ki', [NPOOL, 8192])
        y_s = dout('y_s', [16, D])
        k_s = dout('k_s', [16, 128])
        v_s = dout('v_s', [16, 128])
        ki_s = dout('ki_s', [16, 64])
        st_s = dout('st_s', [4, 8, 64, 64])

    with es:
        P = Prog(nc, es)
        op = P.op

        def sb(name, shape, dt=F32):
            return es.enter_context(nc.sbuf_tensor(name, list(shape), dt))

        Win = sb('Win', [128, 8, 2048], BF16)
        Wout = sb('Wout', [128, 8, D], BF16)
        Wgate = sb('Wgate', [128, 8, D], BF16)
        Wup = sb('Wup', [128, 2, D], BF16)
        w_in_v = w_in.ap().rearrange("(kc p) n -> p kc n", p=128)

        win_loads = []

        def load_win(base, ncols):
            win_loads.append((base, ncols))
        w_out_v = w_out.ap().rearrange("(kc p) n -> p kc n", p=128)
        w_gate_v = w_gate.ap().rearrange("(kc p) n -> p kc n", p=128)
        w_up_v = w_up.ap().rearrange("(kc p) n -> p kc n", p=128)

        for kc in range(8):
            P.dma('pool', Wout[:, kc, :], w_out_v[:, kc, :], writes=['Wout'])
            P.dma('pool', Wgate[:, kc, :], w_gate_v[:, kc, :], writes=['Wgate'])
        for kc in range(2):
            P.dma('pool', Wup[:, kc, :], w_up_v[:, kc, :], writes=['Wup'])

        gpre = sb('gpre', [128, D])
        gpost = sb('gpost', [128, D])
        P.dma('sp', gpre[:], g_pre.ap().partition_broadcast(128), writes=['gpre'])
        P.dma('sp', gpost[:], g_post.ap().partition_broadcast(128), writes=['gpost'])
        C = {}
        for k in ('dinp', 'doutp', 'dtp', 'dcp', 'negtri', 'identf', 'jrev', 'onesf', 'pw2'):
            C[k] = sb('C_' + k, consts[k].shape)
            P.dma('sp', C[k][:], cdr[k].ap(), writes=['C_' + k])
        cosp = sb('cosp', [128, NT, 32])
        sinp = sb('sinp', [128, NT, 32])
        P.dma('sp', cosp[:], cdr['cosp'].ap().rearrange("(t p) d -> p t d", p=128), writes=['cosp'])
        P.dma('sp', sinp[:], cdr['sinp'].ap().rearrange("(t p) d -> p t d", p=128), writes=['sinp'])
        identb = sb('identb', [128, 128], BF16)
        op('dve', lambda e: e.tensor_copy(out=identb[:], in_=C['identf'][:]), reads=['C_identf'], writes=['identb'])

        xt = [sb('xt%d' % i, [128, D]) for i in range(2)]
        pt = [sb('pt%d' % i, [128, 256]) for i in range(2)]
        junkb = sb('junkb', [128, T], BF16)
        hb = sb('hb', [128, D], BF16)
        hT = sb('hT', [128, 8, 128], BF16)
        sm = sb('sm', [128, 72])
        tmpf = [sb('tmpf%d' % i, [128, 512]) for i in range(2)]
        score = sb('score', [128, T])
        R = sb('R', [128, 8, 512], BF16)
        maskT = R[:, 0:4, :].rearrange("p a (b q) -> p (a b) q", q=128)
        MK = ['R0', 'R1', 'R2', 'R3']
        osb = sb('osb', [128, 8, 64])
        x1 = sb('x1', [128, D])
        sqb = x1[:, 0:512]
        ybb = sb('ybb', [128, 512], BF16)
        YB = sb('YB', [128, NT, 512], BF16)
        yT = sb('yT', [128, 8, 128], BF16)
        sig = sb('sig', [128, D])
        kaT = sb('kaT', [128, T], BF16)
        kiT = sb('kiT', [128, T], BF16)
        Vaug = sb('Vaug', [128, NT, 2, 65], BF16)
        S = sb('S', [128, 256])
        Sbf = sb('Sbf', [128, 256], BF16)
        runmax = sb('runmax', [4, 1])
        kvf = sb('kvf', [128, 320])
        kab = sb('kab', [128, 128], BF16)
        kib = sb('kib', [128, 2, 64], BF16)
        wabs = sb('wabs', [128, 8])
        sgn = sb('sgn', [128, 8])
        Dg = sb('Dg', [128, 8, 128], BF16)
        qab = sb('qab', [128, 4, 2, 64], BF16)
        qaT = sb('qaT', [128, 4, 128], BF16)
        qib = sb('qib', [128, 8, 64], BF16)
        qiT = sb('qiT', [128, 4, 128], BF16)
        sga = sb('sga', [128, 512], BF16)
        sgb = sb('sgb', [128, 512], BF16)
        qbd = sb('qbd', [128, 8, 64], BF16)
        kbr = sb('kbr', [128, 8, 64], BF16)
        kbd = sb('kbd', [128, 8, 64], BF16)
        vbb = sb('vbb', [128, 512], BF16)
        qZ = sb('qZ', [128, 8, 128], BF16)
        kbT = sb('kbT', [128, 4, 128], BF16)
        AT = sb('AT', [128, 8, 128], BF16)
        Pt = [sb('Pt%d' % i, [128, 512], BF16) for i in range(3)]
        pbb = sb('pbb', [128, 256], BF16)
        pT = sb('pT', [128, 2, 128], BF16)
        mstat = sb('mstat', [128, 4])
        mT = sb('mT', [4, 128])
        dg4 = sb('dg4', [4, 4])
        negc = sb('negc', [128, 2])
        W16 = sb('W16', [128, NIT + 1])
        BT = sb('BT', [128, 2, 2, 512], BF16)
        qaZ = sb('qaZ', [128, 2, 4, 128], BF16)
        rb = sb('rb', [32, 8])
        ffar = sb('ffar', [8, 1])
        PO = [P.banks[6], P.banks[7]]
        POK = ['ps6', 'ps7']
        ptr = [0]

        def load_win_now(base, ncols):
            stg = [(score[:], ['score']), (R[:].rearrange("p a n -> p (a n)").bitcast(F32), ['R%d' % h for h in range(8)])]
            for kc in range(8):
                st_ap, st_k = stg[kc % 2]
                P.dma('sp', st_ap[:, 0:ncols], w_in_v[:, kc, base:base + ncols], writes=st_k)
                if kc % 2 == 0:
                    op('act', lambda e, kc=kc, st_ap=st_ap: e.copy(out=Win[:, kc, 0:ncols], in_=st_ap[:, 0:ncols]),
                       reads=st_k, writes=['Win'])
                else:
                    op('dve', lambda e, kc=kc, st_ap=st_ap: e.tensor_copy(out=Win[:, kc, 0:ncols], in_=st_ap[:, 0:ncols]),
                       reads=st_k, writes=['Win'])

        op('pool', lambda e: e.memset(Vaug[:], 1.0), writes=['Vaug'])
        op('pool', lambda e: e.memset(S[:], 0.0), writes=['S'])
        op('pool', lambda e: e.memset(Sbf[:], 0.0), writes=['Sbf'])
        op('pool', lambda e: e.memset(runmax[:], 0.0), writes=['runmax'])
        op('pool', lambda e: e.memset(qZ[:], 0.0), writes=['qZ'])
        op('pool', lambda e: e.memset(qaZ[:], 0.0), writes=['qaZ'])

        def setup_bias():
            eb = tmpf[0][0:32, 0:384]
            fsb = tmpf[1][0:8, 0:384]
            btrev = score[:].rearrange("p (a h q) -> p a h q", a=2, h=8)
            P.dma('sp', eb, cdr['ebucket'].ap(), writes=['tmpf0'])
            P.dma('sp', rb[:], rel_bias.ap(), writes=['rb'])
            pb, pk = P.bank()
            op('pe', lambda e: e.matmul(pb[0:8, 0:384], lhsT=rb[:, :], rhs=eb, start=True, stop=True),
               reads=['rb', 'tmpf0'], writes=[pk])
            op('dve', lambda e: e.tensor_copy(out=ffar[:], in_=pb[0:8, 383:384]), reads=[pk], writes=['ffar'])
            op('dve', lambda e: e.tensor_scalar(out=fsb, in0=pb[0:8, 0:384], scalar1=ffar[:, 0:1], scalar2=8.0,
                                                op0=ALU.subtract, op1=ALU.mult), reads=[pk, 'ffar'], writes=['tmpf1'])
            P.dma('sp', scr.ap(), fsb, reads=['tmpf1'], writes=['scr'])
            for ty in range(2):
                P.dma('sp', btrev[:, ty, :, :], bass.AP(scr, 1 + 128 * ty, [[1, 128], [384, 8], [1, 128]]),
                      reads=['scr'], writes=['score'])
            for ty in range(2):
                for g in range(2):
                    pb, pk = P.bank()
                    op('pe', lambda e: e.matmul(pb[:, 0:512], lhsT=C['jrev'][:],
                                                rhs=btrev[:, ty, 4 * g:4 * g + 4, :], start=True, stop=True),
                       reads=['C_jrev', 'score'], writes=[pk])
                    op('act', lambda e: e.copy(out=BT[:, ty, g, :], in_=pb[:, 0:512]), reads=[pk], writes=['BT'])


        def transposes(src_fn, n_in, nblk, dst, dstkeys, srckeys, eng='act', npart=128):
            pb, pk = P.bank()
            pbv = pb[:].bitcast(BF16)
            for b in range(nblk):
                op('pe', lambda e, b=b: e.transpose(out=pbv[0:npart, b * 128:b * 128 + n_in], in_=src_fn(b),
                                                    identity=identb[0:n_in, 0:n_in]),
                   reads=list(srckeys) + ['identb'], writes=[pk])
            src = pbv[0:npart, 0:nblk * 128].rearrange("p (b t) -> p b t", t=128)[:, :, 0:n_in]
            if eng == 'act':
                op('act', lambda e: e.copy(out=dst[0:npart, 0:nblk, 0:n_in], in_=src), reads=[pk], writes=dstkeys)
            else:
                op(eng, lambda e: e.tensor_copy(out=dst[0:npart, 0:nblk, 0:n_in], in_=src), reads=[pk],
                   writes=dstkeys)

        def load_tile(ti, with_p):
            b = ti % 2
            P.dma('sp', xt[b][:], x_p.ap()[ti * 128:(ti + 1) * 128, :], writes=['xt%d' % b])
            if with_p:
                P.dma('sp', pt[b][:], p_p.ap()[ti * 128:(ti + 1) * 128, :], writes=['pt%d' % b])

        def proj_chunk(c0, w, n=128, bank=None):
            if bank is None:
                pb, pk = P.bank()
            else:
                pb, pk = P.banks[bank], 'ps%d' % bank
            for kc in range(8):
                op('pe', lambda e, kc=kc: e.matmul(pb[0:n, 0:w], lhsT=hT[:, kc, 0:n], rhs=Win[:, kc, c0:c0 + w],
                                                   start=(kc == 0), stop=(kc == 7)),
                   reads=['hT', 'Win'], writes=[pk])
            return pb, pk

        def rstd_of(ss_ap, out_ap, sskeys, outkey, tmpcol, n=128):
            op('act', lambda e: e.activation(out=sm[0:n, tmpcol:tmpcol + 1], in_=ss_ap, func=AF.Sqrt,
                                             bias=EPS, scale=1.0 / D), reads=sskeys, writes=['sm_t%d' % tmpcol])
            op('dve', lambda e: e.reciprocal(out=out_ap, in_=sm[0:n, tmpcol:tmpcol + 1]),
               reads=['sm_t%d' % tmpcol], writes=[outkey])

        def prenorm(x, xk, n=128):
            op('act', lambda e: e.activation(out=hb[0:n, :], in_=x[0:n, :], func=AF.Square,
                                             accum_out=sm[0:n, 0:1]), reads=[xk], writes=['hb', 'sm_ss'])
            rstd_of(sm[0:n, 0:1], sm[0:n, 1:2], ['sm_ss'], 'sm_rstd', 64, n)
            op('dve', lambda e: e.scalar_tensor_tensor(out=hb[0:n, :], in0=x[0:n, :], scalar=sm[0:n, 1:2],
                                                       in1=gpre[0:n, :], op0=ALU.mult, op1=ALU.mult),
               reads=[xk, 'sm_rstd', 'gpre'], writes=['hb'])
            transposes(lambda kc: hb[0:n, kc * 128:(kc + 1) * 128], n, 8, hT, ['hT'], ['hb'])

        def tile_ret(ti):
            b = ti % 2
            x = xt[b]
            xk = 'xt%d' % b
            prenorm(x, xk)
            if cut <= 1:
                return
            yield 0
            pj = {}
            for nm_, bk_, c0_ in (('gb', 0, O_GB), ('qb', 1, O_QB), ('kb', 2, O_KB), ('vb', 3, O_VB)):
                pj[nm_] = proj_chunk(c0_ - O_QB, 512, bank=bk_)
            yield 1
            pb, pk = pj['gb']
            op('act', lambda e: e.activation(out=sgb[:], in_=pb[:, 0:512], func=AF.Silu), reads=[pk], writes=['sgb'])
            if cut <= 2:
                return
            t1 = tmpf[0][:].rearrange("p (h a d) -> p h a d", h=8, a=2)
            t2 = tmpf[1][:].rearrange("p (h a d) -> p h a d", h=8, a=2)
            cosb = cosp[:, ti, :].unsqueeze(1).unsqueeze(1).to_broadcast([128, 8, 2, 32])
            sinb = sinp[:, ti, :].unsqueeze(1).unsqueeze(1).to_broadcast([128, 8, 2, 32])
            for which in range(2):
                pb, pk = pj['qb' if which == 0 else 'kb']
                pv = pb[:, 0:512].rearrange("p (h a d) -> p h a d", h=8, a=2)
                op('dve', lambda e: e.tensor_tensor(out=t1, in0=pv, in1=cosb, op=ALU.mult),
                   reads=[pk, 'cosp'], writes=['tmpf0'])
                op('dve', lambda e: e.tensor_tensor(out=t2, in0=pv, in1=sinb, op=ALU.mult),
                   reads=[pk, 'sinp'], writes=['tmpf1'])
                op('dve', lambda e: e.tensor_tensor(out=t1[:, :, 0, :], in0=t1[:, :, 0, :], in1=t2[:, :, 1, :],
                                                    op=ALU.subtract), reads=['tmpf0', 'tmpf1'], writes=['tmpf0'])
                op('pool', lambda e: e.tensor_tensor(out=t1[:, :, 1, :], in0=t1[:, :, 1, :], in1=t2[:, :, 0, :],
                                                     op=ALU.add), reads=['tmpf0', 'tmpf1'], writes=['tmpf0'])
                t1v = tmpf[0][:].rearrange("p (h d) -> p h d", h=8)
                if which == 0:
                    op('pool', lambda e: e.tensor_tensor(out=qbd[:], in0=t1v,
                                                         in1=C['dinp'][:].unsqueeze(2).to_broadcast([128, 8, 64]),
                                                         op=ALU.mult), reads=['tmpf0', 'C_dinp'], writes=['qbd'])
                else:
                    op('act', lambda e: e.mul(out=kbr[:], in_=t1v, mul=0.125), reads=['tmpf0'], writes=['kbr'])
                    op('dve', lambda e: e.tensor_tensor(out=kbd[:], in0=t1v,
                                                        in1=C['doutp'][:].unsqueeze(2).to_broadcast([128, 8, 64]),
                                                        op=ALU.mult), reads=['tmpf0', 'C_doutp'], writes=['kbd'])
            pb, pk = pj['vb']
            op('act', lambda e: e.copy(out=vbb[:], in_=pb[:, 0:512]), reads=[pk], writes=['vbb'])
            yield 2
            if cut <= 3:
                return
            pbq, pkq = P.bank()
            pbqv = pbq[:].bitcast(BF16)
            for f in range(4):
                op('pe', lambda e, f=f: e.transpose(out=pbqv[:, f * 128:(f + 1) * 128],
                                                    in_=qbd[:, 2 * f:2 * f + 2, :].rearrange("p a d -> p (a d)"),
                                                    identity=identb[:]), reads=['qbd', 'identb'], writes=[pkq])
            qzv = qZ[:].rearrange("p (f a) c -> p a f c", a=2)
            pqv = pbqv[:, 0:512].rearrange("p (f c) -> p f c", f=4)
            op('act', lambda e: e.copy(out=qzv[0:64, 0, :, :], in_=pqv[0:64]), reads=[pkq], writes=['qZ'])
            op('act', lambda e: e.copy(out=qzv[64:128, 1, :, :], in_=pqv[64:128]), reads=[pkq], writes=['qZ'])
            transposes(lambda f: kbr[:, 2 * f:2 * f + 2, :].rearrange("p a d -> p (a d)"), 128, 4, kbT, ['kbT'],
                       ['kbr'])
            if cut <= 4:
                return
            pa, pak = P.bank()
            pb2, pbk = P.bank()
            for h in range(8):
                tgt, tk = (pa, pak) if h < 4 else (pb2, pbk)
                op('pe', lambda e, h=h, tgt=tgt: e.matmul(
                    tgt[:, (h % 4) * 128:(h % 4) * 128 + 128], lhsT=kbT[:, h // 2, :],
                    rhs=qZ[:, h, :], start=True, stop=True),
                   reads=['kbT', 'qZ'], writes=[tk])
            dtv = C['dtp'][:].rearrange("p (h c) -> p h c", h=8)
            op('dve', lambda e: e.tensor_tensor(out=AT[:, 0:4, :], in0=pa[:, 0:512].rearrange("p (h c) -> p h c", h=4),
                                                in1=dtv[:, 0:4, :], op=ALU.mult), reads=[pak, 'C_dtp'], writes=['AT'])
            op('dve', lambda e: e.tensor_tensor(out=AT[:, 4:8, :], in0=pb2[:, 0:512].rearrange("p (h c) -> p h c", h=4),
                                                in1=dtv[:, 4:8, :], op=ALU.mult), reads=[pbk, 'C_dtp'], writes=['AT'])
            if cut <= 5:
                return
            po, pok = P.bank()
            for h in range(8):
                lo = (h % 2) * 64
                op('pe', lambda e, h=h: e.matmul(po[:, h * 64:(h + 1) * 64], lhsT=AT[:, h, :],
                                                 rhs=vbb[:, h * 64:(h + 1) * 64], start=True, stop=False),
                   reads=['AT', 'vbb'], writes=[pok])
                op('pe', lambda e, h=h: e.matmul(po[:, h * 64:(h + 1) * 64], lhsT=qZ[:, h, :],
                                                 rhs=Sbf[:, (h // 2) * 64:(h // 2) * 64 + 64],
                                                 start=False, stop=True),
                   reads=['qZ', 'Sbf'], writes=[pok])
            if cut <= 6:
                return
            pst, pstk = P.bank()
            for h in range(8):
                lo = (h % 2) * 64
                op('pe', lambda e, h=h, lo=lo: e.matmul(pst[lo:lo + 64, (h // 2) * 64:(h // 2) * 64 + 64],
                                                        lhsT=kbd[:, h, :], rhs=vbb[:, h * 64:(h + 1) * 64],
                                                        start=True, stop=True),
                   reads=['kbd', 'vbb'], writes=[pstk])
            if cut <= 7:
                return
            op('pool', lambda e: e.tensor_tensor(out=S[:], in0=S[:], in1=C['dcp'][:], op=ALU.mult),
               reads=['S', 'C_dcp'], writes=['S'])
            op('dve', lambda e: e.tensor_tensor(out=S[:], in0=pst[:, 0:256], in1=S[:], op=ALU.add),
               reads=[pstk, 'S'], writes=['S'])
            op('act', lambda e: e.copy(out=Sbf[:], in_=S[:]), reads=['S'], writes=['Sbf'])
            if cut <= 8:
                return
            headnorm_gate(po, pok, YB[:, ti, :], ['YB'])

        def headnorm_gate(po, pok, dst, dstkeys, n=128):
            op('act', lambda e: e.copy(out=osb[0:n], in_=po[0:n, 0:512].rearrange("p (h d) -> p h d", h=8)),
               reads=[pok], writes=['osb'])
            op('act', lambda e: e.activation(out=sqb[0:n], in_=po[0:n, 0:512], func=AF.Square), reads=[pok],
               writes=['x1'])
            op('dve', lambda e: e.tensor_reduce(out=sm[0:n, 24:32], in_=osb[0:n], axis=AX.X, op=ALU.add),
               reads=['osb'], writes=['sm_s1'])
            op('dve', lambda e: e.tensor_reduce(out=sm[0:n, 32:40], in_=sqb[0:n].rearrange("p (h d) -> p h d", h=8),
                                                axis=AX.X, op=ALU.add), reads=['x1'], writes=['sm_s2'])
            op('dve', lambda e: e.tensor_scalar(out=sm[0:n, 24:32], in0=sm[0:n, 24:32], scalar1=1.0 / 64, scalar2=None,
                                                op0=ALU.mult), reads=['sm_s1'], writes=['sm_s1'])
            op('dve', lambda e: e.tensor_tensor(out=sm[0:n, 40:48], in0=sm[0:n, 24:32], in1=sm[0:n, 24:32],
                                                op=ALU.mult), reads=['sm_s1'], writes=['sm_msq'])
            op('dve', lambda e: e.scalar_tensor_tensor(out=sm[0:n, 32:40], in0=sm[0:n, 32:40], scalar=1.0 / 64,
                                                       in1=sm[0:n, 40:48], op0=ALU.mult, op1=ALU.subtract),
               reads=['sm_s2', 'sm_msq'], writes=['sm_s2'])
            op('act', lambda e: e.activation(out=sm[0:n, 40:48], in_=sm[0:n, 32:40], func=AF.Sqrt, bias=EPS,
                                             scale=1.0), reads=['sm_s2'], writes=['sm_msq'])
            op('dve', lambda e: e.reciprocal(out=sm[0:n, 48:56], in_=sm[0:n, 40:48]), reads=['sm_msq'],
               writes=['sm_hr'])
            op('dve', lambda e: e.tensor_tensor(out=osb[0:n], in0=osb[0:n],
                                                in1=sm[0:n, 24:32].unsqueeze(2).to_broadcast([n, 8, 64]),
                                                op=ALU.subtract), reads=['osb', 'sm_s1'], writes=['osb'])
            op('pool', lambda e: e.tensor_tensor(out=osb[0:n], in0=osb[0:n],
                                                 in1=sm[0:n, 48:56].unsqueeze(2).to_broadcast([n, 8, 64]),
                                                 op=ALU.mult), reads=['osb', 'sm_hr'], writes=['osb'])
            op('pool', lambda e: e.tensor_tensor(out=dst, in0=osb[0:n].rearrange("p h d -> p (h d)"), in1=sgb[0:n],
                                                 op=ALU.mult), reads=['osb', 'sgb'], writes=dstkeys)

        def tile_dsa(ti):
            b = ti % 2
            x = xt[b]
            xk = 'xt%d' % b
            pb, pk = proj_chunk(O_KI, 72)
            op('act', lambda e: e.copy(out=kvf[:, 256:320], in_=pb[:, 0:64]), reads=[pk], writes=['kvf'])
            if cut <= 10.1:
                return
            for a in range(2):
                op('dve', lambda e, a=a: e.tensor_copy(out=kib[:, a, :], in_=pb[:, 0:64]), reads=[pk], writes=['kib'])
            if cut <= 10.2:
                return
            op('act', lambda e: e.activation(out=wabs[:], in_=pb[:, 64:72], func=AF.Abs), reads=[pk], writes=['wabs'])
            op('act', lambda e: e.activation(out=sgn[:], in_=pb[:, 64:72], func=AF.Sign), reads=[pk], writes=['sgn'])
            if cut <= 10.3:
                return
            for h in range(8):
                op('dve', lambda e, h=h: e.tensor_scalar(out=Dg[:, h, :], in0=identb[:], scalar1=sgn[:, h:h + 1],
                                                         scalar2=None, op0=ALU.mult),
                   reads=['identb', 'sgn'], writes=['Dg'])
            if cut <= 10.4:
                return
            later = []
            later2 = []
            later.append(lambda: transposes(lambda bb: kib[:].rearrange("p a d -> p (a d)"), 128, 1,
                                            kiT[:, ti * 128:(ti + 1) * 128].unsqueeze(1), ['kiT'], ['kib']))
            if cut <= 11:
                return
            pb, pk = proj_chunk(O_KA, 256)
            op('act', lambda e: e.copy(out=kvf[:, 0:256], in_=pb[:, 0:256]), reads=[pk], writes=['kvf'])
            op('dve', lambda e: e.tensor_copy(out=kab[:], in_=pb[:, 0:128]), reads=[pk], writes=['kab'])
            op('dve', lambda e: e.tensor_copy(out=Vaug[:, ti, :, 0:64],
                                              in_=pb[:, 128:256].rearrange("p (g d) -> p g d", g=2)),
               reads=[pk], writes=['Vaug'])
            P.dma('sp', k_p.ap()[ti * 128:(ti + 1) * 128, :], kvf[:, 0:128], reads=['kvf'])
            P.dma('sp', v_p.ap()[ti * 128:(ti + 1) * 128, :], kvf[:, 128:256], reads=['kvf'])
            P.dma('sp', ki_p.ap()[ti * 128:(ti + 1) * 128, :], kvf[:, 256:320], reads=['kvf'])
            later.append(lambda: transposes(lambda bb: kab[:], 128, 1, kaT[:, ti * 128:(ti + 1) * 128].unsqueeze(1),
                                            ['kaT'], ['kab']))
            if cut <= 12:
                return
            pb, pk = proj_chunk(O_QA, 512)
            op('act', lambda e: e.copy(out=qab[:], in_=pb[:, 0:512].rearrange("p (g f d) -> p f g d", g=2, f=4)),
               reads=[pk], writes=['qab'])
            op('act', lambda e: e.activation(out=sqb, in_=pb[:, 0:512], func=AF.Square), reads=[pk], writes=['x1'])
            op('dve', lambda e: e.tensor_reduce(out=sm[:, 8:16], in_=sqb.rearrange("p (h d) -> p h d", h=8),
                                                axis=AX.X, op=ALU.add), reads=['x1'], writes=['sm_qn'])
            op('dve', lambda e: e.tensor_reduce(out=mstat[:, 0:2], in_=sm[:, 8:16].rearrange("p (g f) -> p g f", g=2),
                                                axis=AX.X, op=ALU.max), reads=['sm_qn'], writes=['mstat'])
            def _qa_tr():
                transposes(lambda f: qab[:, f].rearrange("p g d -> p (g d)"), 128, 4, qaT, ['qaT'], ['qab'])
                for g in range(2):
                    op('act', lambda e, g=g: e.copy(out=qaZ[g * 64:(g + 1) * 64, g, :, :],
                                                    in_=qaT[g * 64:(g + 1) * 64, :, :]), reads=['qaT'], writes=['qaZ'])
            later.append(_qa_tr)
            op('act', lambda e: e.activation(out=sqb[:, 0:128], in_=kvf[:, 0:128], func=AF.Square),
               reads=['kvf'], writes=['x1'])
            op('dve', lambda e: e.tensor_reduce(out=mstat[:, 2:4], in_=sqb[:, 0:128].rearrange("p (g d) -> p g d", g=2),
                                                axis=AX.X, op=ALU.add), reads=['x1'], writes=['mstat'])
            def _negc_chain():
                pb, pk = P.bank()
                op('pe', lambda e: e.transpose(out=pb[0:4, 0:128], in_=mstat[:, 0:4], identity=C['identf'][:]),
                   reads=['mstat', 'C_identf'], writes=[pk])
                op('dve', lambda e: e.tensor_reduce(out=mT[:, 0:1], in_=pb[0:4, 0:128], axis=AX.X, op=ALU.max),
                   reads=[pk], writes=['mT'])
                op('dve', lambda e: e.tensor_tensor(out=runmax[:], in0=runmax[:], in1=mT[:, 0:1], op=ALU.max),
                   reads=['mT', 'runmax'], writes=['runmax'])
                op('dve', lambda e: e.tensor_scalar(out=dg4[:], in0=C['identf'][0:4, 0:4], scalar1=runmax[:, 0:1],
                                                    scalar2=None, op0=ALU.mult),
                   reads=['runmax', 'C_identf'], writes=['dg4'])
                pb, pk = P.bank()
                op('pe', lambda e: e.matmul(pb[:, 0:4], lhsT=C['onesf'][0:4, :], rhs=dg4[:], start=True, stop=True),
                   reads=['dg4', 'C_onesf'], writes=[pk])
                op('dve', lambda e: e.tensor_copy(out=sm[:, 16:20], in_=pb[:, 0:4]), reads=[pk], writes=['sm_bc'])
                op('dve', lambda e: e.tensor_tensor(out=sm[:, 20:22], in0=sm[:, 16:18], in1=sm[:, 18:20], op=ALU.mult),
                   reads=['sm_bc'], writes=['sm_c2'])
                op('act', lambda e: e.activation(out=sm[:, 22:24], in_=sm[:, 20:22], func=AF.Sqrt, scale=1.0 / 64.0),
                   reads=['sm_c2'], writes=['sm_c'])
                op('dve', lambda e: e.tensor_scalar(out=negc[:], in0=sm[:, 22:24], scalar1=-1.0, scalar2=None,
                                                    op0=ALU.mult), reads=['sm_c'], writes=['negc'])
            later2.append(_negc_chain)
            if cut <= 13:
                return
            pb, pk = proj_chunk(O_QI, 512)
            op('dve', lambda e: e.tensor_tensor(out=qib[:], in0=pb[:, 0:512].rearrange("p (h d) -> p h d", h=8),
                                                in1=wabs[:].unsqueeze(2).to_broadcast([128, 8, 64]), op=ALU.mult),
               reads=[pk, 'wabs'], writes=['qib'])
            later.append(lambda: transposes(lambda f: qib[:, 2 * f:2 * f + 2, :].rearrange("p a d -> p (a d)"), 128, 4,
                                            qiT, ['qiT'], ['qib']))
            pb, pk = proj_chunk(O_GA, 512)
            op('act', lambda e: e.activation(out=sga[:], in_=pb[:, 0:512], func=AF.Silu), reads=[pk], writes=['sga'])
            for fn_ in later:
                fn_()

            if cut <= 14:
                return
            nk = (ti + 1) * 128
            nch = (nk + 511) // 512
            for cc in range(nch):
                w = min(512, nk - cc * 512)
                for h in range(8):
                    lo = (h % 2) * 64
                    pb, pk = P.bank()
                    op('pe', lambda e, h=h, lo=lo, pb=pb: e.matmul(
                        pb[:, 0:w], lhsT=qiT[lo:lo + 64, h // 2, :], rhs=kiT[lo:lo + 64, cc * 512:cc * 512 + w],
                        start=True, stop=True), reads=['qiT', 'kiT'], writes=[pk])
                    if h % 2 == 0:
                        op('act', lambda e, h=h, pb=pb: e.activation(out=R[:, h, 0:w], in_=pb[:, 0:w], func=AF.Relu),
                           reads=[pk], writes=['R%d' % h])
                    else:
                        op('dve', lambda e, h=h, pb=pb: e.tensor_scalar(out=R[:, h, 0:w], in0=pb[:, 0:w], scalar1=0.0,
                                                                        scalar2=None, op0=ALU.max),
                           reads=[pk], writes=['R%d' % h])
                pb, pk = P.bank()
                for h in range(8):
                    op('pe', lambda e, h=h, pb=pb: e.matmul(pb[:, 0:w], lhsT=Dg[:, h, :], rhs=R[:, h, 0:w],
                                                            start=(h == 0), stop=(h == 7)),
                       reads=['Dg', 'R%d' % h], writes=[pk])
                op('dve', lambda e, pb=pb: e.tensor_reduce(out=sm[:, 2 + cc:3 + cc], in_=pb[:, 0:w], axis=AX.X,
                                                           op=ALU.max, apply_absolute_value=True),
                   reads=[pk], writes=['sm_mx%d' % cc])
                last = (cc == nch - 1)
                wc = w - 128 if last else w
                if wc > 0:
                    op('act', lambda e, pb=pb, wc=wc: e.copy(out=score[:, cc * 512:cc * 512 + wc], in_=pb[:, 0:wc]),
                       reads=[pk], writes=['score'])
                if last:
                    op('dve', lambda e, pb=pb: e.tensor_tensor(out=score[:, nk - 128:nk], in0=pb[:, w - 128:w],
                                                               in1=C['negtri'][:], op=ALU.add),
                       reads=[pk, 'C_negtri'], writes=['score'])
            for fn_ in later2:
                fn_()
            if cut <= 15:
                return
            yield 1
            op('dve', lambda e: e.tensor_reduce(out=sm[:, 6:7], in_=sm[:, 2:2 + nch], axis=AX.X, op=ALU.max),
               reads=['sm_mx%d' % c for c in range(nch)], writes=['sm_bd'])
            op('dve', lambda e: e.tensor_scalar(out=sm[:, 7:8], in0=sm[:, 6:7], scalar1=1.0, scalar2=-1.0,
                                                op0=ALU.add, op1=ALU.mult), reads=['sm_bd'], writes=['sm_lo'])
            op('dve', lambda e: e.tensor_scalar(out=sm[:, 56:57], in0=sm[:, 6:7], scalar1=1.0, scalar2=2.0,
                                                op0=ALU.add, op1=ALU.mult), reads=['sm_bd'], writes=['sm_w0'])
            op('dve', lambda e: e.tensor_scalar(out=W16[:], in0=C['pw2'][:], scalar1=sm[:, 56:57], scalar2=None,
                                                op0=ALU.mult), reads=['sm_w0', 'C_pw2'], writes=['W16'])
            op('dve', lambda e: e.tensor_tensor(out=sm[:, 57:58], in0=sm[:, 7:8], in1=W16[:, 0:1], op=ALU.add),
               reads=['sm_lo', 'W16'], writes=['sm_mid'])
            for it in range(NIT):
                op('dve', lambda e: e.tensor_scalar(out=junkb[:, 0:nk], in0=score[:, 0:nk], scalar1=sm[:, 57:58],
                                                    scalar2=None, op0=ALU.is_ge, op1=ALU.add,
                                                    accum_out=sm[:, 58:59]),
                   reads=['score', 'sm_mid'], writes=['junkb', 'sm_cnt'])
                op('dve', lambda e, it=it: e.tensor_scalar(out=sm[:, 59:60], in0=sm[:, 58:59], scalar1=TOPK - 0.5,
                                                           scalar2=W16[:, it:it + 1], op0=ALU.is_ge, op1=ALU.mult),
                   reads=['sm_cnt', 'W16'], writes=['sm_pw'])
                op('dve', lambda e, it=it: e.scalar_tensor_tensor(out=sm[:, 57:58], in0=sm[:, 59:60],
                                                                  scalar=W16[:, it + 1:it + 2], in1=sm[:, 57:58],
                                                                  op0=ALU.subtract, op1=ALU.add),
                   reads=['sm_pw', 'W16', 'sm_mid'], writes=['sm_mid'])
            op('dve', lambda e: e.tensor_tensor(out=sm[:, 7:8], in0=sm[:, 57:58], in1=W16[:, NIT:NIT + 1],
                                                op=ALU.subtract), reads=['sm_mid', 'W16'], writes=['sm_lo'])
            op('dve', lambda e: e.tensor_scalar(out=junkb[:, 0:nk], in0=score[:, 0:nk], scalar1=sm[:, 7:8],
                                                scalar2=None, op0=ALU.is_ge), reads=['score', 'sm_lo'], writes=['junkb'])
            yield 'bisected'
            for j0 in range(0, ti + 1, 8):
                nb = min(8, ti + 1 - j0)
                pbm, pkm = P.bank()
                pbmv = pbm[:].bitcast(BF16)
                for bb in range(nb):
                    op('pe', lambda e, bb=bb: e.transpose(out=pbmv[:, bb * 128:(bb + 1) * 128],
                                                          in_=junkb[:, (j0 + bb) * 128:(j0 + bb + 1) * 128],
                                                          identity=identb[:]), reads=['junkb', 'identb'], writes=[pkm])
                op('act', lambda e, nb=nb, j0=j0: e.activation(
                    out=maskT[:, j0:j0 + nb, :], in_=pbmv[:, 0:nb * 128].rearrange("p (b q) -> p b q", q=128),
                    func=AF.Identity, scale=MB, bias=-MB), reads=[pkm], writes=MK)
            if cut <= 16:
                return
            for g in range(2):
                qk = {}

                def emit_qk(j, g=g):
                    pb, pk = P.bank()
                    ty = ti - j
                    op('pe', lambda e: e.matmul(pb[:, 0:512], lhsT=kaT[:, j * 128:(j + 1) * 128], rhs=qaZ[:, g, :, :],
                                                start=True, stop=False), reads=['kaT', 'qaZ'], writes=[pk])
                    op('pe', lambda e: e.matmul(pb[:, 0:512], lhsT=identb[:],
                                                rhs=maskT[:, j, :].unsqueeze(1).to_broadcast([128, 4, 128]),
                                                start=False, stop=(ty > 1)), reads=['identb'] + MK, writes=[pk])
                    if ty <= 1:
                        op('pe', lambda e: e.matmul(pb[:, 0:512], lhsT=identb[:], rhs=BT[:, ty, g, :],
                                                    start=False, stop=True), reads=['identb', 'BT'], writes=[pk])
                    qk[j] = (pb, pk)
                emit_qk(0)
                for j in range(ti + 1):
                    if j + 1 <= ti:
                        emit_qk(j + 1)
                    pb, pk = qk.pop(j)
                    Pb = Pt[ptr[0] % 3]
                    Pk = 'Pt%d' % (ptr[0] % 3)
                    ptr[0] += 1
                    op('act', lambda e, pb=pb, Pb=Pb, g=g: e.activation(out=Pb[:], in_=pb[:, 0:512], func=AF.Exp,
                                                                        bias=negc[:, g:g + 1], scale=0.125),
                       reads=[pk, 'negc'], writes=[Pk])
                    for h4 in range(4):
                        op('pe', lambda e, Pb=Pb, j=j, h4=h4, g=g: e.matmul(
                            PO[g][:, h4 * 65:(h4 + 1) * 65], lhsT=Pb[:, h4 * 128:(h4 + 1) * 128],
                            rhs=Vaug[:, j, g, :], start=(j == 0 and h4 == 0), stop=(j == ti and h4 == 3),
                            skip_group_check=True), reads=[Pk, 'Vaug'], writes=[POK[g]])
                pov = PO[g][:, 0:260].rearrange("p (h d) -> p h d", h=4)
                op('dve', lambda e, pov=pov: e.reciprocal(out=sm[:, 60:64].unsqueeze(2), in_=pov[:, :, 64:65]),
                   reads=[POK[g]], writes=['sm_rec'])
                op('dve', lambda e, pov=pov, g=g: e.tensor_tensor(
                    out=osb[:, 4 * g:4 * g + 4, :], in0=pov[:, :, 0:64],
                    in1=sm[:, 60:64].unsqueeze(2).to_broadcast([128, 4, 64]), op=ALU.mult),
                   reads=[POK[g], 'sm_rec'], writes=['osb'])
            if cut <= 17:
                return
            op('pool', lambda e: e.tensor_tensor(out=ybb[:], in0=osb[:].rearrange("p h d -> p (h d)"), in1=sga[:],
                                                 op=ALU.mult), reads=['osb', 'sga'], writes=['ybb'])
            transposes(lambda f: ybb[:, f * 128:(f + 1) * 128], 128, 4, yT[:, 0:4, :], ['yTa'], ['ybb'])
            transposes(lambda f: YB[:, ti, f * 128:(f + 1) * 128], 128, 4, yT[:, 4:8, :], ['yTb'], ['YB'])
            if cut <= 18:
                return
            yield 2

        mhalf = sb('mhalf', [128, 1])
        op('pool', lambda e: e.memset(mhalf[:], -0.5), writes=['mhalf'])

        def merge2(x, xk, pin, pink, ydst):
            py = [P.bank(), P.bank()]
            for hf in range(2):
                for kc in range(8):
                    op('pe', lambda e, hf=hf, kc=kc: e.matmul(py[hf][0][:, 0:512], lhsT=yT[:, kc, :],
                                                              rhs=Wout[:, kc, hf * 512:(hf + 1) * 512],
                                                              start=(kc == 0), stop=(kc == 7)),
                       reads=['yTa', 'yTb', 'Wout'], writes=[py[hf][1]])
                op('act', lambda e, hf=hf: e.activation(out=sig[:, hf * 512:(hf + 1) * 512], in_=py[hf][0][:, 0:512],
                                                        func=AF.Square, accum_out=sm[:, 66 + hf:67 + hf]),
                   reads=[py[hf][1]], writes=['sig', 'sm_y%d' % hf])
            op('pool', lambda e: e.tensor_tensor(out=sm[:, 66:67], in0=sm[:, 66:67], in1=sm[:, 67:68], op=ALU.add),
               reads=['sm_y0', 'sm_y1'], writes=['sm_y0'])
            op('pool', lambda e: e.tensor_scalar(out=sm[:, 65:66], in0=sm[:, 66:67], scalar1=1.0 / D, scalar2=EPS,
                                                 op0=ALU.mult, op1=ALU.add), reads=['sm_y0'], writes=['sm_t65'])
            op('pool', lambda e: e.tensor_tensor(out=sm[:, 67:68], in0=sm[:, 65:66], in1=mhalf[:], op=ALU.pow),
               reads=['sm_t65', 'mhalf'], writes=['sm_y1'])
            for hf in range(2):
                op('act', lambda e, hf=hf: e.activation(out=x1[:, hf * 512:(hf + 1) * 512], in_=py[hf][0][:, 0:512],
                                                        func=AF.Copy, scale=sm[:, 67:68]),
                   reads=[py[hf][1], 'sm_y1'], writes=['x1'])
            op('pool', lambda e: e.tensor_tensor(out=x1[:], in0=x1[:], in1=gpost[:], op=ALU.mult),
               reads=['x1', 'gpost'], writes=['x1'])
            op('pool', lambda e: e.tensor_tensor(out=x1[:], in0=x1[:], in1=x[:], op=ALU.add),
               reads=['x1', xk], writes=['x1'])
            op('act', lambda e: e.copy(out=hb[:], in_=x1[:]), reads=['x1'], writes=['hb'])
            transposes(lambda kc: hb[:, kc * 128:(kc + 1) * 128], 128, 8, hT, ['hT'], ['hb'])
            op('act', lambda e: e.copy(out=pbb[:], in_=pin[:]), reads=[pink], writes=['pbb'])
            transposes(lambda kc: pbb[:, kc * 128:(kc + 1) * 128], 128, 2, pT, ['pT'], ['pbb'])
            for hf in range(2):
                pg, pgk = P.bank()
                for kc in range(8):
                    op('pe', lambda e, hf=hf, kc=kc, pg=pg: e.matmul(pg[:, 0:512], lhsT=hT[:, kc, :],
                                                                     rhs=Wgate[:, kc, hf * 512:(hf + 1) * 512],
                                                                     start=(kc == 0), stop=(kc == 7)),
                       reads=['hT', 'Wgate'], writes=[pgk])
                op('act', lambda e, hf=hf, pg=pg: e.activation(out=sig[:, hf * 512:(hf + 1) * 512], in_=pg[:, 0:512],
                                                               func=AF.Sigmoid), reads=[pgk], writes=['sig'])
                pu, puk = P.bank()
                for kc in range(2):
                    op('pe', lambda e, hf=hf, kc=kc, pu=pu: e.matmul(pu[:, 0:512], lhsT=pT[:, kc, :],
                                                                     rhs=Wup[:, kc, hf * 512:(hf + 1) * 512],
                                                                     start=(kc == 0), stop=(kc == 1)),
                       reads=['pT', 'Wup'], writes=[puk])
                op('act', lambda e, hf=hf, pu=pu: e.copy(out=tmpf[hf][:], in_=pu[:, 0:512]), reads=[puk],
                   writes=['tmpf%d' % hf])
                op('pool', lambda e, hf=hf: e.tensor_tensor(out=sig[:, hf * 512:(hf + 1) * 512],
                                                            in0=sig[:, hf * 512:(hf + 1) * 512], in1=tmpf[hf][:],
                                                            op=ALU.mult), reads=['sig', 'tmpf%d' % hf], writes=['sig'])
            op('pool', lambda e: e.tensor_tensor(out=sig[:], in0=sig[:], in1=x1[:], op=ALU.add),
               reads=['sig', 'x1'], writes=['sig'])
            P.dma('sp', ydst, sig[:], reads=['sig'])

        def merge(x, xk, pin, pink, ydst, n=128):
            py = [P.bank(), P.bank()]
            for hf in range(2):
                for kc in range(8):
                    op('pe', lambda e, hf=hf, kc=kc: e.matmul(py[hf][0][0:n, 0:512], lhsT=yT[:, kc, 0:n],
                                                              rhs=Wout[:, kc, hf * 512:(hf + 1) * 512],
                                                              start=(kc == 0), stop=(kc == 7)),
                       reads=['yTa', 'yTb', 'Wout'], writes=[py[hf][1]])
                op('act', lambda e, hf=hf: e.activation(out=junkb[0:n, hf * 512:(hf + 1) * 512],
                                                        in_=py[hf][0][0:n, 0:512], func=AF.Square,
                                                        accum_out=sm[0:n, 66 + hf:67 + hf]),
                   reads=[py[hf][1]], writes=['junkb', 'sm_y%d' % hf])
            op('dve', lambda e: e.tensor_tensor(out=sm[0:n, 66:67], in0=sm[0:n, 66:67], in1=sm[0:n, 67:68], op=ALU.add),
               reads=['sm_y0', 'sm_y1'], writes=['sm_y0'])
            rstd_of(sm[0:n, 66:67], sm[0:n, 67:68], ['sm_y0'], 'sm_y1', 65, n)
            for hf in range(2):
                op('dve', lambda e, hf=hf: e.scalar_tensor_tensor(
                    out=x1[0:n, hf * 512:(hf + 1) * 512], in0=py[hf][0][0:n, 0:512], scalar=sm[0:n, 67:68],
                    in1=gpost[0:n, hf * 512:(hf + 1) * 512], op0=ALU.mult, op1=ALU.mult),
                   reads=[py[hf][1], 'sm_y1', 'gpost'], writes=['x1'])
            op('pool', lambda e: e.tensor_tensor(out=x1[0:n], in0=x1[0:n], in1=x[0:n], op=ALU.add),
               reads=['x1', xk], writes=['x1'])
            op('act', lambda e: e.copy(out=hb[0:n], in_=x1[0:n]), reads=['x1'], writes=['hb'])
            transposes(lambda kc: hb[0:n, kc * 128:(kc + 1) * 128], n, 8, hT, ['hT'], ['hb'])
            op('act', lambda e: e.copy(out=pbb[0:n], in_=pin[0:n]), reads=[pink], writes=['pbb'])
            transposes(lambda kc: pbb[0:n, kc * 128:(kc + 1) * 128], n, 2, pT, ['pT'], ['pbb'])
            for hf in range(2):
                pg, pgk = P.bank()
                for kc in range(8):
                    op('pe', lambda e, hf=hf, kc=kc, pg=pg: e.matmul(pg[0:n, 0:512], lhsT=hT[:, kc, 0:n],
                                                                     rhs=Wgate[:, kc, hf * 512:(hf + 1) * 512],
                                                                     start=(kc == 0), stop=(kc == 7)),
                       reads=['hT', 'Wgate'], writes=[pgk])
                op('act', lambda e, hf=hf, pg=pg: e.activation(out=sig[0:n, hf * 512:(hf + 1) * 512],
                                                               in_=pg[0:n, 0:512], func=AF.Sigmoid),
                   reads=[pgk], writes=['sig'])
                pu, puk = P.bank()
                for kc in range(2):
                    op('pe', lambda e, hf=hf, kc=kc, pu=pu: e.matmul(pu[0:n, 0:512], lhsT=pT[:, kc, 0:n],
                                                                     rhs=Wup[:, kc, hf * 512:(hf + 1) * 512],
                                                                     start=(kc == 0), stop=(kc == 1)),
                       reads=['pT', 'Wup'], writes=[puk])
                op('dve', lambda e, hf=hf, pu=pu: e.tensor_tensor(out=sig[0:n, hf * 512:(hf + 1) * 512],
                                                                  in0=pu[0:n, 0:512],
                                                                  in1=sig[0:n, hf * 512:(hf + 1) * 512],
                                                                  op=ALU.mult), reads=[puk, 'sig'], writes=['sig'])
            op('pool', lambda e: e.tensor_tensor(out=sig[0:n], in0=sig[0:n], in1=x1[0:n], op=ALU.add),
               reads=['sig', 'x1'], writes=['sig'])
            P.dma('sp', ydst, sig[0:n], reads=['sig'])


        if with_sample:
            def small(name, shape, dt=F32, src=None):
                t_ = sb(name, shape, dt)
                if src is not None:
                    P.dma('sp', t_[:], cdr[src].ap(), writes=[name])
                return t_
            coss = small('coss', [16, 32], src='coss')
            sins = small('sins', [16, 32], src='sins')
            dins = small('dins', [16, 8], src='dins')
            douts = small('douts', [16, 8], src='douts')
            dts = small('dts', [16, 128], src='dts')
            bmaskf = small('bmaskf', [128, 64], src='bmaskf')
            bmask = small('bmask', [16, 4], src='bmask')
            negnew = small('negnew', [16, 16], src='negnew')
            negp = small('negp', [128, 1], src='negp')
            rc4 = small('rc4', [128, 4], src='rc4')
            iotap = small('iotap', [128, 1], src='iotap')
            YBs = small('YBs', [16, 512], BF16)
            kiTs = small('kiTs', [128, 16], BF16)
            kaTs = small('kaTs', [128, 16], BF16)
            vnew = small('vnew', [128, 2, 64], BF16)
            QI = small('QI', [128, 8, 16], BF16)
            rc8 = small('rc8', [128, 8], src='rc8')
            QAz = small('QAz', [128, 4, 32], BF16)
            idxP = small('idxP', [128, 4], I32)
            ptl = small('ptl', [128, 4], I32)
            ptf = small('ptf', [128, 8])
            idxF = small('idxF', [128, 52])
            idxC = [[small('idxC%d_%d' % (b_, rc), [128, 1], I32) for rc in range(4)] for b_ in range(4)]
            idxK = [[small('idxK%d_%d' % (b_, c_), [128, 1], I32) for c_ in range(8)] for b_ in range(4)]
            idxT = [small('idxT%d' % b_, [128, 1], I32) for b_ in range(4)]
            scoreX = small('scoreX', [128, 4, 2, 4])
            maskX = small('maskX', [128, 4, 2, 4], BF16)
            TCt = small('TCt', [128, 128], BF16)
            Vt = small('Vt', [128, 2, 64], BF16)
            P2 = small('P2', [128, 2, 32], BF16)
            Bn = small('Bn', [16, 8, 4])
            bq = small('bq', [128, 96])
            m1 = small('m1', [128, 64])
            dg32 = small('dg32', [32, 32])
            mT32 = small('mT32', [32, 4])
            den2 = small('den2', [16, 4])
            op('pool', lambda e: e.memset(vnew[:], 0.0), writes=['vnew'])

        def sample_ret():
            n = 16
            x = xt[0]
            xk = 'xt0'
            P.dma('sp', x[0:16, :], x_s.ap(), writes=[xk])
            S0 = sig[:].rearrange("p (b c) -> p b c", b=4)
            for b_ in range(4):
                sv = st_in.ap()[b_].rearrange("(hh hp) k v -> hp k hh v", hp=2)
                for hp in range(2):
                    P.dma('sp', S0[hp * 64:(hp + 1) * 64, b_, :].rearrange("p (hh v) -> p hh v", hh=4), sv[hp],
                          writes=['sig'])
            dcs = tmpf[1][:, 0:256]
            P.dma('sp', dcs, cdr['dcs'].ap(), writes=['tmpf1'])
            S0bf = x1[:, 512:1024].bitcast(BF16).rearrange("p (b c) -> p b c", b=4)
            op('act', lambda e: e.copy(out=S0bf, in_=S0), reads=['sig'], writes=['x1'])
            prenorm(x, xk, n=16)
            if cut <= 21:
                return
            pb, pk = proj_chunk(O_GB - O_QB, 512, n=16)
            op('act', lambda e: e.activation(out=sgb[0:16], in_=pb[0:16, 0:512], func=AF.Silu), reads=[pk],
               writes=['sgb'])
            t1 = tmpf[0][0:16].rearrange("p (h a d) -> p h a d", h=8, a=2)
            t2 = x1[0:16, 0:512].rearrange("p (h a d) -> p h a d", h=8, a=2)
            cosb = coss[:].unsqueeze(1).unsqueeze(1).to_broadcast([16, 8, 2, 32])
            sinb = sins[:].unsqueeze(1).unsqueeze(1).to_broadcast([16, 8, 2, 32])
            for which in range(2):
                pb, pk = proj_chunk((O_QB if which == 0 else O_KB) - O_QB, 512, n=16)
                pv = pb[0:16, 0:512].rearrange("p (h a d) -> p h a d", h=8, a=2)
                op('dve', lambda e: e.tensor_tensor(out=t1, in0=pv, in1=cosb, op=ALU.mult),
                   reads=[pk, 'coss'], writes=['tmpf0'])
                op('dve', lambda e: e.tensor_tensor(out=t2, in0=pv, in1=sinb, op=ALU.mult),
                   reads=[pk, 'sins'], writes=['x1'])
                op('pool', lambda e: e.tensor_tensor(out=t1[:, :, 0, :], in0=t1[:, :, 0, :], in1=t2[:, :, 1, :],
                                                     op=ALU.subtract), reads=['tmpf0', 'x1'], writes=['tmpf0'])
                op('pool', lambda e: e.tensor_tensor(out=t1[:, :, 1, :], in0=t1[:, :, 1, :], in1=t2[:, :, 0, :],
                                                     op=ALU.add), reads=['tmpf0', 'x1'], writes=['tmpf0'])
                t1v = tmpf[0][0:16].rearrange("p (h d) -> p h d", h=8)
                if which == 0:
                    op('pool', lambda e: e.tensor_tensor(out=qbd[0:16], in0=t1v,
                                                         in1=dins[:].unsqueeze(2).to_broadcast([16, 8, 64]),
                                                         op=ALU.mult), reads=['tmpf0', 'dins'], writes=['qbd'])
                else:
                    op('pool', lambda e: e.tensor_scalar(out=kbr[0:16], in0=t1v, scalar1=0.125, scalar2=None,
                                                         op0=ALU.mult), reads=['tmpf0'], writes=['kbr'])
                    op('pool', lambda e: e.tensor_tensor(out=kbd[0:16], in0=t1v,
                                                         in1=douts[:].unsqueeze(2).to_broadcast([16, 8, 64]),
                                                         op=ALU.mult), reads=['tmpf0', 'douts'], writes=['kbd'])
            if cut <= 22:
                return
            pbq, pkq = P.bank()
            pbqv = pbq[:].bitcast(BF16)
            for f in range(4):
                op('pe', lambda e, f=f: e.transpose(out=pbqv[:, f * 128:f * 128 + 16],
                                                    in_=qbd[0:16, 2 * f:2 * f + 2, :].rearrange("p a d -> p (a d)"),
                                                    identity=identb[0:16, 0:16]), reads=['qbd', 'identb'], writes=[pkq])
            qzv = qZ[:].rearrange("p (f a) c -> p a f c", a=2)
            pqv = pbqv[:, 0:512].rearrange("p (f c) -> p f c", f=4)
            op('act', lambda e: e.copy(out=qzv[0:64, 0, :, 0:16], in_=pqv[0:64, :, 0:16]), reads=[pkq], writes=['qZ'])
            op('act', lambda e: e.copy(out=qzv[64:128, 1, :, 0:16], in_=pqv[64:128, :, 0:16]), reads=[pkq],
               writes=['qZ'])
            transposes(lambda f: kbr[0:16, 2 * f:2 * f + 2, :].rearrange("p a d -> p (a d)"), 16, 4, kbT, ['kbT'],
                       ['kbr'])
            pb, pk = proj_chunk(O_VB - O_QB, 512, n=16)
            op('act', lambda e: e.copy(out=vbb[0:16], in_=pb[0:16, 0:512]), reads=[pk], writes=['vbb'])
            if cut <= 23:
                return
            pa, pak = P.bank()
            for h in range(8):
                op('pe', lambda e, h=h: e.matmul(pa[0:16, h * 16:(h + 1) * 16], lhsT=kbT[:, h // 2, 0:16],
                                                 rhs=qZ[:, h, 0:16], start=True, stop=True),
                   reads=['kbT', 'qZ'], writes=[pak])
            op('pool', lambda e: e.memset(AT[:, :, 0:16], 0.0), writes=['AT'])
            op('dve', lambda e: e.tensor_tensor(out=AT[0:16, :, 0:16],
                                                in0=pa[0:16, 0:128].rearrange("p (h c) -> p h c", h=8),
                                                in1=dts[:].rearrange("p (h c) -> p h c", h=8), op=ALU.mult),
               reads=[pak, 'dts'], writes=['AT'])
            if cut <= 24:
                return
            qZb = Pt[0][:].rearrange("p (b h c) -> p b h c", b=4, h=8)
            bmf = bmaskf[:].rearrange("p (b c) -> p b c", b=4)
            for b_ in range(4):
                op('pool', lambda e, b_=b_: e.tensor_tensor(out=qZb[:, b_], in0=qZ[:, :, 0:16],
                                                            in1=bmf[:, b_, :].unsqueeze(1).to_broadcast([128, 8, 16]),
                                                            op=ALU.mult), reads=['qZ', 'bmaskf'], writes=['Pt0'])
            po, pok = P.bank()
            for h in range(8):
                op('pe', lambda e, h=h: e.matmul(po[0:16, h * 64:(h + 1) * 64], lhsT=AT[:, h, 0:16],
                                                 rhs=vbb[:, h * 64:(h + 1) * 64], start=True, stop=False),
                   reads=['AT', 'vbb'], writes=[pok])
                for b_ in range(4):
                    op('pe', lambda e, h=h, b_=b_: e.matmul(
                        po[0:16, h * 64:(h + 1) * 64], lhsT=qZb[:, b_, h, :],
                        rhs=S0bf[:, b_, (h // 2) * 64:(h // 2) * 64 + 64], start=False, stop=(b_ == 3)),
                       reads=['Pt0', 'x1'], writes=[pok])
            if cut <= 25:
                return
            kbdm = junkb[:, :].rearrange("p (b c) -> p b c", b=4)
            op('pool', lambda e: e.memset(junkb[:], 0.0), writes=['junkb'])
            for b_ in range(4):
                op('dve', lambda e, b_=b_: e.tensor_scalar(out=kbdm[0:16, b_, :],
                                                           in0=kbd[0:16].rearrange("p h d -> p (h d)"),
                                                           scalar1=bmask[:, b_:b_ + 1], scalar2=None, op0=ALU.mult),
                   reads=['kbd', 'bmask'], writes=['junkb'])
            pst2 = [P.bank(), P.bank()]
            for b_ in range(4):
                bank, bk = pst2[b_ // 2]
                for h in range(8):
                    lo = (h % 2) * 64
                    c0 = (b_ % 2) * 256 + (h // 2) * 64
                    op('pe', lambda e, h=h, b_=b_, lo=lo, c0=c0, bank=bank: e.matmul(
                        bank[lo:lo + 64, c0:c0 + 64], lhsT=kbdm[:, b_, h * 64:(h + 1) * 64],
                        rhs=vbb[:, h * 64:(h + 1) * 64], start=True, stop=True),
                       reads=['junkb', 'vbb'], writes=[bk])
            if cut <= 26:
                return
            op('pool', lambda e: e.tensor_tensor(out=S0, in0=S0, in1=dcs.unsqueeze(1).to_broadcast([128, 4, 256]),
                                                 op=ALU.mult), reads=['sig', 'tmpf1'], writes=['sig'])
            for i2 in range(2):
                op('dve', lambda e, i2=i2: e.tensor_tensor(
                    out=S0[:, 2 * i2:2 * i2 + 2, :], in0=pst2[i2][0][:, 0:512].rearrange("p (b c) -> p b c", b=2),
                    in1=S0[:, 2 * i2:2 * i2 + 2, :], op=ALU.add), reads=[pst2[i2][1], 'sig'], writes=['sig'])
            for b_ in range(4):
                sv = st_s.ap()[b_].rearrange("(hh hp) k v -> hp k hh v", hp=2)
                for hp in range(2):
                    P.dma('sp', sv[hp], S0[hp * 64:(hp + 1) * 64, b_, :].rearrange("p (hh v) -> p hh v", hh=4),
                          reads=['sig'])
            if cut <= 27:
                return
            headnorm_gate(po, pok, YBs[0:16, :], ['YBs'], n=16)

        def sample_dsa():
            n = 16
            x = xt[0]
            xk = 'xt0'
            P.dma('sp', x[0:16, :], x_s.ap(), writes=[xk])
            P.dma('sp', pt[0][0:16, :], p_s.ap(), writes=['pt0'])
            prenorm(x, xk, n=16)
            pb, pk = proj_chunk(O_KI, 72, n=16)
            op('act', lambda e: e.copy(out=kvf[0:16, 256:320], in_=pb[0:16, 0:64]), reads=[pk], writes=['kvf'])
            for a in range(2):
                op('dve', lambda e, a=a: e.tensor_copy(out=kib[0:16, a, :], in_=pb[0:16, 0:64]), reads=[pk],
                   writes=['kib'])
            op('act', lambda e: e.activation(out=wabs[0:16], in_=pb[0:16, 64:72], func=AF.Abs), reads=[pk],
               writes=['wabs'])
            op('act', lambda e: e.activation(out=sgn[0:16], in_=pb[0:16, 64:72], func=AF.Sign), reads=[pk],
               writes=['sgn'])
            transposes(lambda bb: kib[0:16].rearrange("p a d -> p (a d)"), 16, 1, kiTs[:, 0:16].unsqueeze(1),
                       ['kiTs'], ['kib'])
            pb, pk = proj_chunk(O_KA, 256, n=16)
            op('act', lambda e: e.copy(out=kvf[0:16, 0:256], in_=pb[0:16, 0:256]), reads=[pk], writes=['kvf'])
            op('dve', lambda e: e.tensor_copy(out=kab[0:16], in_=pb[0:16, 0:128]), reads=[pk], writes=['kab'])
            op('dve', lambda e: e.tensor_copy(out=vnew[0:16], in_=pb[0:16, 128:256].rearrange("p (g d) -> p g d", g=2)),
               reads=[pk], writes=['vnew'])
            P.dma('sp', k_s.ap(), kvf[0:16, 0:128], reads=['kvf'])
            P.dma('sp', v_s.ap(), kvf[0:16, 128:256], reads=['kvf'])
            P.dma('sp', ki_s.ap(), kvf[0:16, 256:320], reads=['kvf'])
            transposes(lambda bb: kab[0:16], 16, 1, kaTs[:, 0:16].unsqueeze(1), ['kaTs'], ['kab'])
            pb, pk = proj_chunk(O_QA, 512, n=16)
            op('act', lambda e: e.copy(out=qab[0:16], in_=pb[0:16, 0:512].rearrange("p (g f d) -> p f g d", g=2, f=4)),
               reads=[pk], writes=['qab'])
            transposes(lambda f: qab[0:16, f].rearrange("p g d -> p (g d)"), 16, 4, qaT, ['qaT'], ['qab'])
            pb, pk = proj_chunk(O_QI, 512, n=16)
            op('dve', lambda e: e.tensor_tensor(out=qib[0:16], in0=pb[0:16, 0:512].rearrange("p (h d) -> p h d", h=8),
                                                in1=wabs[0:16].unsqueeze(2).to_broadcast([16, 8, 64]), op=ALU.mult),
               reads=[pk, 'wabs'], writes=['qib'])
            qib2 = tmpf[0][0:16, :].bitcast(BF16).rearrange("p (h a d) -> p h a d", h=8, a=2)
            for a in range(2):
                op('dve', lambda e, a=a: e.tensor_copy(out=qib2[:, :, a, :], in_=qib[0:16]), reads=['qib'],
                   writes=['tmpf0'])
            transposes(lambda h: qib2[:, h].rearrange("p a d -> p (a d)"), 16, 8, QI, ['QI'], ['tmpf0'])
            pb, pk = proj_chunk(O_GA, 512, n=16)
            op('act', lambda e: e.activation(out=sga[0:16], in_=pb[0:16, 0:512], func=AF.Silu), reads=[pk],
               writes=['sga'])
            if cut <= 31:
                return
            P.nbank = 5
            P.brr = 0
            for nk_, ok_ in (('WinA', 'Win'), ('WinB', 'Win'), ('YBa', 'YB'), ('YBb', 'YB')):
                P.alias(nk_, ok_)
            G32 = Win[:, 0:4, :].rearrange("p a n -> p (a n)").bitcast(F32)
            LG = Win[:, 4:8, :].rearrange("p a n -> p (a n)").bitcast(F32).rearrange("p (r c) -> p r c", c=32)
            TCv = YB[:, 0:8, :].rearrange("p a n -> p (a n)").rearrange("p (r c) -> p r c", c=128)
            Vc = YB[:, 8:16, :].rearrange("p a n -> p (a n)").rearrange("p (r g d) -> p r g d", g=2, d=64)
            scoreM = score[:].rearrange("p (b r q) -> p b r q", b=4, q=4)
            maskM = junkb[:].rearrange("p (b r q) -> p b r q", b=4, q=4)
            Pm = R[:].rearrange("p a n -> p (a n)").rearrange("p (r c) -> p r c", c=32)
            RK = ['R%d' % h for h in range(8)]
            GT = x1[:, 0:128]
            oSn = x1[:, 128:256].rearrange("p (g d) -> p g d", g=2)
            LG2 = x1[:, 256:320].rearrange("p (c k) -> p c k", c=2)
            Bl = x1[:, 320:352]
            sgnB = x1[:, 384:512]
            selt = tmpf[1][:, 0:256]
            WW = osb[:].rearrange("p h d -> p (h d)")[:, 0:NIT * 16].rearrange("p (i c) -> p i c", c=16)
            OA = P.banks[5]
            OAK = 'ps5'
            POs = [P.banks[6], P.banks[7]]
            P.dma('sp', selt, cdr['sel'].ap(), writes=['tmpf1'])
            X = tmpf[0][0:16, 0:128].rearrange("p (t h) -> p t h", t=16)
            op('dve', lambda e: e.tensor_tensor(out=X, in0=C['identf'][0:16, 0:16].unsqueeze(2).to_broadcast([16, 16, 8]),
                                                in1=sgn[0:16].unsqueeze(1).to_broadcast([16, 16, 8]), op=ALU.mult),
               reads=['C_identf', 'sgn'], writes=['tmpf0'])
            pb, pk = P.bank()
            op('pe', lambda e: e.matmul(pb[:, 0:128], lhsT=C['onesf'][0:16, :], rhs=tmpf[0][0:16, 0:128],
                                        start=True, stop=True), reads=['tmpf0', 'C_onesf'], writes=[pk])
            op('act', lambda e: e.copy(out=sgnB, in_=pb[:, 0:128]), reads=[pk], writes=['x1'])
            sgnBv = sgnB.rearrange("p (b q h) -> p b h q", b=4, q=4)
            blrev = tmpf[0][:, 128:160].rearrange("p (h q) -> p h q", h=8)
            P.dma('sp', blrev, bass.AP(scr, 129, [[1, 128], [384, 8], [1, 4]]), reads=['scr'], writes=['tmpf0'])
            pb, pk = P.bank()
            op('pe', lambda e: e.matmul(pb[:, 0:32], lhsT=C['jrev'][:], rhs=tmpf[0][:, 128:160], start=True, stop=True),
               reads=['C_jrev', 'tmpf0'], writes=[pk])
            op('act', lambda e: e.copy(out=Bl, in_=pb[:, 0:32]), reads=[pk], writes=['x1'])
            for t_ in range(4):
                for b2 in range(4):
                    j_ = 4 * b2 + t_
                    P.dma('sp', Bn[j_:j_ + 1], bass.AP(scr, 128 - t_, [[0, 1], [384, 8], [1, 4]]), reads=['scr'],
                          writes=['Bn'])
            if cut <= 32:
                return
            for b_ in range(4):
                P.dma('sp', idxP[:, b_:b_ + 1], bass.AP(pt_s, b_ * 128, [[1, 128], [1, 1]]), writes=['idxP'])
                P.dma('sp', ptl[:, b_:b_ + 1], bass.AP(pt_s, b_ * 128 + 127, [[0, 128], [1, 1]]), writes=['ptl'])
            op('dve', lambda e: e.tensor_copy(out=ptf[:, 0:4], in_=idxP[:]), reads=['idxP'], writes=['ptf'])
            op('dve', lambda e: e.tensor_copy(out=ptf[:, 4:8], in_=ptl[:]), reads=['ptl'], writes=['ptf'])
            for b_ in range(4):
                op('dve', lambda e, b_=b_: e.scalar_tensor_tensor(out=idxF[:, 4 * b_:4 * b_ + 4],
                                                                  in0=ptf[:, b_:b_ + 1].to_broadcast([128, 4]),
                                                                  scalar=4.0, in1=rc4[:], op0=ALU.mult, op1=ALU.add),
                   reads=['ptf', 'rc4'], writes=['idxF'])
                op('dve', lambda e, b_=b_: e.scalar_tensor_tensor(out=idxF[:, 16 + 8 * b_:24 + 8 * b_],
                                                                  in0=ptf[:, b_:b_ + 1].to_broadcast([128, 8]),
                                                                  scalar=8.0, in1=rc8[:], op0=ALU.mult, op1=ALU.add),
                   reads=['ptf', 'rc8'], writes=['idxF'])
            op('dve', lambda e: e.tensor_scalar(out=idxF[:, 48:52], in0=ptf[:, 4:8], scalar1=128.0,
                                                scalar2=iotap[:, 0:1], op0=ALU.mult, op1=ALU.add),
               reads=['ptf', 'iotap'], writes=['idxF'])
            for b_ in range(4):
                for rc in range(4):
                    op('dve', lambda e, b_=b_, rc=rc: e.tensor_copy(out=idxC[b_][rc][:],
                                                                    in_=idxF[:, 4 * b_ + rc:4 * b_ + rc + 1]),
                       reads=['idxF'], writes=['idx'])
                for c_ in range(8):
                    op('dve', lambda e, b_=b_, c_=c_: e.tensor_copy(
                        out=idxK[b_][c_][:], in_=idxF[:, 16 + 8 * b_ + c_:17 + 8 * b_ + c_]),
                       reads=['idxF'], writes=['idx'])
                op('dve', lambda e, b_=b_: e.tensor_copy(out=idxT[b_][:], in_=idxF[:, 48 + b_:49 + b_]),
                   reads=['idxF'], writes=['idx'])
            if cut <= 33:
                return
            cki4 = c_ki.ap().rearrange("n (c w) -> (n c) w", w=2048)
            ckirow = c_ki.ap().rearrange("n (r d) -> (n r) d", d=64)
            ck8 = c_k.ap().rearrange("n (c w) -> (n c) w", w=2048)
            ckrow = c_k.ap().rearrange("n (r d) -> (n r) d", d=128)
            cv8 = c_v.ap().rearrange("n (c w) -> (n c) w", w=2048)
            cvrow = c_v.ap().rearrange("n (r d) -> (n r) d", d=128)
            Gs = [Win[:, s_, :] for s_ in range(4)]
            GK = ['G%d' % s_ for s_ in range(4)]
            for k_ in GK:
                P.alias(k_, 'Win')
            gctr = [0]

            def gather(src, idx_tile, width=2048):
                s_ = gctr[0] % 4
                gctr[0] += 1
                P.dma('pool', Gs[s_][:, 0:width], src, reads=['idx'], writes=[GK[s_]], gather_idx=idx_tile[:, :])
                return Gs[s_], GK[s_]
            TCh = [TCv[:, 0:16, :], TCv[:, 16:32, :]]
            TCK = ['YBa0', 'YBa1']
            for k_ in TCK:
                P.alias(k_, 'YB')
            GTb = x1[:, 0:64].bitcast(BF16)

            def score_from(psv, pkey, npart, ni, dst, b_, scr_ap, scr_key):
                Rs = scr_ap[0:npart, 0:ni * 32].rearrange("p (i h q) -> p i h q", h=8, q=4)
                op('act', lambda e: e.activation(out=Rs, in_=psv, func=AF.Relu), reads=[pkey], writes=[scr_key])
                op('dve', lambda e: e.tensor_tensor(
                    out=Rs, in0=Rs, in1=sgnBv[0:npart, b_].unsqueeze(1).to_broadcast([npart, ni, 8, 4]), op=ALU.mult),
                   reads=[scr_key, 'x1'], writes=[scr_key])
                op('dve', lambda e: e.tensor_reduce(out=dst, in_=Rs.rearrange("p i h q -> p i q h"), axis=AX.X,
                                                    op=ALU.add), reads=[scr_key], writes=['score', 'scoreX'])
            scrE = tmpf[0][:, :]
            scrO = sig[:, 0:512]

            op('pool', lambda e: e.memset(scoreX[:], NEG), writes=['score', 'scoreX'])
            op('pool', lambda e: e.memset(m1[:, 0:16], 0.0), writes=['m1'])
            ev = [0]

            def evac(out_ap, in_ap, rk, wk):
                eng = 'act' if ev[0] % 2 == 0 else 'dve'
                ev[0] += 1
                if eng == 'act':
                    op('act', lambda e: e.copy(out=out_ap, in_=in_ap), reads=rk, writes=wk)
                else:
                    op('dve', lambda e: e.tensor_copy(out=out_ap, in_=in_ap), reads=rk, writes=wk)

            def transpose_chunk(G, gk, npairs, dstT, dstk, kparts=128):
                for h0 in range(0, npairs, 8):
                    nb = min(8, npairs - h0)
                    pbk_, pkk = P.bank()
                    pv_ = pbk_[:].bitcast(BF16)
                    for i in range(nb):
                        op('pe', lambda e, i=i: e.transpose(out=pv_[:, i * 128:(i + 1) * 128],
                                                            in_=G[:, (h0 + i) * 128:(h0 + i + 1) * 128],
                                                            identity=identb[:]), reads=[gk, 'identb'], writes=[pkk])
                    evac(dstT[:, h0:h0 + nb, :], pv_[:, 0:nb * 128].rearrange("p (i c) -> p i c", i=nb), [pkk], [dstk])

            tcc = [0]
            for b_ in range(4):
                qrhs = [QI[0:64, :, 4 * b_:4 * b_ + 4], QI[64:128, :, 4 * b_:4 * b_ + 4]]
                for rc in range(4):
                    G, gk = gather(cki4, idxC[b_][rc])
                    tci = tcc[0] % 2
                    tcc[0] += 1
                    transpose_chunk(G, gk, 16, TCh[tci], TCK[tci])
                    pbe, pke = P.bank()
                    pbo, pko = P.bank()
                    for pr in range(16):
                        op('pe', lambda e, pr=pr: e.matmul(pbe[:, pr * 32:(pr + 1) * 32], lhsT=TCh[tci][0:64, pr, :],
                                                           rhs=qrhs[0], start=True, stop=True),
                           reads=[TCK[tci], 'QI'], writes=[pke])
                        op('pe', lambda e, pr=pr: e.matmul(pbo[:, pr * 32:(pr + 1) * 32], lhsT=TCh[tci][64:128, pr, :],
                                                           rhs=qrhs[1], start=True, stop=True),
                           reads=[TCK[tci], 'QI'], writes=[pko])
                    rows = scoreM[:, b_, rc * 32:(rc + 1) * 32, :].rearrange("p (i two) q -> p i two q", two=2)
                    score_from(pbe[:, 0:512].rearrange("p (i h q) -> p i h q", h=8, q=4), pke, 128, 16,
                               rows[:, :, 0, :], b_, scrE, 'tmpf0')
                    score_from(pbo[:, 0:512].rearrange("p (i h q) -> p i h q", h=8, q=4), pko, 128, 16,
                               rows[:, :, 1, :], b_, scrO, 'sig')
                op('dve', lambda e, b_=b_: e.tensor_reduce(out=m1[:, b_:b_ + 1], in_=scoreM[:, b_], axis=AX.XY,
                                                           op=ALU.max, apply_absolute_value=True),
                   reads=['score'], writes=['m1'])
                op('dve', lambda e, b_=b_: e.tensor_scalar(out=scoreM[:, b_], in0=scoreM[:, b_],
                                                           scalar1=negp[:, 0:1], scalar2=None, op0=ALU.add),
                   reads=['score', 'negp'], writes=['score'])
                P.dma('pool', GTb[:, 0:64], ckirow, reads=['idx'], writes=['x1'], gather_idx=idxT[b_][:, :])
                pbk_, pkk = P.bank()
                pv_ = pbk_[:].bitcast(BF16)
                op('pe', lambda e: e.transpose(out=pv_[0:64, 0:128], in_=GTb[:, 0:64], identity=identb[:]),
                   reads=['x1', 'identb'], writes=[pkk])
                evac(TCt[0:64, :], pv_[0:64, 0:128], [pkk], ['TCt'])
                pbs, pks = P.bank()
                op('pe', lambda e: e.matmul(pbs[:, 0:32], lhsT=TCt[0:64, :], rhs=qrhs[0], start=True, stop=True),
                   reads=['TCt', 'QI'], writes=[pks])
                op('pe', lambda e: e.matmul(pbs[0:16, 32:64], lhsT=kiTs[0:64, 0:16], rhs=qrhs[0],
                                            start=True, stop=True), reads=['kiTs', 'QI'], writes=[pks])
                score_from(pbs[:, 0:32].rearrange("p (i h q) -> p i h q", h=8, q=4), pks, 128, 1,
                           scoreX[:, b_, 0:1, :], b_, scrE, 'tmpf0')
                score_from(pbs[0:16, 32:64].rearrange("p (i h q) -> p i h q", h=8, q=4), pks, 16, 1,
                           scoreX[0:16, b_, 1:2, :], b_, scrE, 'tmpf0')
                op('dve', lambda e, b_=b_: e.tensor_reduce(out=m1[:, 4 + b_:5 + b_], in_=scoreX[:, b_, 0, :], axis=AX.X,
                                                           op=ALU.max, apply_absolute_value=True),
                   reads=['scoreX'], writes=['m1'])
                op('dve', lambda e, b_=b_: e.tensor_reduce(out=m1[0:16, 8 + b_:9 + b_], in_=scoreX[0:16, b_, 1, :],
                                                           axis=AX.X, op=ALU.max, apply_absolute_value=True),
                   reads=['scoreX'], writes=['m1'])
                op('dve', lambda e, b_=b_: e.tensor_tensor(
                    out=scoreX[0:16, b_, 1, :], in0=scoreX[0:16, b_, 1, :],
                    in1=negnew[:].rearrange("p (b q) -> p b q", b=4)[:, b_, :], op=ALU.add),
                   reads=['scoreX', 'negnew'], writes=['scoreX'])
            if cut <= 35:
                return
            op('dve', lambda e: e.tensor_reduce(out=m1[:, 16:20], in_=m1[:, 0:12].rearrange("p (c b) -> p b c", b=4),
                                                axis=AX.X, op=ALU.max), reads=['m1'], writes=['m1'])
            pb, pk = P.bank()
            op('pe', lambda e: e.transpose(out=pb[0:4, 0:128], in_=m1[:, 16:20], identity=C['identf'][:]),
               reads=['m1', 'C_identf'], writes=[pk])
            op('dve', lambda e: e.tensor_reduce(out=mT32[0:4, 0:1], in_=pb[0:4, 0:128], axis=AX.X, op=ALU.max),
               reads=[pk], writes=['mT32'])
            op('dve', lambda e: e.tensor_scalar(out=dg32[0:4, 0:4], in0=C['identf'][0:4, 0:4], scalar1=mT32[0:4, 0:1],
                                                scalar2=None, op0=ALU.mult), reads=['mT32', 'C_identf'], writes=['dg32'])
            pb, pk = P.bank()
            op('pe', lambda e: e.matmul(pb[:, 0:4], lhsT=C['onesf'][0:4, :], rhs=dg32[0:4, 0:4], start=True, stop=True),
               reads=['dg32', 'C_onesf'], writes=[pk])
            lo16 = bq[:, 0:16]
            mid16 = bq[:, 16:32]
            pw16 = bq[:, 32:48]
            cp16 = bq[:, 48:64]
            w016 = bq[:, 64:80]
            bd4 = bq[:, 80:84]
            op('dve', lambda e: e.tensor_scalar(out=bd4, in0=pb[:, 0:4], scalar1=1.0, scalar2=None, op0=ALU.add),
               reads=[pk], writes=['bq'])
            bdb = bd4.unsqueeze(2).to_broadcast([128, 4, 4])
            op('dve', lambda e: e.tensor_scalar(out=lo16.rearrange("p (b q) -> p b q", b=4), in0=bdb, scalar1=-1.0,
                                                scalar2=None, op0=ALU.mult), reads=['bq'], writes=['bq'])
            op('dve', lambda e: e.tensor_scalar(out=w016.rearrange("p (b q) -> p b q", b=4), in0=bdb, scalar1=2.0,
                                                scalar2=None, op0=ALU.mult), reads=['bq'], writes=['bq'])
            op('dve', lambda e: e.tensor_tensor(out=WW, in0=w016.unsqueeze(1).to_broadcast([128, NIT, 16]),
                                                in1=C['pw2'][:, 0:NIT].unsqueeze(2).to_broadcast([128, NIT, 16]), op=ALU.mult),
               reads=['bq', 'C_pw2'], writes=['osb'])
            cx = tmpf[0][:, 0:32].rearrange("p (b c q) -> p b c q", b=4, c=2)

            def thr_bc(t16, nr):
                return t16.rearrange("p (b q) -> p b q", b=4).unsqueeze(2).to_broadcast([128, 4, nr, 4])
            for it in range(NIT + 1):
                final = (it == NIT)
                if not final:
                    op('dve', lambda e, it=it: e.tensor_tensor(out=mid16, in0=lo16, in1=WW[:, it, :], op=ALU.add),
                       reads=['bq', 'osb'], writes=['bq'])
                thr = lo16 if final else mid16
                op('dve', lambda e, thr=thr: e.tensor_tensor(out=maskM, in0=scoreM, in1=thr_bc(thr, 128), op=ALU.is_ge),
                   reads=['score', 'bq'], writes=['junkb'])
                if final:
                    op('dve', lambda e, thr=thr: e.tensor_tensor(out=maskX[:], in0=scoreX[:], in1=thr_bc(thr, 2),
                                                                 op=ALU.is_ge), reads=['scoreX', 'bq'], writes=['maskX'])
                    break
                op('dve', lambda e: e.tensor_reduce(out=cp16.rearrange("p (b q) -> p b q", b=4),
                                                    in_=maskM.rearrange("p b r q -> p b q r"), axis=AX.X, op=ALU.add),
                   reads=['junkb'], writes=['bq'])
                op('dve', lambda e, thr=thr: e.tensor_tensor(out=cx, in0=scoreX[:], in1=thr_bc(thr, 2), op=ALU.is_ge),
                   reads=['scoreX', 'bq'], writes=['tmpf0'])
                for c_ in range(2):
                    op('dve', lambda e, c_=c_: e.tensor_tensor(out=cp16.rearrange("p (b q) -> p b q", b=4),
                                                               in0=cp16.rearrange("p (b q) -> p b q", b=4),
                                                               in1=cx[:, :, c_, :], op=ALU.add),
                       reads=['bq', 'tmpf0'], writes=['bq'])
                pb, pk = P.bank()
                op('pe', lambda e, pb=pb: e.matmul(pb[:, 0:16], lhsT=C['onesf'][:], rhs=cp16, start=True, stop=True),
                   reads=['bq', 'C_onesf'], writes=[pk])
                op('dve', lambda e, pb=pb, it=it: e.scalar_tensor_tensor(out=pw16, in0=pb[:, 0:16], scalar=TOPK - 0.5,
                                                                         in1=WW[:, it, :], op0=ALU.is_ge, op1=ALU.mult),
                   reads=[pk, 'osb'], writes=['bq'])
                op('dve', lambda e: e.tensor_tensor(out=lo16, in0=lo16, in1=pw16, op=ALU.add), reads=['bq'],
                   writes=['bq'])
            if cut <= 37:
                return

            op('pool', lambda e: e.memset(QAz[:], 0.0), writes=['QAz'])
            for b_ in range(4):
                for g in range(2):
                    op('act', lambda e, b_=b_, g=g: e.copy(
                        out=QAz[g * 64:(g + 1) * 64, b_, g * 16:(g + 1) * 16].rearrange("p (f q) -> p f q", f=4),
                        in_=qaT[g * 64:(g + 1) * 64, :, 4 * b_:4 * b_ + 4]), reads=['qaT'], writes=['QAz'])
            first_oa = [True]
            for b_ in range(4):
                qz = QAz[:, b_, :]
                for c_ in range(8):
                    G, gk = gather(ck8, idxK[b_][c_])
                    tci = tcc[0] % 2
                    tcc[0] += 1
                    transpose_chunk(G, gk, 16, TCh[tci], TCK[tci])
                    pbl, pkl = P.bank()
                    for r in range(16):
                        op('pe', lambda e, r=r: e.matmul(pbl[:, r * 32:(r + 1) * 32], lhsT=TCh[tci][:, r, :], rhs=qz,
                                                         start=True, stop=True), reads=[TCK[tci], 'QAz'], writes=[pkl])
                    evac(LG[:, c_ * 16:(c_ + 1) * 16, :], pbl[:, 0:512].rearrange("p (r c) -> p r c", c=32),
                         [pkl], ['WinB'])
                P.dma('pool', GTb, ckrow, reads=['idx'], writes=['x1'], gather_idx=idxT[b_][:, :])
                pbk_, pkk = P.bank()
                pv_ = pbk_[:].bitcast(BF16)
                op('pe', lambda e: e.transpose(out=pv_[:, 0:128], in_=GTb, identity=identb[:]),
                   reads=['x1', 'identb'], writes=[pkk])
                evac(TCt[:, :], pv_[:, 0:128], [pkk], ['TCt'])
                op('pool', lambda e: e.memset(LG2, 0.0), writes=['x1'])
                pbt, pkt = P.bank()
                op('pe', lambda e: e.matmul(pbt[:, 0:32], lhsT=TCt[:, :], rhs=qz, start=True, stop=True),
                   reads=['TCt', 'QAz'], writes=[pkt])
                op('pe', lambda e: e.matmul(pbt[0:16, 32:64], lhsT=kaTs[:, 0:16], rhs=qz, start=True, stop=True),
                   reads=['kaTs', 'QAz'], writes=[pkt])
                op('dve', lambda e: e.tensor_tensor(out=LG2[:, 0, :], in0=pbt[:, 0:32], in1=Bl, op=ALU.add),
                   reads=[pkt, 'x1'], writes=['x1'])
                op('dve', lambda e: e.tensor_tensor(out=LG2[0:16, 1, :], in0=pbt[0:16, 32:64],
                                                    in1=Bn[:].rearrange("p h q -> p (h q)"), op=ALU.add),
                   reads=[pkt, 'Bn', 'x1'], writes=['x1'])
                op('dve', lambda e: e.tensor_reduce(out=m1[:, 32:64], in_=LG.rearrange("p r c -> p c r"), axis=AX.X,
                                                    op=ALU.max), reads=['WinB'], writes=['m1'])
                op('dve', lambda e: e.tensor_tensor(out=m1[:, 32:64], in0=m1[:, 32:64], in1=LG2[:, 0, :], op=ALU.max),
                   reads=['m1', 'x1'], writes=['m1'])
                op('dve', lambda e: e.tensor_tensor(out=m1[:, 32:64], in0=m1[:, 32:64], in1=LG2[:, 1, :], op=ALU.max),
                   reads=['m1', 'x1'], writes=['m1'])
                pb, pk = P.bank()
                op('pe', lambda e, pb=pb: e.transpose(out=pb[0:32, 0:128], in_=m1[:, 32:64], identity=C['identf'][:]),
                   reads=['m1', 'C_identf'], writes=[pk])
                op('dve', lambda e, pb=pb: e.tensor_reduce(out=mT32[:, 1:2], in_=pb[0:32, 0:128], axis=AX.X, op=ALU.max),
                   reads=[pk], writes=['mT32'])
                op('dve', lambda e: e.tensor_scalar(out=dg32[:], in0=C['identf'][0:32, 0:32], scalar1=mT32[:, 1:2],
                                                    scalar2=None, op0=ALU.mult), reads=['mT32', 'C_identf'],
                   writes=['dg32'])
                pb, pk = P.bank()
                op('pe', lambda e, pb=pb: e.matmul(pb[:, 0:32], lhsT=C['onesf'][0:32, :], rhs=dg32[:], start=True,
                                                   stop=True), reads=['dg32', 'C_onesf'], writes=[pk])
                op('act', lambda e, pb=pb: e.copy(out=m1[:, 0:32], in_=pb[:, 0:32]), reads=[pk], writes=['m1'])
                op('dve', lambda e: e.tensor_tensor(out=LG, in0=LG, in1=m1[:, 0:32].unsqueeze(1).to_broadcast([128, 128, 32]),
                                                    op=ALU.subtract), reads=['WinB', 'm1'], writes=['WinB'])
                op('dve', lambda e: e.tensor_tensor(out=LG2, in0=LG2, in1=m1[:, 0:32].unsqueeze(1).to_broadcast([128, 2, 32]),
                                                    op=ALU.subtract), reads=['x1', 'm1'], writes=['x1'])
                op('act', lambda e: e.activation(out=Pm, in_=LG, func=AF.Exp, scale=0.125), reads=['WinB'], writes=RK)
                op('act', lambda e: e.activation(out=P2[:], in_=LG2, func=AF.Exp, scale=0.125), reads=['x1'],
                   writes=['P2'])
                op('pool', lambda e, b_=b_: e.tensor_tensor(
                    out=Pm.rearrange("p r (h q) -> p r h q", q=4), in0=Pm.rearrange("p r (h q) -> p r h q", q=4),
                    in1=maskM[:, b_].unsqueeze(2).to_broadcast([128, 128, 8, 4]), op=ALU.mult),
                   reads=RK + ['junkb'], writes=RK)
                op('dve', lambda e, b_=b_: e.tensor_tensor(
                    out=P2[:].rearrange("p c (h q) -> p c h q", q=4), in0=P2[:].rearrange("p c (h q) -> p c h q", q=4),
                    in1=maskX[:, b_].unsqueeze(2).to_broadcast([128, 2, 8, 4]), op=ALU.mult),
                   reads=['P2', 'maskX'], writes=['P2'])
                op('dve', lambda e: e.tensor_reduce(out=m1[:, 32:64], in_=Pm.rearrange("p r c -> p c r"), axis=AX.X,
                                                    op=ALU.add), reads=RK, writes=['m1'])
                for c_ in range(2):
                    op('dve', lambda e, c_=c_: e.tensor_tensor(out=m1[:, 32:64], in0=m1[:, 32:64], in1=P2[:, c_, :],
                                                               op=ALU.add), reads=['m1', 'P2'], writes=['m1'])
                pbd, pkd = P.bank()
                for g in range(2):
                    op('pe', lambda e, g=g, pbd=pbd: e.matmul(pbd[0:16, 2 * g:2 * g + 2], lhsT=m1[:, 32 + g * 16:48 + g * 16],
                                                              rhs=C['onesf'][:, 0:2], start=True, stop=True),
                       reads=['m1', 'C_onesf'], writes=[pkd])
                op('dve', lambda e, pbd=pbd: e.reciprocal(out=den2[:, 0:4], in_=pbd[0:16, 0:4]), reads=[pkd],
                   writes=['den2'])
                for c_ in range(8):
                    G, gk = gather(cv8, idxK[b_][c_])
                    Vs = G[:, 0:2048].rearrange("p (r g d) -> p r g d", g=2, d=64)
                    for g in range(2):
                        for r in range(16):
                            rr = c_ * 16 + r
                            op('pe', lambda e, g=g, r=r, rr=rr, Vs=Vs: e.matmul(
                                POs[g][0:16, 0:64], lhsT=Pm[:, rr, g * 16:(g + 1) * 16], rhs=Vs[:, r, g, :],
                                start=(rr == 0), stop=False), reads=RK + [gk], writes=[POK[g]])
                P.dma('pool', Vt[:].rearrange("p g d -> p (g d)"), cvrow, reads=['idx'], writes=['Vt'],
                      gather_idx=idxT[b_][:, :])
                op('pool', lambda e: e.memset(oSn, 0.0), writes=['x1'])
                for g in range(2):
                    op('pe', lambda e, g=g: e.matmul(POs[g][0:16, 0:64], lhsT=P2[:, 0, g * 16:(g + 1) * 16],
                                                     rhs=Vt[:, g, :], start=False, stop=False),
                       reads=['P2', 'Vt'], writes=[POK[g]])
                    op('pe', lambda e, g=g: e.matmul(POs[g][0:16, 0:64], lhsT=P2[:, 1, g * 16:(g + 1) * 16],
                                                     rhs=vnew[:, g, :], start=False, stop=True),
                       reads=['P2', 'vnew'], writes=[POK[g]])
                    op('dve', lambda e, g=g: e.tensor_scalar(out=oSn[0:16, g, :], in0=POs[g][0:16, 0:64],
                                                             scalar1=den2[:, 2 * g:2 * g + 1], scalar2=None, op0=ALU.mult),
                       reads=[POK[g], 'den2'], writes=['x1'])
                oav = OA[0:16, 0:512].rearrange("p (g h d) -> p g h d", g=2, h=4)
                for h4 in range(4):
                    c0 = (b_ * 4 + h4) * 16
                    for g in range(2):
                        op('pe', lambda e, h4=h4, c0=c0, g=g: e.matmul(
                            oav[:, g, h4, :], lhsT=selt[:, c0:c0 + 16], rhs=oSn[:, g, :], start=first_oa[0],
                            stop=(b_ == 3 and h4 == 3 and g == 1), skip_group_check=True),
                           reads=['tmpf1', 'x1'], writes=[OAK])
                        first_oa[0] = False
                if cut <= 37.6:
                    return
            if cut <= 39:
                return
            op('dve', lambda e: e.tensor_tensor(out=ybb[0:16], in0=OA[0:16, 0:512], in1=sga[0:16], op=ALU.mult),
               reads=[OAK, 'sga'], writes=['ybb'])
            transposes(lambda f: ybb[0:16, f * 128:(f + 1) * 128], 16, 4, yT[:, 0:4, :], ['yTa'], ['ybb'])
            transposes(lambda f: YBs[0:16, f * 128:(f + 1) * 128], 16, 4, yT[:, 4:8, :], ['yTb'], ['YBs'])
            P.dma('sp', x[0:16, :], x_s.ap(), writes=[xk])
            merge(x, xk, pt[0], 'pt0', y_s.ap(), n=16)

        load_win_now(O_QB, 2048)
        n1 = ntiles if stage >= 1 else 0
        if n1 > 0:
            P.ring = [4, 5, 6, 7]
            P.brr = 0
            load_tile(0, False)
            if n1 > 1:
                load_tile(1, False)
            gens = {0: tile_ret(0)}
            next(gens[0], None)
            next(gens[0], None)
            for ti in range(n1):
                g_ = gens.pop(ti)
                if ti + 1 < n1:
                    gens[ti + 1] = tile_ret(ti + 1)
                    next(gens[ti + 1], None)
                next(g_, None)
                if ti + 1 < n1:
                    next(gens[ti + 1], None)
                if ti + 2 < n1:
                    load_tile(ti + 2, False)
                for _ in g_:
                    pass
            P.ring = None
            P.brr = 0
        if with_sample and stage >= 1:
            pass
        stv = st_p.ap().rearrange("(hh hp) k v -> hp k hh v", hp=2)
        for hp in range(2):
            P.dma('sp', stv[hp], S[hp * 64:(hp + 1) * 64, :].rearrange("p (hh v) -> p hh v", hh=4), reads=['S'])
        if with_sample and stage >= 1:
            sample_ret()
        if stage < 2:
            ntiles = 0
        load_win_now(0, O_QB)
        if ntiles > 0:
            load_tile(0, True)

        def do_merge(tj):
            bj = tj % 2
            merge2(xt[bj], 'xt%d' % bj, pt[bj], 'pt%d' % bj, y_p.ap()[tj * 128:(tj + 1) * 128, :])
        if ntiles > 0:
            prenorm(xt[0], 'xt0')
        setup_bias()
        for ti in range(ntiles):
            gen = tile_dsa(ti)
            next(gen, None)
            if ti > 0 and cut > 18:
                do_merge(ti - 1)
            if ti + 1 < ntiles:
                load_tile(ti + 1, True)
            if WARM > 0 and cut > 18 and P.nbank <= 5:
                est_us = NIT * (1.0 + (ti + 1) * 128 / 960.0) - (12.0 if ti > 0 else 0.0)
                for _d in range(max(0, int(est_us * WARM))):
                    op('pe', lambda e: e.matmul(P.banks[5][:, 0:512], lhsT=identb[:], rhs=BT[:, 0, 0, :],
                                                start=True, stop=True), reads=['identb', 'BT'], writes=['ps5'])
            for st_ in gen:
                if st_ == 'bisected' and ti + 1 < ntiles:
                    prenorm(xt[(ti + 1) % 2], 'xt%d' % ((ti + 1) % 2))
        if ntiles > 0 and cut > 18:
            do_merge(ntiles - 1)
        if with_sample and stage >= 2:
            sample_dsa()
        P.finish()
    return nc, consts


_CACHE = {}


def kernel(x_prompt, x_sample, cache_k, cache_v, cache_kidx, state_ret, page_table, p_prompt, p_sample,
           rel_bias, w_in, w_out, g_pre, g_post, w_ple_up, w_ple_gate):
    if 'nc' not in _CACHE:
        _CACHE['nc'] = build()
    nc, consts = _CACHE['nc']
    f = lambda a: np.ascontiguousarray(np.asarray(a, dtype=np.float32))
    ck = f(cache_k[0]).reshape(NPOOL, 16384)
    cv = f(cache_v[0]).reshape(NPOOL, 16384)
    cki = f(cache_kidx[0]).reshape(NPOOL, 8192)
    shared = {
        'w_in': f(w_in[0]), 'w_out': f(w_out[0]), 'w_gate': f(w_ple_gate[0]), 'w_up': f(w_ple_up[0]),
        'g_pre': f(g_pre), 'g_post': f(g_post), 'rel_bias': f(rel_bias), 'c_k': ck, 'c_v': cv, 'c_ki': cki,
    }
    for k, v in consts.items():
        shared['c_' + k] = v
    in_maps = []
    for c in range(8):
        sl = slice(4 * c, 4 * c + 4)
        m = dict(shared)
        m['x_p'] = f(x_prompt[c])
        m['p_p'] = f(p_prompt[0, c])
        m['x_s'] = f(x_sample[sl]).reshape(16, D)
        m['p_s'] = f(p_sample[0, sl]).reshape(16, 256)
        m['st_in'] = f(state_ret[0, sl])
        m['pt_s'] = np.ascontiguousarray(np.asarray(page_table[sl]).astype(np.int32))
        in_maps.append(m)
    res = run_bass_kernel_spmd(nc, in_maps, core_ids=list(range(8)))
    r = res.results
    cat = lambda k: np.stack([r[c][k] for c in range(8)])
    y_p = cat('y_p')
    k_p = cat('k_p').reshape(1, 8, T, 2, 64)
    v_p = cat('v_p').reshape(1, 8, T, 2, 64)
    ki_p = cat('ki_p').reshape(1, 8, T, 64)
    st_p = cat('st_p').reshape(1, 8, 8, 64, 64)
    y_s = cat('y_s').reshape(32, 4, D)
    k_s = cat('k_s').reshape(1, 32, 4, 2, 64)
    v_s = cat('v_s').reshape(1, 32, 4, 2, 64)
    ki_s = cat('ki_s').reshape(1, 32, 4, 64)
    st_s = cat('st_s').reshape(1, 32, 8, 64, 64)
    return (y_p, y_s, k_p, v_p, ki_p, st_p, k_s, v_s, ki_s, st_s)
```

```python
import math
import contextlib
import numpy as np
import concourse.bass as bass
import concourse.mybir as mybir
from concourse.bass_utils import run_bass_kernel_spmd

F32 = mybir.dt.float32
BF16 = mybir.dt.bfloat16
I32 = mybir.dt.int32
ALU = mybir.AluOpType
AF = mybir.ActivationFunctionType
AX = mybir.AxisListType

D = 1024
T = 2048
NT = 16
DIN = 3912
EPS = 1e-6
NIT = 16
TOPK = 256
NPAGE = 128
NPOOL = 5120
PAST = 16384
O_QA, O_KA, O_VA, O_QI, O_KI, O_WI, O_GA, O_QB, O_KB, O_VB, O_GB = (
    0, 512, 640, 768, 1280, 1344, 1352, 1864, 2376, 2888, 3400)
NEG = -1.0e30
MB = 240000.0


class Prog:
    def __init__(self, nc, es):
        self.nc = nc
        self.es = es
        self.eng = {'pe': nc.tensor, 'act': nc.scalar, 'dve': nc.vector, 'pool': nc.gpsimd, 'sp': nc.sync}
        self.sem = {e: es.enter_context(nc.semaphore('s_' + e)) for e in ('pe', 'act', 'dve', 'pool')}
        self.cnt = {e: 0 for e in self.sem}
        self.waited = {e: {} for e in self.eng}
        self.last_w = {}
        self.readers = {}
        nd = {'sp': 8, 'pool': 6, 'act': 2}
        self.dsem = {}
        self.dcnt = {}
        self.drr = {q: 0 for q in nd}
        self.nd = nd
        for q, n in nd.items():
            for j in range(n):
                self.dsem[(q, j)] = es.enter_context(nc.semaphore('d_%s%d' % (q, j)))
                self.dcnt[(q, j)] = 0
        self.nbank = 6
        self.ring = None
        self.brr = 0
        self.banks = [es.enter_context(nc.psum_tensor('psb%d' % b, [128, 512], F32)) for b in range(8)]

    def bank(self):
        ring = self.ring if self.ring is not None else list(range(self.nbank))
        b = ring[self.brr % len(ring)]
        self.brr = (self.brr + 1) % len(ring)
        return self.banks[b], 'ps%d' % b

    def _wait(self, eng, tok):
        kind, s, v = tok
        if kind == 'E' and s == eng and eng == 'pe':
            return
        key = (kind, s)
        if self.waited[eng].get(key, 0) >= v:
            return
        self.waited[eng][key] = v
        semobj = self.sem[s] if kind == 'E' else self.dsem[s]
        self.eng[eng].wait_ge(semobj, v)

    def _deps(self, eng, reads, writes):
        deps = []
        for k in reads:
            if k in self.last_w:
                deps.append(self.last_w[k])
            if k.startswith('ps'):
                deps.extend(t for t in self.readers.get(k, {}).values() if not (t[0] == 'E' and t[1] == eng))
        for k in writes:
            if k in self.last_w:
                deps.append(self.last_w[k])
            deps.extend(self.readers.get(k, {}).values())
        for tok in deps:
            self._wait(eng, tok)

    def _reg(self, tok, reads, writes):
        rk = (tok[0], tok[1])
        for k in reads:
            self.readers.setdefault(k, {})[rk] = tok
        for k in writes:
            self.last_w[k] = tok
            self.readers[k] = {}

    def op(self, eng, fn, reads=(), writes=()):
        self._deps(eng, reads, writes)
        ins = fn(self.eng[eng])
        self.cnt[eng] += 1
        ins.then_inc(self.sem[eng], 1)
        tok = ('E', eng, self.cnt[eng])
        self._reg(tok, reads, writes)
        return tok

    def dma(self, q, out, in_, reads=(), writes=(), gather_idx=None):
        self._deps(q, reads, writes)
        j = self.drr[q]
        self.drr[q] = (j + 1) % self.nd[q]
        prev = self.dcnt[(q, j)]
        if prev > 0:
            self._wait(q, ('D', (q, j), prev))
        if gather_idx is None:
            ins = self.eng[q].dma_start(out=out, in_=in_)
        else:
            ins = self.eng[q].indirect_dma_start(
                out=out, out_offset=None, in_=in_,
                in_offset=bass.IndirectOffsetOnAxis(ap=gather_idx, axis=0))
        self.dcnt[(q, j)] = prev + 16
        ins.then_inc(self.dsem[(q, j)], 16)
        tok = ('D', (q, j), prev + 16)
        self._reg(tok, reads, writes)
        return tok

    def alias(self, new, old):
        if old in self.last_w:
            self.last_w[new] = self.last_w[old]
        self.readers[new] = dict(self.readers.get(old, {}))

    def finish(self):
        for (q, j), v in self.dcnt.items():
            if v > 0:
                self._wait(q, ('D', (q, j), v))
        for e in self.eng:
            for s in self.sem:
                if s != e and self.cnt[s] > 0:
                    self._wait(e, ('E', s, self.cnt[s]))


def _consts():
    c = {}
    half = 32
    inv = np.power(np.float32(10000.0), -np.arange(half, dtype=np.float32) / np.float32(half)).astype(np.float32)

    def cs(pos):
        ang = pos.astype(np.float32)[:, None] * inv[None, :]
        return np.cos(ang).astype(np.float32), np.sin(ang).astype(np.float32)
    cp, sp_ = cs(np.arange(T))
    c['cosp'] = cp
    c['sinp'] = sp_
    pos_s = np.tile(PAST + np.arange(4), 4)
    c_s, s_s = cs(pos_s)
    c['coss'] = c_s
    c['sins'] = s_s
    h = np.arange(8, dtype=np.float32)
    log_g = np.log1p(-np.exp2(-5.0 - h)).astype(np.float64)
    i = np.arange(128, dtype=np.float64)
    c['dinp'] = np.exp((i + 1.0)[:, None] * log_g[None, :]).astype(np.float32)
    c['doutp'] = (np.exp((127.0 - i)[:, None] * log_g[None, :]) * 0.125).astype(np.float32)
    dt = np.exp(-(i + 1.0)[:, None, None] * log_g[None, :, None]) * (i[None, None, :] >= i[:, None, None])
    c['dtp'] = dt.astype(np.float32).reshape(128, 1024)
    dc = np.zeros((128, 4, 64), np.float64)
    for hp in range(2):
        for hh in range(4):
            dc[hp * 64:(hp + 1) * 64, hh, :] = np.exp(128.0 * log_g[2 * hh + hp])
    c['dcp'] = dc.astype(np.float32).reshape(128, 256)
    tt = np.tile(np.arange(4, dtype=np.float64), 4)
    bb = np.repeat(np.arange(4), 4)
    c['dins'] = np.exp((tt + 1.0)[:, None] * log_g[None, :]).astype(np.float32)
    c['douts'] = (np.exp((3.0 - tt)[:, None] * log_g[None, :]) * 0.125).astype(np.float32)
    dts = np.exp(-(tt + 1.0)[:, None, None] * log_g[None, :, None]) * \
        ((tt[None, None, :] >= tt[:, None, None]) & (bb[None, None, :] == bb[:, None, None]))
    c['dts'] = dts.astype(np.float32).reshape(16, 128)
    dcs = np.zeros((128, 4, 64), np.float64)
    for hp in range(2):
        for hh in range(4):
            dcs[hp * 64:(hp + 1) * 64, hh, :] = np.exp(4.0 * log_g[2 * hh + hp])
    c['dcs'] = dcs.astype(np.float32).reshape(128, 256)
    bm = np.zeros((16, 4), np.float32)
    bm[np.arange(16), bb] = 1.0
    c['bmask'] = bm
    q = np.arange(128)
    c['negtri'] = np.where(q[None, :] <= q[:, None], 0.0, NEG).astype(np.float32)
    c['identf'] = np.eye(128, dtype=np.float32)
    c['jrev'] = np.eye(128, dtype=np.float32)[::-1].copy()
    c['onesf'] = np.ones((128, 128), np.float32)
    n = np.arange(-128, 256)
    nn = np.maximum(n, 0)
    nf = np.maximum(nn, 1).astype(np.float32)
    large = 16 + (np.log(nf / np.float32(16)) / np.float32(math.log(128 / 16)) * np.float32(16)).astype(np.int32)
    large = np.minimum(large, 31)
    bucket = np.where(nn < 16, nn, large)
    e = np.zeros((32, 384), np.float32)
    e[bucket, np.arange(384)] = 1.0
    c['ebucket'] = e
    c['iotap'] = np.arange(128, dtype=np.float32)[:, None].copy()
    c['rc4'] = np.tile(np.arange(4, dtype=np.float32)[None, :], (128, 1))
    c['rc8'] = np.tile(np.arange(8, dtype=np.float32)[None, :], (128, 1))
    bmf = np.zeros((128, 4, 16), np.float32)
    for b_ in range(4):
        bmf[:, b_, 4 * b_:4 * b_ + 4] = 1.0
    c['bmaskf'] = bmf.reshape(128, 64)
    nn_ = np.full((16, 4, 4), NEG, np.float32)
    for j_ in range(16):
        for q_ in range(4):
            if (j_ % 4) <= q_:
                nn_[j_, j_ // 4, q_] = 0.0
    c['negnew'] = nn_.reshape(16, 16)
    negp = np.zeros((128, 1), np.float32)
    negp[127, 0] = NEG
    c['negp'] = negp
    sel = np.zeros((16, 4, 4, 16), np.float32)
    for b_ in range(4):
        for h4_ in range(4):
            for q_ in range(4):
                sel[h4_ * 4 + q_, b_, h4_, 4 * b_ + q_] = 1.0
    selp = np.zeros((128, 256), np.float32)
    selp[0:16] = sel.reshape(16, 256)
    c['sel'] = selp
    c['pw2'] = np.tile((0.5 ** (np.arange(NIT + 1, dtype=np.float64) + 1.0))[None, :], (128, 1)).astype(np.float32)
    return c


CONST_SHAPES = None


def build(with_sample=True, ntiles=NT, stage=9, cut=99, WARM=0.0):
    nc = bass.Bass("TRN2", target_bir_lowering=False)
    es = contextlib.ExitStack()
    consts = _consts()

    def din(name, shape, dt=F32):
        return nc.dram_tensor(name, list(shape), dt, kind="ExternalInput")

    def dout(name, shape, dt=F32):
        return nc.dram_tensor(name, list(shape), dt, kind="ExternalOutput")

    x_p = din('x_p', [T, D])
    p_p = din('p_p', [T, 256])
    w_in = din('w_in', [D, DIN])
    w_out = din('w_out', [D, D])
    w_gate = din('w_gate', [D, D])
    w_up = din('w_up', [256, D])
    g_pre = din('g_pre', [1, D])
    g_post = din('g_post', [1, D])
    rel_bias = din('rel_bias', [32, 8])
    cdr = {k: din('c_' + k, v.shape) for k, v in consts.items()}
    y_p = dout('y_p', [T, D])
    k_p = dout('k_p', [T, 128])
    v_p = dout('v_p', [T, 128])
    ki_p = dout('ki_p', [T, 64])
    st_p = dout('st_p', [8, 64, 64])
    scr = nc.dram_tensor('scr', [8, 384], F32, kind="Internal")
    if with_sample:
        x_s = din('x_s', [16, D])
        p_s = din('p_s', [16, 256])
        st_in = din('st_in', [4, 8, 64, 64])
        pt_s = din('pt_s', [4, 128], I32)
        c_k = din('c_k', [NPOOL, 16384])
        c_v = din('c_v', [NPOOL, 16384])
        c_ki = din('c_ki', [NPOOL, 8192])
        y_s = dout('y_s', [16, D])
        k_s = dout('k_s', [16, 128])
        v_s = dout('v_s', [16, 128])
        ki_s = dout('ki_s', [16, 64])
        st_s = dout('st_s', [4, 8, 64, 64])

    with es:
        P = Prog(nc, es)
        op = P.op

        def sb(name, shape, dt=F32):
            return es.enter_context(nc.sbuf_tensor(name, list(shape), dt))

        Win = sb('Win', [128, 8, 2048], BF16)
        Wout = sb('Wout', [128, 8, D], BF16)
        Wgate = sb('Wgate', [128, 8, D], BF16)
        Wup = sb('Wup', [128, 2, D], BF16)
        w_in_v = w_in.ap().rearrange("(kc p) n -> p kc n", p=128)

        win_loads = []

        def load_win(base, ncols):
            win_loads.append((base, ncols))
        w_out_v = w_out.ap().rearrange("(kc p) n -> p kc n", p=128)
        w_gate_v = w_gate.ap().rearrange("(kc p) n -> p kc n", p=128)
        w_up_v = w_up.ap().rearrange("(kc p) n -> p kc n", p=128)

        for kc in range(8):
            P.dma('pool', Wout[:, kc, :], w_out_v[:, kc, :], writes=['Wout'])
            P.dma('pool', Wgate[:, kc, :], w_gate_v[:, kc, :], writes=['Wgate'])
        for kc in range(2):
            P.dma('pool', Wup[:, kc, :], w_up_v[:, kc, :], writes=['Wup'])

        gpre = sb('gpre', [128, D])
        gpost = sb('gpost', [128, D])
        P.dma('sp', gpre[:], g_pre.ap().partition_broadcast(128), writes=['gpre'])
        P.dma('sp', gpost[:], g_post.ap().partition_broadcast(128), writes=['gpost'])
        C = {}
        for k in ('dinp', 'doutp', 'dtp', 'dcp', 'negtri', 'identf', 'jrev', 'onesf', 'pw2'):
            C[k] = sb('C_' + k, consts[k].shape)
            P.dma('sp', C[k][:], cdr[k].ap(), writes=['C_' + k])
        cosp = sb('cosp', [128, NT, 32])
        sinp = sb('sinp', [128, NT, 32])
        P.dma('sp', cosp[:], cdr['cosp'].ap().rearrange("(t p) d -> p t d", p=128), writes=['cosp'])
        P.dma('sp', sinp[:], cdr['sinp'].ap().rearrange("(t p) d -> p t d", p=128), writes=['sinp'])
        identb = sb('identb', [128, 128], BF16)
        op('dve', lambda e: e.tensor_copy(out=identb[:], in_=C['identf'][:]), reads=['C_identf'], writes=['identb'])

        xt = [sb('xt%d' % i, [128, D]) for i in range(2)]
        pt = [sb('pt%d' % i, [128, 256]) for i in range(2)]
        junkb = sb('junkb', [128, T], BF16)
        hb = sb('hb', [128, D], BF16)
        hT = sb('hT', [128, 8, 128], BF16)
        sm = sb('sm', [128, 72])
        tmpf = [sb('tmpf%d' % i, [128, 512]) for i in range(2)]
        score = sb('score', [128, T])
        R = sb('R', [128, 8, 512], BF16)
        maskT = R[:, 0:4, :].rearrange("p a (b q) -> p (a b) q", q=128)
        MK = ['R0', 'R1', 'R2', 'R3']
        osb = sb('osb', [128, 8, 64])
        x1 = sb('x1', [128, D])
        sqb = x1[:, 0:512]
        ybb = sb('ybb', [128, 512], BF16)
        YB = sb('YB', [128, NT, 512], BF16)
        yT = sb('yT', [128, 8, 128], BF16)
        sig = sb('sig', [128, D])
        kaT = sb('kaT', [128, T], BF16)
        kiT = sb('kiT', [128, T], BF16)
        Vaug = sb('Vaug', [128, NT, 2, 65], BF16)
        S = sb('S', [128, 256])
        Sbf = sb('Sbf', [128, 256], BF16)
        runmax = sb('runmax', [4, 1])
        kvf = sb('kvf', [128, 320])
        kab = sb('kab', [128, 128], BF16)
        kib = sb('kib', [128, 2, 64], BF16)
        wabs = sb('wabs', [128, 8])
        sgn = sb('sgn', [128, 8])
        Dg = sb('Dg', [128, 8, 128], BF16)
        qab = sb('qab', [128, 4, 2, 64], BF16)
        qaT = sb('qaT', [128, 4, 128], BF16)
        qib = sb('qib', [128, 8, 64], BF16)
        qiT = sb('qiT', [128, 4, 128], BF16)
        sga = sb('sga', [128, 512], BF16)
        sgb = sb('sgb', [128, 512], BF16)
        qbd = sb('qbd', [128, 8, 64], BF16)
        kbr = sb('kbr', [128, 8, 64], BF16)
        kbd = sb('kbd', [128, 8, 64], BF16)
        vbb = sb('vbb', [128, 512], BF16)
        qZ = sb('qZ', [128, 8, 128], BF16)
        kbT = sb('kbT', [128, 4, 128], BF16)
        AT = sb('AT', [128, 8, 128], BF16)
        Pt = [sb('Pt%d' % i, [128, 512], BF16) for i in range(3)]
        pbb = sb('pbb', [128, 256], BF16)
        pT = sb('pT', [128, 2, 128], BF16)
        mstat = sb('mstat', [128, 4])
        mT = sb('mT', [4, 128])
        dg4 = sb('dg4', [4, 4])
        negc = sb('negc', [128, 2])
        W16 = sb('W16', [128, NIT + 1])
        BT = sb('BT', [128, 2, 2, 512], BF16)
        qaZ = sb('qaZ', [128, 2, 4, 128], BF16)
        rb = sb('rb', [32, 8])
        ffar = sb('ffar', [8, 1])
        PO = [P.banks[6], P.banks[7]]
        POK = ['ps6', 'ps7']
        ptr = [0]

        def load_win_now(base, ncols):
            stg = [(score[:], ['score']), (R[:].rearrange("p a n -> p (a n)").bitcast(F32), ['R%d' % h for h in range(8)])]
            for kc in range(8):
                st_ap, st_k = stg[kc % 2]
                P.dma('sp', st_ap[:, 0:ncols], w_in_v[:, kc, base:base + ncols], writes=st_k)
                if kc % 2 == 0:
                    op('act', lambda e, kc=kc, st_ap=st_ap: e.copy(out=Win[:, kc, 0:ncols], in_=st_ap[:, 0:ncols]),
                       reads=st_k, writes=['Win'])
                else:
                    op('dve', lambda e, kc=kc, st_ap=st_ap: e.tensor_copy(out=Win[:, kc, 0:ncols], in_=st_ap[:, 0:ncols]),
                       reads=st_k, writes=['Win'])

        op('pool', lambda e: e.memset(Vaug[:], 1.0), writes=['Vaug'])
        op('pool', lambda e: e.memset(S[:], 0.0), writes=['S'])
        op('pool', lambda e: e.memset(Sbf[:], 0.0), writes=['Sbf'])
        op('pool', lambda e: e.memset(runmax[:], 0.0), writes=['runmax'])
        op('pool', lambda e: e.memset(qZ[:], 0.0), writes=['qZ'])
        op('pool', lambda e: e.memset(qaZ[:], 0.0), writes=['qaZ'])

        eb = tmpf[0][0:32, 0:384]
        fsb = tmpf[1][0:8, 0:384]
        btrev = score[:].rearrange("p (a h q) -> p a h q", a=2, h=8)
        P.dma('sp', eb, cdr['ebucket'].ap(), writes=['tmpf0'])
        P.dma('sp', rb[:], rel_bias.ap(), writes=['rb'])
        pb, pk = P.bank()
        op('pe', lambda e: e.matmul(pb[0:8, 0:384], lhsT=rb[:, :], rhs=eb, start=True, stop=True),
           reads=['rb', 'tmpf0'], writes=[pk])
        op('dve', lambda e: e.tensor_copy(out=ffar[:], in_=pb[0:8, 383:384]), reads=[pk], writes=['ffar'])
        op('dve', lambda e: e.tensor_scalar(out=fsb, in0=pb[0:8, 0:384], scalar1=ffar[:, 0:1], scalar2=8.0,
                                            op0=ALU.subtract, op1=ALU.mult), reads=[pk, 'ffar'], writes=['tmpf1'])
        P.dma('sp', scr.ap(), fsb, reads=['tmpf1'], writes=['scr'])
        for ty in range(2):
            P.dma('sp', btrev[:, ty, :, :], bass.AP(scr, 1 + 128 * ty, [[1, 128], [384, 8], [1, 128]]),
                  reads=['scr'], writes=['score'])
        for ty in range(2):
            for g in range(2):
                pb, pk = P.bank()
                op('pe', lambda e: e.matmul(pb[:, 0:512], lhsT=C['jrev'][:],
                                            rhs=btrev[:, ty, 4 * g:4 * g + 4, :], start=True, stop=True),
                   reads=['C_jrev', 'score'], writes=[pk])
                op('act', lambda e: e.copy(out=BT[:, ty, g, :], in_=pb[:, 0:512]), reads=[pk], writes=['BT'])

        def transposes(src_fn, n_in, nblk, dst, dstkeys, srckeys, eng='act', npart=128):
            pb, pk = P.bank()
            pbv = pb[:].bitcast(BF16)
            for b in range(nblk):
                op('pe', lambda e, b=b: e.transpose(out=pbv[0:npart, b * 128:b * 128 + n_in], in_=src_fn(b),
                                                    identity=identb[0:n_in, 0:n_in]),
                   reads=list(srckeys) + ['identb'], writes=[pk])
            src = pbv[0:npart, 0:nblk * 128].rearrange("p (b t) -> p b t", t=128)[:, :, 0:n_in]
            if eng == 'act':
                op('act', lambda e: e.copy(out=dst[0:npart, 0:nblk, 0:n_in], in_=src), reads=[pk], writes=dstkeys)
            else:
                op(eng, lambda e: e.tensor_copy(out=dst[0:npart, 0:nblk, 0:n_in], in_=src), reads=[pk],
                   writes=dstkeys)

        def load_tile(ti, with_p):
            b = ti % 2
            P.dma('sp', xt[b][:], x_p.ap()[ti * 128:(ti + 1) * 128, :], writes=['xt%d' % b])
            if with_p:
                P.dma('sp', pt[b][:], p_p.ap()[ti * 128:(ti + 1) * 128, :], writes=['pt%d' % b])

        def proj_chunk(c0, w, n=128, bank=None):
            if bank is None:
                pb, pk = P.bank()
            else:
                pb, pk = P.banks[bank], 'ps%d' % bank
            for kc in range(8):
                op('pe', lambda e, kc=kc: e.matmul(pb[0:n, 0:w], lhsT=hT[:, kc, 0:n], rhs=Win[:, kc, c0:c0 + w],
                                                   start=(kc == 0), stop=(kc == 7)),
                   reads=['hT', 'Win'], writes=[pk])
            return pb, pk

        def rstd_of(ss_ap, out_ap, sskeys, outkey, tmpcol, n=128):
            op('act', lambda e: e.activation(out=sm[0:n, tmpcol:tmpcol + 1], in_=ss_ap, func=AF.Sqrt,
                                             bias=EPS, scale=1.0 / D), reads=sskeys, writes=['sm_t%d' % tmpcol])
            op('dve', lambda e: e.reciprocal(out=out_ap, in_=sm[0:n, tmpcol:tmpcol + 1]),
               reads=['sm_t%d' % tmpcol], writes=[outkey])

        def prenorm(x, xk, n=128):
            op('act', lambda e: e.activation(out=hb[0:n, :], in_=x[0:n, :], func=AF.Square,
                                             accum_out=sm[0:n, 0:1]), reads=[xk], writes=['hb', 'sm_ss'])
            rstd_of(sm[0:n, 0:1], sm[0:n, 1:2], ['sm_ss'], 'sm_rstd', 64, n)
            op('dve', lambda e: e.scalar_tensor_tensor(out=hb[0:n, :], in0=x[0:n, :], scalar=sm[0:n, 1:2],
                                                       in1=gpre[0:n, :], op0=ALU.mult, op1=ALU.mult),
               reads=[xk, 'sm_rstd', 'gpre'], writes=['hb'])
            transposes(lambda kc: hb[0:n, kc * 128:(kc + 1) * 128], n, 8, hT, ['hT'], ['hb'])

        def tile_ret(ti):
            b = ti % 2
            x = xt[b]
            xk = 'xt%d' % b
            prenorm(x, xk)
            if cut <= 1:
                return
            yield 0
            pj = {}
            for nm_, bk_, c0_ in (('gb', 0, O_GB), ('qb', 1, O_QB), ('kb', 2, O_KB), ('vb', 3, O_VB)):
                pj[nm_] = proj_chunk(c0_ - O_QB, 512, bank=bk_)
            yield 1
            pb, pk = pj['gb']
            op('act', lambda e: e.activation(out=sgb[:], in_=pb[:, 0:512], func=AF.Silu), reads=[pk], writes=['sgb'])
            if cut <= 2:
                return
            t1 = tmpf[0][:].rearrange("p (h a d) -> p h a d", h=8, a=2)
            t2 = tmpf[1][:].rearrange("p (h a d) -> p h a d", h=8, a=2)
            cosb = cosp[:, ti, :].unsqueeze(1).unsqueeze(1).to_broadcast([128, 8, 2, 32])
            sinb = sinp[:, ti, :].unsqueeze(1).unsqueeze(1).to_broadcast([128, 8, 2, 32])
            for which in range(2):
                pb, pk = pj['qb' if which == 0 else 'kb']
                pv = pb[:, 0:512].rearrange("p (h a d) -> p h a d", h=8, a=2)
                op('dve', lambda e: e.tensor_tensor(out=t1, in0=pv, in1=cosb, op=ALU.mult),
                   reads=[pk, 'cosp'], writes=['tmpf0'])
                op('dve', lambda e: e.tensor_tensor(out=t2, in0=pv, in1=sinb, op=ALU.mult),
                   reads=[pk, 'sinp'], writes=['tmpf1'])
                op('dve', lambda e: e.tensor_tensor(out=t1[:, :, 0, :], in0=t1[:, :, 0, :], in1=t2[:, :, 1, :],
                                                    op=ALU.subtract), reads=['tmpf0', 'tmpf1'], writes=['tmpf0'])
                op('pool', lambda e: e.tensor_tensor(out=t1[:, :, 1, :], in0=t1[:, :, 1, :], in1=t2[:, :, 0, :],
                                                     op=ALU.add), reads=['tmpf0', 'tmpf1'], writes=['tmpf0'])
                t1v = tmpf[0][:].rearrange("p (h d) -> p h d", h=8)
                if which == 0:
                    op('pool', lambda e: e.tensor_tensor(out=qbd[:], in0=t1v,
                                                         in1=C['dinp'][:].unsqueeze(2).to_broadcast([128, 8, 64]),
                                                         op=ALU.mult), reads=['tmpf0', 'C_dinp'], writes=['qbd'])
                else:
                    op('act', lambda e: e.mul(out=kbr[:], in_=t1v, mul=0.125), reads=['tmpf0'], writes=['kbr'])
                    op('dve', lambda e: e.tensor_tensor(out=kbd[:], in0=t1v,
                                                        in1=C['doutp'][:].unsqueeze(2).to_broadcast([128, 8, 64]),
                                                        op=ALU.mult), reads=['tmpf0', 'C_doutp'], writes=['kbd'])
            pb, pk = pj['vb']
            op('act', lambda e: e.copy(out=vbb[:], in_=pb[:, 0:512]), reads=[pk], writes=['vbb'])
            yield 2
            if cut <= 3:
                return
            pbq, pkq = P.bank()
            pbqv = pbq[:].bitcast(BF16)
            for f in range(4):
                op('pe', lambda e, f=f: e.transpose(out=pbqv[:, f * 128:(f + 1) * 128],
                                                    in_=qbd[:, 2 * f:2 * f + 2, :].rearrange("p a d -> p (a d)"),
                                                    identity=identb[:]), reads=['qbd', 'identb'], writes=[pkq])
            qzv = qZ[:].rearrange("p (f a) c -> p a f c", a=2)
            pqv = pbqv[:, 0:512].rearrange("p (f c) -> p f c", f=4)
            op('act', lambda e: e.copy(out=qzv[0:64, 0, :, :], in_=pqv[0:64]), reads=[pkq], writes=['qZ'])
            op('act', lambda e: e.copy(out=qzv[64:128, 1, :, :], in_=pqv[64:128]), reads=[pkq], writes=['qZ'])
            transposes(lambda f: kbr[:, 2 * f:2 * f + 2, :].rearrange("p a d -> p (a d)"), 128, 4, kbT, ['kbT'],
                       ['kbr'])
            if cut <= 4:
                return
            pa, pak = P.bank()
            pb2, pbk = P.bank()
            for h in range(8):
                tgt, tk = (pa, pak) if h < 4 else (pb2, pbk)
                op('pe', lambda e, h=h, tgt=tgt: e.matmul(
                    tgt[:, (h % 4) * 128:(h % 4) * 128 + 128], lhsT=kbT[:, h // 2, :],
                    rhs=qZ[:, h, :], start=True, stop=True),
                   reads=['kbT', 'qZ'], writes=[tk])
            dtv = C['dtp'][:].rearrange("p (h c) -> p h c", h=8)
            op('dve', lambda e: e.tensor_tensor(out=AT[:, 0:4, :], in0=pa[:, 0:512].rearrange("p (h c) -> p h c", h=4),
                                                in1=dtv[:, 0:4, :], op=ALU.mult), reads=[pak, 'C_dtp'], writes=['AT'])
            op('dve', lambda e: e.tensor_tensor(out=AT[:, 4:8, :], in0=pb2[:, 0:512].rearrange("p (h c) -> p h c", h=4),
                                                in1=dtv[:, 4:8, :], op=ALU.mult), reads=[pbk, 'C_dtp'], writes=['AT'])
            if cut <= 5:
                return
            po, pok = P.bank()
            for h in range(8):
                lo = (h % 2) * 64
                op('pe', lambda e, h=h: e.matmul(po[:, h * 64:(h + 1) * 64], lhsT=AT[:, h, :],
                                                 rhs=vbb[:, h * 64:(h + 1) * 64], start=True, stop=False),
                   reads=['AT', 'vbb'], writes=[pok])
                op('pe', lambda e, h=h: e.matmul(po[:, h * 64:(h + 1) * 64], lhsT=qZ[:, h, :],
                                                 rhs=Sbf[:, (h // 2) * 64:(h // 2) * 64 + 64],
                                                 start=False, stop=True),
                   reads=['qZ', 'Sbf'], writes=[pok])
            if cut <= 6:
                return
            pst, pstk = P.bank()
            for h in range(8):
                lo = (h % 2) * 64
                op('pe', lambda e, h=h, lo=lo: e.matmul(pst[lo:lo + 64, (h // 2) * 64:(h // 2) * 64 + 64],
                                                        lhsT=kbd[:, h, :], rhs=vbb[:, h * 64:(h + 1) * 64],
                                                        start=True, stop=True),
                   reads=['kbd', 'vbb'], writes=[pstk])
            if cut <= 7:
                return
            op('pool', lambda e: e.tensor_tensor(out=S[:], in0=S[:], in1=C['dcp'][:], op=ALU.mult),
               reads=['S', 'C_dcp'], writes=['S'])
            op('dve', lambda e: e.tensor_tensor(out=S[:], in0=pst[:, 0:256], in1=S[:], op=ALU.add),
               reads=[pstk, 'S'], writes=['S'])
            op('act', lambda e: e.copy(out=Sbf[:], in_=S[:]), reads=['S'], writes=['Sbf'])
            if cut <= 8:
                return
            headnorm_gate(po, pok, YB[:, ti, :], ['YB'])

        def headnorm_gate(po, pok, dst, dstkeys, n=128):
            op('act', lambda e: e.copy(out=osb[0:n], in_=po[0:n, 0:512].rearrange("p (h d) -> p h d", h=8)),
               reads=[pok], writes=['osb'])
            op('act', lambda e: e.activation(out=sqb[0:n], in_=po[0:n, 0:512], func=AF.Square), reads=[pok],
               writes=['x1'])
            op('dve', lambda e: e.tensor_reduce(out=sm[0:n, 24:32], in_=osb[0:n], axis=AX.X, op=ALU.add),
               reads=['osb'], writes=['sm_s1'])
            op('dve', lambda e: e.tensor_reduce(out=sm[0:n, 32:40], in_=sqb[0:n].rearrange("p (h d) -> p h d", h=8),
                                                axis=AX.X, op=ALU.add), reads=['x1'], writes=['sm_s2'])
            op('dve', lambda e: e.tensor_scalar(out=sm[0:n, 24:32], in0=sm[0:n, 24:32], scalar1=1.0 / 64, scalar2=None,
                                                op0=ALU.mult), reads=['sm_s1'], writes=['sm_s1'])
            op('dve', lambda e: e.tensor_tensor(out=sm[0:n, 40:48], in0=sm[0:n, 24:32], in1=sm[0:n, 24:32],
                                                op=ALU.mult), reads=['sm_s1'], writes=['sm_msq'])
            op('dve', lambda e: e.scalar_tensor_tensor(out=sm[0:n, 32:40], in0=sm[0:n, 32:40], scalar=1.0 / 64,
                                                       in1=sm[0:n, 40:48], op0=ALU.mult, op1=ALU.subtract),
               reads=['sm_s2', 'sm_msq'], writes=['sm_s2'])
            op('act', lambda e: e.activation(out=sm[0:n, 40:48], in_=sm[0:n, 32:40], func=AF.Sqrt, bias=EPS,
                                             scale=1.0), reads=['sm_s2'], writes=['sm_msq'])
            op('dve', lambda e: e.reciprocal(out=sm[0:n, 48:56], in_=sm[0:n, 40:48]), reads=['sm_msq'],
               writes=['sm_hr'])
            op('dve', lambda e: e.tensor_tensor(out=osb[0:n], in0=osb[0:n],
                                                in1=sm[0:n, 24:32].unsqueeze(2).to_broadcast([n, 8, 64]),
                                                op=ALU.subtract), reads=['osb', 'sm_s1'], writes=['osb'])
            op('pool', lambda e: e.tensor_tensor(out=osb[0:n], in0=osb[0:n],
                                                 in1=sm[0:n, 48:56].unsqueeze(2).to_broadcast([n, 8, 64]),
                                                 op=ALU.mult), reads=['osb', 'sm_hr'], writes=['osb'])
            op('pool', lambda e: e.tensor_tensor(out=dst, in0=osb[0:n].rearrange("p h d -> p (h d)"), in1=sgb[0:n],
                                                 op=ALU.mult), reads=['osb', 'sgb'], writes=dstkeys)

        def tile_dsa(ti):
            b = ti % 2
            x = xt[b]
            xk = 'xt%d' % b
            pb, pk = proj_chunk(O_KI, 72)
            op('act', lambda e: e.copy(out=kvf[:, 256:320], in_=pb[:, 0:64]), reads=[pk], writes=['kvf'])
            if cut <= 10.1:
                return
            for a in range(2):
                op('dve', lambda e, a=a: e.tensor_copy(out=kib[:, a, :], in_=pb[:, 0:64]), reads=[pk], writes=['kib'])
            if cut <= 10.2:
                return
            op('act', lambda e: e.activation(out=wabs[:], in_=pb[:, 64:72], func=AF.Abs), reads=[pk], writes=['wabs'])
            op('act', lambda e: e.activation(out=sgn[:], in_=pb[:, 64:72], func=AF.Sign), reads=[pk], writes=['sgn'])
            if cut <= 10.3:
                return
            for h in range(8):
                op('dve', lambda e, h=h: e.tensor_scalar(out=Dg[:, h, :], in0=identb[:], scalar1=sgn[:, h:h + 1],
                                                         scalar2=None, op0=ALU.mult),
                   reads=['identb', 'sgn'], writes=['Dg'])
            if cut <= 10.4:
                return
            later = []
            later2 = []
            later.append(lambda: transposes(lambda bb: kib[:].rearrange("p a d -> p (a d)"), 128, 1,
                                            kiT[:, ti * 128:(ti + 1) * 128].unsqueeze(1), ['kiT'], ['kib']))
            if cut <= 11:
                return
            pb, pk = proj_chunk(O_KA, 256)
            op('act', lambda e: e.copy(out=kvf[:, 0:256], in_=pb[:, 0:256]), reads=[pk], writes=['kvf'])
            op('dve', lambda e: e.tensor_copy(out=kab[:], in_=pb[:, 0:128]), reads=[pk], writes=['kab'])
            op('dve', lambda e: e.tensor_copy(out=Vaug[:, ti, :, 0:64],
                                              in_=pb[:, 128:256].rearrange("p (g d) -> p g d", g=2)),
               reads=[pk], writes=['Vaug'])
            P.dma('sp', k_p.ap()[ti * 128:(ti + 1) * 128, :], kvf[:, 0:128], reads=['kvf'])
            P.dma('sp', v_p.ap()[ti * 128:(ti + 1) * 128, :], kvf[:, 128:256], reads=['kvf'])
            P.dma('sp', ki_p.ap()[ti * 128:(ti + 1) * 128, :], kvf[:, 256:320], reads=['kvf'])
            later.append(lambda: transposes(lambda bb: kab[:], 128, 1, kaT[:, ti * 128:(ti + 1) * 128].unsqueeze(1),
                                            ['kaT'], ['kab']))
            if cut <= 12:
                return
            pb, pk = proj_chunk(O_QA, 512)
            op('act', lambda e: e.copy(out=qab[:], in_=pb[:, 0:512].rearrange("p (g f d) -> p f g d", g=2, f=4)),
               reads=[pk], writes=['qab'])
            op('act', lambda e: e.activation(out=sqb, in_=pb[:, 0:512], func=AF.Square), reads=[pk], writes=['x1'])
            op('dve', lambda e: e.tensor_reduce(out=sm[:, 8:16], in_=sqb.rearrange("p (h d) -> p h d", h=8),
                                                axis=AX.X, op=ALU.add), reads=['x1'], writes=['sm_qn'])
            op('dve', lambda e: e.tensor_reduce(out=mstat[:, 0:2], in_=sm[:, 8:16].rearrange("p (g f) -> p g f", g=2),
                                                axis=AX.X, op=ALU.max), reads=['sm_qn'], writes=['mstat'])
            def _qa_tr():
                transposes(lambda f: qab[:, f].rearrange("p g d -> p (g d)"), 128, 4, qaT, ['qaT'], ['qab'])
                for g in range(2):
                    op('act', lambda e, g=g: e.copy(out=qaZ[g * 64:(g + 1) * 64, g, :, :],
                                                    in_=qaT[g * 64:(g + 1) * 64, :, :]), reads=['qaT'], writes=['qaZ'])
            later.append(_qa_tr)
            op('act', lambda e: e.activation(out=sqb[:, 0:128], in_=kvf[:, 0:128], func=AF.Square),
               reads=['kvf'], writes=['x1'])
            op('dve', lambda e: e.tensor_reduce(out=mstat[:, 2:4], in_=sqb[:, 0:128].rearrange("p (g d) -> p g d", g=2),
                                                axis=AX.X, op=ALU.add), reads=['x1'], writes=['mstat'])
            def _negc_chain():
                pb, pk = P.bank()
                op('pe', lambda e: e.transpose(out=pb[0:4, 0:128], in_=mstat[:, 0:4], identity=C['identf'][:]),
                   reads=['mstat', 'C_identf'], writes=[pk])
                op('dve', lambda e: e.tensor_reduce(out=mT[:, 0:1], in_=pb[0:4, 0:128], axis=AX.X, op=ALU.max),
                   reads=[pk], writes=['mT'])
                op('dve', lambda e: e.tensor_tensor(out=runmax[:], in0=runmax[:], in1=mT[:, 0:1], op=ALU.max),
                   reads=['mT', 'runmax'], writes=['runmax'])
                op('dve', lambda e: e.tensor_scalar(out=dg4[:], in0=C['identf'][0:4, 0:4], scalar1=runmax[:, 0:1],
                                                    scalar2=None, op0=ALU.mult),
                   reads=['runmax', 'C_identf'], writes=['dg4'])
                pb, pk = P.bank()
                op('pe', lambda e: e.matmul(pb[:, 0:4], lhsT=C['onesf'][0:4, :], rhs=dg4[:], start=True, stop=True),
                   reads=['dg4', 'C_onesf'], writes=[pk])
                op('dve', lambda e: e.tensor_copy(out=sm[:, 16:20], in_=pb[:, 0:4]), reads=[pk], writes=['sm_bc'])
                op('dve', lambda e: e.tensor_tensor(out=sm[:, 20:22], in0=sm[:, 16:18], in1=sm[:, 18:20], op=ALU.mult),
                   reads=['sm_bc'], writes=['sm_c2'])
                op('act', lambda e: e.activation(out=sm[:, 22:24], in_=sm[:, 20:22], func=AF.Sqrt, scale=1.0 / 64.0),
                   reads=['sm_c2'], writes=['sm_c'])
                op('dve', lambda e: e.tensor_scalar(out=negc[:], in0=sm[:, 22:24], scalar1=-1.0, scalar2=None,
                                                    op0=ALU.mult), reads=['sm_c'], writes=['negc'])
            later2.append(_negc_chain)
            if cut <= 13:
                return
            pb, pk = proj_chunk(O_QI, 512)
            op('dve', lambda e: e.tensor_tensor(out=qib[:], in0=pb[:, 0:512].rearrange("p (h d) -> p h d", h=8),
                                                in1=wabs[:].unsqueeze(2).to_broadcast([128, 8, 64]), op=ALU.mult),
               reads=[pk, 'wabs'], writes=['qib'])
            later.append(lambda: transposes(lambda f: qib[:, 2 * f:2 * f + 2, :].rearrange("p a d -> p (a d)"), 128, 4,
                                            qiT, ['qiT'], ['qib']))
            pb, pk = proj_chunk(O_GA, 512)
            op('act', lambda e: e.activation(out=sga[:], in_=pb[:, 0:512], func=AF.Silu), reads=[pk], writes=['sga'])
            for fn_ in later:
                fn_()

            if cut <= 14:
                return
            nk = (ti + 1) * 128
            nch = (nk + 511) // 512
            for cc in range(nch):
                w = min(512, nk - cc * 512)
                for h in range(8):
                    lo = (h % 2) * 64
                    pb, pk = P.bank()
                    op('pe', lambda e, h=h, lo=lo, pb=pb: e.matmul(
                        pb[:, 0:w], lhsT=qiT[lo:lo + 64, h // 2, :], rhs=kiT[lo:lo + 64, cc * 512:cc * 512 + w],
                        start=True, stop=True), reads=['qiT', 'kiT'], writes=[pk])
                    if h % 2 == 0:
                        op('act', lambda e, h=h, pb=pb: e.activation(out=R[:, h, 0:w], in_=pb[:, 0:w], func=AF.Relu),
                           reads=[pk], writes=['R%d' % h])
                    else:
                        op('dve', lambda e, h=h, pb=pb: e.tensor_scalar(out=R[:, h, 0:w], in0=pb[:, 0:w], scalar1=0.0,
                                                                        scalar2=None, op0=ALU.max),
                           reads=[pk], writes=['R%d' % h])
                pb, pk = P.bank()
                for h in range(8):
                    op('pe', lambda e, h=h, pb=pb: e.matmul(pb[:, 0:w], lhsT=Dg[:, h, :], rhs=R[:, h, 0:w],
                                                            start=(h == 0), stop=(h == 7)),
                       reads=['Dg', 'R%d' % h], writes=[pk])
                op('dve', lambda e, pb=pb: e.tensor_reduce(out=sm[:, 2 + cc:3 + cc], in_=pb[:, 0:w], axis=AX.X,
                                                           op=ALU.max, apply_absolute_value=True),
                   reads=[pk], writes=['sm_mx%d' % cc])
                last = (cc == nch - 1)
                wc = w - 128 if last else w
                if wc > 0:
                    op('act', lambda e, pb=pb, wc=wc: e.copy(out=score[:, cc * 512:cc * 512 + wc], in_=pb[:, 0:wc]),
                       reads=[pk], writes=['score'])
                if last:
                    op('dve', lambda e, pb=pb: e.tensor_tensor(out=score[:, nk - 128:nk], in0=pb[:, w - 128:w],
                                                               in1=C['negtri'][:], op=ALU.add),
                       reads=[pk, 'C_negtri'], writes=['score'])
            for fn_ in later2:
                fn_()
            if cut <= 15:
                return
            yield 1
            op('dve', lambda e: e.tensor_reduce(out=sm[:, 6:7], in_=sm[:, 2:2 + nch], axis=AX.X, op=ALU.max),
               reads=['sm_mx%d' % c for c in range(nch)], writes=['sm_bd'])
            op('dve', lambda e: e.tensor_scalar(out=sm[:, 7:8], in0=sm[:, 6:7], scalar1=1.0, scalar2=-1.0,
                                                op0=ALU.add, op1=ALU.mult), reads=['sm_bd'], writes=['sm_lo'])
            op('dve', lambda e: e.tensor_scalar(out=sm[:, 56:57], in0=sm[:, 6:7], scalar1=1.0, scalar2=2.0,
                                                op0=ALU.add, op1=ALU.mult), reads=['sm_bd'], writes=['sm_w0'])
            op('dve', lambda e: e.tensor_scalar(out=W16[:], in0=C['pw2'][:], scalar1=sm[:, 56:57], scalar2=None,
                                                op0=ALU.mult), reads=['sm_w0', 'C_pw2'], writes=['W16'])
            op('dve', lambda e: e.tensor_tensor(out=sm[:, 57:58], in0=sm[:, 7:8], in1=W16[:, 0:1], op=ALU.add),
               reads=['sm_lo', 'W16'], writes=['sm_mid'])
            for it in range(NIT):
                op('dve', lambda e: e.tensor_scalar(out=junkb[:, 0:nk], in0=score[:, 0:nk], scalar1=sm[:, 57:58],
                                                    scalar2=None, op0=ALU.is_ge, op1=ALU.add,
                                                    accum_out=sm[:, 58:59]),
                   reads=['score', 'sm_mid'], writes=['junkb', 'sm_cnt'])
                op('dve', lambda e, it=it: e.tensor_scalar(out=sm[:, 59:60], in0=sm[:, 58:59], scalar1=TOPK - 0.5,
                                                           scalar2=W16[:, it:it + 1], op0=ALU.is_ge, op1=ALU.mult),
                   reads=['sm_cnt', 'W16'], writes=['sm_pw'])
                op('dve', lambda e, it=it: e.scalar_tensor_tensor(out=sm[:, 57:58], in0=sm[:, 59:60],
                                                                  scalar=W16[:, it + 1:it + 2], in1=sm[:, 57:58],
                                                                  op0=ALU.subtract, op1=ALU.add),
                   reads=['sm_pw', 'W16', 'sm_mid'], writes=['sm_mid'])
            op('dve', lambda e: e.tensor_tensor(out=sm[:, 7:8], in0=sm[:, 57:58], in1=W16[:, NIT:NIT + 1],
                                                op=ALU.subtract), reads=['sm_mid', 'W16'], writes=['sm_lo'])
            op('dve', lambda e: e.tensor_scalar(out=junkb[:, 0:nk], in0=score[:, 0:nk], scalar1=sm[:, 7:8],
                                                scalar2=None, op0=ALU.is_ge), reads=['score', 'sm_lo'], writes=['junkb'])
            yield 'bisected'
            for j0 in range(0, ti + 1, 8):
                nb = min(8, ti + 1 - j0)
                pbm, pkm = P.bank()
                pbmv = pbm[:].bitcast(BF16)
                for bb in range(nb):
                    op('pe', lambda e, bb=bb: e.transpose(out=pbmv[:, bb * 128:(bb + 1) * 128],
                                                          in_=junkb[:, (j0 + bb) * 128:(j0 + bb + 1) * 128],
                                                          identity=identb[:]), reads=['junkb', 'identb'], writes=[pkm])
                op('act', lambda e, nb=nb, j0=j0: e.activation(
                    out=maskT[:, j0:j0 + nb, :], in_=pbmv[:, 0:nb * 128].rearrange("p (b q) -> p b q", q=128),
                    func=AF.Identity, scale=MB, bias=-MB), reads=[pkm], writes=MK)
            if cut <= 16:
                return
            for g in range(2):
                qk = {}

                def emit_qk(j, g=g):
                    pb, pk = P.bank()
                    ty = ti - j
                    op('pe', lambda e: e.matmul(pb[:, 0:512], lhsT=kaT[:, j * 128:(j + 1) * 128], rhs=qaZ[:, g, :, :],
                                                start=True, stop=False), reads=['kaT', 'qaZ'], writes=[pk])
                    op('pe', lambda e: e.matmul(pb[:, 0:512], lhsT=identb[:],
                                                rhs=maskT[:, j, :].unsqueeze(1).to_broadcast([128, 4, 128]),
                                                start=False, stop=(ty > 1)), reads=['identb'] + MK, writes=[pk])
                    if ty <= 1:
                        op('pe', lambda e: e.matmul(pb[:, 0:512], lhsT=identb[:], rhs=BT[:, ty, g, :],
                                                    start=False, stop=True), reads=['identb', 'BT'], writes=[pk])
                    qk[j] = (pb, pk)
                emit_qk(0)
                for j in range(ti + 1):
                    if j + 1 <= ti:
                        emit_qk(j + 1)
                    pb, pk = qk.pop(j)
                    Pb = Pt[ptr[0] % 3]
                    Pk = 'Pt%d' % (ptr[0] % 3)
                    ptr[0] += 1
                    op('act', lambda e, pb=pb, Pb=Pb, g=g: e.activation(out=Pb[:], in_=pb[:, 0:512], func=AF.Exp,
                                                                        bias=negc[:, g:g + 1], scale=0.125),
                       reads=[pk, 'negc'], writes=[Pk])
                    for h4 in range(4):
                        op('pe', lambda e, Pb=Pb, j=j, h4=h4, g=g: e.matmul(
                            PO[g][:, h4 * 65:(h4 + 1) * 65], lhsT=Pb[:, h4 * 128:(h4 + 1) * 128],
                            rhs=Vaug[:, j, g, :], start=(j == 0 and h4 == 0), stop=(j == ti and h4 == 3),
                            skip_group_check=True), reads=[Pk, 'Vaug'], writes=[POK[g]])
                pov = PO[g][:, 0:260].rearrange("p (h d) -> p h d", h=4)
                op('dve', lambda e, pov=pov: e.reciprocal(out=sm[:, 60:64].unsqueeze(2), in_=pov[:, :, 64:65]),
                   reads=[POK[g]], writes=['sm_rec'])
                op('dve', lambda e, pov=pov, g=g: e.tensor_tensor(
                    out=osb[:, 4 * g:4 * g + 4, :], in0=pov[:, :, 0:64],
                    in1=sm[:, 60:64].unsqueeze(2).to_broadcast([128, 4, 64]), op=ALU.mult),
                   reads=[POK[g], 'sm_rec'], writes=['osb'])
            if cut <= 17:
                return
            op('pool', lambda e: e.tensor_tensor(out=ybb[:], in0=osb[:].rearrange("p h d -> p (h d)"), in1=sga[:],
                                                 op=ALU.mult), reads=['osb', 'sga'], writes=['ybb'])
            transposes(lambda f: ybb[:, f * 128:(f + 1) * 128], 128, 4, yT[:, 0:4, :], ['yTa'], ['ybb'])
            transposes(lambda f: YB[:, ti, f * 128:(f + 1) * 128], 128, 4, yT[:, 4:8, :], ['yTb'], ['YB'])
            if cut <= 18:
                return
            yield 2

        mhalf = sb('mhalf', [128, 1])
        op('pool', lambda e: e.memset(mhalf[:], -0.5), writes=['mhalf'])

        def merge2(x, xk, pin, pink, ydst):
            py = [P.bank(), P.bank()]
            for hf in range(2):
                for kc in range(8):
                    op('pe', lambda e, hf=hf, kc=kc: e.matmul(py[hf][0][:, 0:512], lhsT=yT[:, kc, :],
                                                              rhs=Wout[:, kc, hf * 512:(hf + 1) * 512],
                                                              start=(kc == 0), stop=(kc == 7)),
                       reads=['yTa', 'yTb', 'Wout'], writes=[py[hf][1]])
                op('act', lambda e, hf=hf: e.activation(out=sig[:, hf * 512:(hf + 1) * 512], in_=py[hf][0][:, 0:512],
                                                        func=AF.Square, accum_out=sm[:, 66 + hf:67 + hf]),
                   reads=[py[hf][1]], writes=['sig', 'sm_y%d' % hf])
            op('pool', lambda e: e.tensor_tensor(out=sm[:, 66:67], in0=sm[:, 66:67], in1=sm[:, 67:68], op=ALU.add),
               reads=['sm_y0', 'sm_y1'], writes=['sm_y0'])
            op('pool', lambda e: e.tensor_scalar(out=sm[:, 65:66], in0=sm[:, 66:67], scalar1=1.0 / D, scalar2=EPS,
                                                 op0=ALU.mult, op1=ALU.add), reads=['sm_y0'], writes=['sm_t65'])
            op('pool', lambda e: e.tensor_tensor(out=sm[:, 67:68], in0=sm[:, 65:66], in1=mhalf[:], op=ALU.pow),
               reads=['sm_t65', 'mhalf'], writes=['sm_y1'])
            for hf in range(2):
                op('act', lambda e, hf=hf: e.activation(out=x1[:, hf * 512:(hf + 1) * 512], in_=py[hf][0][:, 0:512],
                                                        func=AF.Copy, scale=sm[:, 67:68]),
                   reads=[py[hf][1], 'sm_y1'], writes=['x1'])
            op('pool', lambda e: e.tensor_tensor(out=x1[:], in0=x1[:], in1=gpost[:], op=ALU.mult),
               reads=['x1', 'gpost'], writes=['x1'])
            op('pool', lambda e: e.tensor_tensor(out=x1[:], in0=x1[:], in1=x[:], op=ALU.add),
               reads=['x1', xk], writes=['x1'])
            op('act', lambda e: e.copy(out=hb[:], in_=x1[:]), reads=['x1'], writes=['hb'])
            transposes(lambda kc: hb[:, kc * 128:(kc + 1) * 128], 128, 8, hT, ['hT'], ['hb'])
            op('act', lambda e: e.copy(out=pbb[:], in_=pin[:]), reads=[pink], writes=['pbb'])
            transposes(lambda kc: pbb[:, kc * 128:(kc + 1) * 128], 128, 2, pT, ['pT'], ['pbb'])
            for hf in range(2):
                pg, pgk = P.bank()
                for kc in range(8):
                    op('pe', lambda e, hf=hf, kc=kc, pg=pg: e.matmul(pg[:, 0:512], lhsT=hT[:, kc, :],
                                                                     rhs=Wgate[:, kc, hf * 512:(hf + 1) * 512],
                                                                     start=(kc == 0), stop=(kc == 7)),
                       reads=['hT', 'Wgate'], writes=[pgk])
                op('act', lambda e, hf=hf, pg=pg: e.activation(out=sig[:, hf * 512:(hf + 1) * 512], in_=pg[:, 0:512],
                                                               func=AF.Sigmoid), reads=[pgk], writes=['sig'])
                pu, puk = P.bank()
                for kc in range(2):
                    op('pe', lambda e, hf=hf, kc=kc, pu=pu: e.matmul(pu[:, 0:512], lhsT=pT[:, kc, :],
                                                                     rhs=Wup[:, kc, hf * 512:(hf + 1) * 512],
                                                                     start=(kc == 0), stop=(kc == 1)),
                       reads=['pT', 'Wup'], writes=[puk])
                op('act', lambda e, hf=hf, pu=pu: e.copy(out=tmpf[hf][:], in_=pu[:, 0:512]), reads=[puk],
                   writes=['tmpf%d' % hf])
                op('pool', lambda e, hf=hf: e.tensor_tensor(out=sig[:, hf * 512:(hf + 1) * 512],
                                                            in0=sig[:, hf * 512:(hf + 1) * 512], in1=tmpf[hf][:],
                                                            op=ALU.mult), reads=['sig', 'tmpf%d' % hf], writes=['sig'])
            op('pool', lambda e: e.tensor_tensor(out=sig[:], in0=sig[:], in1=x1[:], op=ALU.add),
               reads=['sig', 'x1'], writes=['sig'])
            P.dma('sp', ydst, sig[:], reads=['sig'])

        def merge(x, xk, pin, pink, ydst, n=128):
            py = [P.bank(), P.bank()]
            for hf in range(2):
                for kc in range(8):
                    op('pe', lambda e, hf=hf, kc=kc: e.matmul(py[hf][0][0:n, 0:512], lhsT=yT[:, kc, 0:n],
                                                              rhs=Wout[:, kc, hf * 512:(hf + 1) * 512],
                                                              start=(kc == 0), stop=(kc == 7)),
                       reads=['yTa', 'yTb', 'Wout'], writes=[py[hf][1]])
                op('act', lambda e, hf=hf: e.activation(out=junkb[0:n, hf * 512:(hf + 1) * 512],
                                                        in_=py[hf][0][0:n, 0:512], func=AF.Square,
                                                        accum_out=sm[0:n, 66 + hf:67 + hf]),
                   reads=[py[hf][1]], writes=['junkb', 'sm_y%d' % hf])
            op('dve', lambda e: e.tensor_tensor(out=sm[0:n, 66:67], in0=sm[0:n, 66:67], in1=sm[0:n, 67:68], op=ALU.add),
               reads=['sm_y0', 'sm_y1'], writes=['sm_y0'])
            rstd_of(sm[0:n, 66:67], sm[0:n, 67:68], ['sm_y0'], 'sm_y1', 65, n)
            for hf in range(2):
                op('dve', lambda e, hf=hf: e.scalar_tensor_tensor(
                    out=x1[0:n, hf * 512:(hf + 1) * 512], in0=py[hf][0][0:n, 0:512], scalar=sm[0:n, 67:68],
                    in1=gpost[0:n, hf * 512:(hf + 1) * 512], op0=ALU.mult, op1=ALU.mult),
                   reads=[py[hf][1], 'sm_y1', 'gpost'], writes=['x1'])
            op('pool', lambda e: e.tensor_tensor(out=x1[0:n], in0=x1[0:n], in1=x[0:n], op=ALU.add),
               reads=['x1', xk], writes=['x1'])
            op('act', lambda e: e.copy(out=hb[0:n], in_=x1[0:n]), reads=['x1'], writes=['hb'])
            transposes(lambda kc: hb[0:n, kc * 128:(kc + 1) * 128], n, 8, hT, ['hT'], ['hb'])
            op('act', lambda e: e.copy(out=pbb[0:n], in_=pin[0:n]), reads=[pink], writes=['pbb'])
            transposes(lambda kc: pbb[0:n, kc * 128:(kc + 1) * 128], n, 2, pT, ['pT'], ['pbb'])
            for hf in range(2):
                pg, pgk = P.bank()
                for kc in range(8):
                    op('pe', lambda e, hf=hf, kc=kc, pg=pg: e.matmul(pg[0:n, 0:512], lhsT=hT[:, kc, 0:n],
                                                                     rhs=Wgate[:, kc, hf * 512:(hf + 1) * 512],
                                                                     start=(kc == 0), stop=(kc == 7)),
                       reads=['hT', 'Wgate'], writes=[pgk])
                op('act', lambda e, hf=hf, pg=pg: e.activation(out=sig[0:n, hf * 512:(hf + 1) * 512],
                                                               in_=pg[0:n, 0:512], func=AF.Sigmoid),
                   reads=[pgk], writes=['sig'])
                pu, puk = P.bank()
                for kc in range(2):
                    op('pe', lambda e, hf=hf, kc=kc, pu=pu: e.matmul(pu[0:n, 0:512], lhsT=pT[:, kc, 0:n],
                                                                     rhs=Wup[:, kc, hf * 512:(hf + 1) * 512],
                                                                     start=(kc == 0), stop=(kc == 1)),
                       reads=['pT', 'Wup'], writes=[puk])
                op('dve', lambda e, hf=hf, pu=pu: e.tensor_tensor(out=sig[0:n, hf * 512:(hf + 1) * 512],
                                                                  in0=pu[0:n, 0:512],
                                                                  in1=sig[0:n, hf * 512:(hf + 1) * 512],
                                                                  op=ALU.mult), reads=[puk, 'sig'], writes=['sig'])
            op('pool', lambda e: e.tensor_tensor(out=sig[0:n], in0=sig[0:n], in1=x1[0:n], op=ALU.add),
               reads=['sig', 'x1'], writes=['sig'])
            P.dma('sp', ydst, sig[0:n], reads=['sig'])


        if with_sample:
            def small(name, shape, dt=F32, src=None):
                t_ = sb(name, shape, dt)
                if src is not None:
                    P.dma('sp', t_[:], cdr[src].ap(), writes=[name])
                return t_
            coss = small('coss', [16, 32], src='coss')
            sins = small('sins', [16, 32], src='sins')
            dins = small('dins', [16, 8], src='dins')
            douts = small('douts', [16, 8], src='douts')
            dts = small('dts', [16, 128], src='dts')
            bmaskf = small('bmaskf', [128, 64], src='bmaskf')
            bmask = small('bmask', [16, 4], src='bmask')
            negnew = small('negnew', [16, 16], src='negnew')
            negp = small('negp', [128, 1], src='negp')
            rc4 = small('rc4', [128, 4], src='rc4')
            iotap = small('iotap', [128, 1], src='iotap')
            YBs = small('YBs', [16, 512], BF16)
            kiTs = small('kiTs', [128, 16], BF16)
            kaTs = small('kaTs', [128, 16], BF16)
            vnew = small('vnew', [128, 2, 64], BF16)
            QI = small('QI', [128, 8, 16], BF16)
            rc8 = small('rc8', [128, 8], src='rc8')
            QAz = small('QAz', [128, 4, 32], BF16)
            idxP = small('idxP', [128, 4], I32)
            ptl = small('ptl', [128, 4], I32)
            ptf = small('ptf', [128, 8])
            idxF = small('idxF', [128, 52])
            idxC = [[small('idxC%d_%d' % (b_, rc), [128, 1], I32) for rc in range(4)] for b_ in range(4)]
            idxK = [[small('idxK%d_%d' % (b_, c_), [128, 1], I32) for c_ in range(8)] for b_ in range(4)]
            idxT = [small('idxT%d' % b_, [128, 1], I32) for b_ in range(4)]
            scoreX = small('scoreX', [128, 4, 2, 4])
            maskX = small('maskX', [128, 4, 2, 4], BF16)
            TCt = small('TCt', [128, 128], BF16)
            Vt = small('Vt', [128, 2, 64], BF16)
            P2 = small('P2', [128, 2, 32], BF16)
            Bn = small('Bn', [16, 8, 4])
            bq = small('bq', [128, 96])
            m1 = small('m1', [128, 64])
            dg32 = small('dg32', [32, 32])
            mT32 = small('mT32', [32, 4])
            den2 = small('den2', [16, 4])
            op('pool', lambda e: e.memset(vnew[:], 0.0), writes=['vnew'])

        def sample_ret():
            n = 16
            x = xt[0]
            xk = 'xt0'
            P.dma('sp', x[0:16, :], x_s.ap(), writes=[xk])
            S0 = sig[:].rearrange("p (b c) -> p b c", b=4)
            for b_ in range(4):
                sv = st_in.ap()[b_].rearrange("(hh hp) k v -> hp k hh v", hp=2)
                for hp in range(2):
                    P.dma('sp', S0[hp * 64:(hp + 1) * 64, b_, :].rearrange("p (hh v) -> p hh v", hh=4), sv[hp],
                          writes=['sig'])
            dcs = tmpf[1][:, 0:256]
            P.dma('sp', dcs, cdr['dcs'].ap(), writes=['tmpf1'])
            S0bf = x1[:, 512:1024].bitcast(BF16).rearrange("p (b c) -> p b c", b=4)
            op('act', lambda e: e.copy(out=S0bf, in_=S0), reads=['sig'], writes=['x1'])
            prenorm(x, xk, n=16)
            if cut <= 21:
                return
            pb, pk = proj_chunk(O_GB - O_QB, 512, n=16)
            op('act', lambda e: e.activation(out=sgb[0:16], in_=pb[0:16, 0:512], func=AF.Silu), reads=[pk],
               writes=['sgb'])
            t1 = tmpf[0][0:16].rearrange("p (h a d) -> p h a d", h=8, a=2)
            t2 = x1[0:16, 0:512].rearrange("p (h a d) -> p h a d", h=8, a=2)
            cosb = coss[:].unsqueeze(1).unsqueeze(1).to_broadcast([16, 8, 2, 32])
            sinb = sins[:].unsqueeze(1).unsqueeze(1).to_broadcast([16, 8, 2, 32])
            for which in range(2):
                pb, pk = proj_chunk((O_QB if which == 0 else O_KB) - O_QB, 512, n=16)
                pv = pb[0:16, 0:512].rearrange("p (h a d) -> p h a d", h=8, a=2)
                op('dve', lambda e: e.tensor_tensor(out=t1, in0=pv, in1=cosb, op=ALU.mult),
                   reads=[pk, 'coss'], writes=['tmpf0'])
                op('dve', lambda e: e.tensor_tensor(out=t2, in0=pv, in1=sinb, op=ALU.mult),
                   reads=[pk, 'sins'], writes=['x1'])
                op('pool', lambda e: e.tensor_tensor(out=t1[:, :, 0, :], in0=t1[:, :, 0, :], in1=t2[:, :, 1, :],
                                                     op=ALU.subtract), reads=['tmpf0', 'x1'], writes=['tmpf0'])
                op('pool', lambda e: e.tensor_tensor(out=t1[:, :, 1, :], in0=t1[:, :, 1, :], in1=t2[:, :, 0, :],
                                                     op=ALU.add), reads=['tmpf0', 'x1'], writes=['tmpf0'])
                t1v = tmpf[0][0:16].rearrange("p (h d) -> p h d", h=8)
                if which == 0:
                    op('pool', lambda e: e.tensor_tensor(out=qbd[0:16], in0=t1v,
                                                         in1=dins[:].unsqueeze(2).to_broadcast([16, 8, 64]),
                                                         op=ALU.mult), reads=['tmpf0', 'dins'], writes=['qbd'])
                else:
                    op('pool', lambda e: e.tensor_scalar(out=kbr[0:16], in0=t1v, scalar1=0.125, scalar2=None,
                                                         op0=ALU.mult), reads=['tmpf0'], writes=['kbr'])
                    op('pool', lambda e: e.tensor_tensor(out=kbd[0:16], in0=t1v,
                                                         in1=douts[:].unsqueeze(2).to_broadcast([16, 8, 64]),
                                                         op=ALU.mult), reads=['tmpf0', 'douts'], writes=['kbd'])
            if cut <= 22:
                return
            pbq, pkq = P.bank()
            pbqv = pbq[:].bitcast(BF16)
            for f in range(4):
                op('pe', lambda e, f=f: e.transpose(out=pbqv[:, f * 128:f * 128 + 16],
                                                    in_=qbd[0:16, 2 * f:2 * f + 2, :].rearrange("p a d -> p (a d)"),
                                                    identity=identb[0:16, 0:16]), reads=['qbd', 'identb'], writes=[pkq])
            qzv = qZ[:].rearrange("p (f a) c -> p a f c", a=2)
            pqv = pbqv[:, 0:512].rearrange("p (f c) -> p f c", f=4)
            op('act', lambda e: e.copy(out=qzv[0:64, 0, :, 0:16], in_=pqv[0:64, :, 0:16]), reads=[pkq], writes=['qZ'])
            op('act', lambda e: e.copy(out=qzv[64:128, 1, :, 0:16], in_=pqv[64:128, :, 0:16]), reads=[pkq],
               writes=['qZ'])
            transposes(lambda f: kbr[0:16, 2 * f:2 * f + 2, :].rearrange("p a d -> p (a d)"), 16, 4, kbT, ['kbT'],
                       ['kbr'])
            pb, pk = proj_chunk(O_VB - O_QB, 512, n=16)
            op('act', lambda e: e.copy(out=vbb[0:16], in_=pb[0:16, 0:512]), reads=[pk], writes=['vbb'])
            if cut <= 23:
                return
            pa, pak = P.bank()
            for h in range(8):
                op('pe', lambda e, h=h: e.matmul(pa[0:16, h * 16:(h + 1) * 16], lhsT=kbT[:, h // 2, 0:16],
                                                 rhs=qZ[:, h, 0:16], start=True, stop=True),
                   reads=['kbT', 'qZ'], writes=[pak])
            op('pool', lambda e: e.memset(AT[:, :, 0:16], 0.0), writes=['AT'])
            op('dve', lambda e: e.tensor_tensor(out=AT[0:16, :, 0:16],
                                                in0=pa[0:16, 0:128].rearrange("p (h c) -> p h c", h=8),
                                                in1=dts[:].rearrange("p (h c) -> p h c", h=8), op=ALU.mult),
               reads=[pak, 'dts'], writes=['AT'])
            if cut <= 24:
                return
            qZb = Pt[0][:].rearrange("p (b h c) -> p b h c", b=4, h=8)
            bmf = bmaskf[:].rearrange("p (b c) -> p b c", b=4)
            for b_ in range(4):
                op('pool', lambda e, b_=b_: e.tensor_tensor(out=qZb[:, b_], in0=qZ[:, :, 0:16],
                                                            in1=bmf[:, b_, :].unsqueeze(1).to_broadcast([128, 8, 16]),
                                                            op=ALU.mult), reads=['qZ', 'bmaskf'], writes=['Pt0'])
            po, pok = P.bank()
            for h in range(8):
                op('pe', lambda e, h=h: e.matmul(po[0:16, h * 64:(h + 1) * 64], lhsT=AT[:, h, 0:16],
                                                 rhs=vbb[:, h * 64:(h + 1) * 64], start=True, stop=False),
                   reads=['AT', 'vbb'], writes=[pok])
                for b_ in range(4):
                    op('pe', lambda e, h=h, b_=b_: e.matmul(
                        po[0:16, h * 64:(h + 1) * 64], lhsT=qZb[:, b_, h, :],
                        rhs=S0bf[:, b_, (h // 2) * 64:(h // 2) * 64 + 64], start=False, stop=(b_ == 3)),
                       reads=['Pt0', 'x1'], writes=[pok])
            if cut <= 25:
                return
            kbdm = junkb[:, :].rearrange("p (b c) -> p b c", b=4)
            op('pool', lambda e: e.memset(junkb[:], 0.0), writes=['junkb'])
            for b_ in range(4):
                op('dve', lambda e, b_=b_: e.tensor_scalar(out=kbdm[0:16, b_, :],
                                                           in0=kbd[0:16].rearrange("p h d -> p (h d)"),
                                                           scalar1=bmask[:, b_:b_ + 1], scalar2=None, op0=ALU.mult),
                   reads=['kbd', 'bmask'], writes=['junkb'])
            pst2 = [P.bank(), P.bank()]
            for b_ in range(4):
                bank, bk = pst2[b_ // 2]
                for h in range(8):
                    lo = (h % 2) * 64
                    c0 = (b_ % 2) * 256 + (h // 2) * 64
                    op('pe', lambda e, h=h, b_=b_, lo=lo, c0=c0, bank=bank: e.matmul(
                        bank[lo:lo + 64, c0:c0 + 64], lhsT=kbdm[:, b_, h * 64:(h + 1) * 64],
                        rhs=vbb[:, h * 64:(h + 1) * 64], start=True, stop=True),
                       reads=['junkb', 'vbb'], writes=[bk])
            if cut <= 26:
                return
            op('pool', lambda e: e.tensor_tensor(out=S0, in0=S0, in1=dcs.unsqueeze(1).to_broadcast([128, 4, 256]),
                                                 op=ALU.mult), reads=['sig', 'tmpf1'], writes=['sig'])
            for i2 in range(2):
                op('dve', lambda e, i2=i2: e.tensor_tensor(
                    out=S0[:, 2 * i2:2 * i2 + 2, :], in0=pst2[i2][0][:, 0:512].rearrange("p (b c) -> p b c", b=2),
                    in1=S0[:, 2 * i2:2 * i2 + 2, :], op=ALU.add), reads=[pst2[i2][1], 'sig'], writes=['sig'])
            for b_ in range(4):
                sv = st_s.ap()[b_].rearrange("(hh hp) k v -> hp k hh v", hp=2)
                for hp in range(2):
                    P.dma('sp', sv[hp], S0[hp * 64:(hp + 1) * 64, b_, :].rearrange("p (hh v) -> p hh v", hh=4),
                          reads=['sig'])
            if cut <= 27:
                return
            headnorm_gate(po, pok, YBs[0:16, :], ['YBs'], n=16)

        def sample_dsa():
            n = 16
            x = xt[0]
            xk = 'xt0'
            P.dma('sp', x[0:16, :], x_s.ap(), writes=[xk])
            P.dma('sp', pt[0][0:16, :], p_s.ap(), writes=['pt0'])
            prenorm(x, xk, n=16)
            pb, pk = proj_chunk(O_KI, 72, n=16)
            op('act', lambda e: e.copy(out=kvf[0:16, 256:320], in_=pb[0:16, 0:64]), reads=[pk], writes=['kvf'])
            for a in range(2):
                op('dve', lambda e, a=a: e.tensor_copy(out=kib[0:16, a, :], in_=pb[0:16, 0:64]), reads=[pk],
                   writes=['kib'])
            op('act', lambda e: e.activation(out=wabs[0:16], in_=pb[0:16, 64:72], func=AF.Abs), reads=[pk],
               writes=['wabs'])
            op('act', lambda e: e.activation(out=sgn[0:16], in_=pb[0:16, 64:72], func=AF.Sign), reads=[pk],
               writes=['sgn'])
            transposes(lambda bb: kib[0:16].rearrange("p a d -> p (a d)"), 16, 1, kiTs[:, 0:16].unsqueeze(1),
                       ['kiTs'], ['kib'])
            pb, pk = proj_chunk(O_KA, 256, n=16)
            op('act', lambda e: e.copy(out=kvf[0:16, 0:256], in_=pb[0:16, 0:256]), reads=[pk], writes=['kvf'])
            op('dve', lambda e: e.tensor_copy(out=kab[0:16], in_=pb[0:16, 0:128]), reads=[pk], writes=['kab'])
            op('dve', lambda e: e.tensor_copy(out=vnew[0:16], in_=pb[0:16, 128:256].rearrange("p (g d) -> p g d", g=2)),
               reads=[pk], writes=['vnew'])
            P.dma('sp', k_s.ap(), kvf[0:16, 0:128], reads=['kvf'])
            P.dma('sp', v_s.ap(), kvf[0:16, 128:256], reads=['kvf'])
            P.dma('sp', ki_s.ap(), kvf[0:16, 256:320], reads=['kvf'])
            transposes(lambda bb: kab[0:16], 16, 1, kaTs[:, 0:16].unsqueeze(1), ['kaTs'], ['kab'])
            pb, pk = proj_chunk(O_QA, 512, n=16)
            op('act', lambda e: e.copy(out=qab[0:16], in_=pb[0:16, 0:512].rearrange("p (g f d) -> p f g d", g=2, f=4)),
               reads=[pk], writes=['qab'])
            transposes(lambda f: qab[0:16, f].rearrange("p g d -> p (g d)"), 16, 4, qaT, ['qaT'], ['qab'])
            pb, pk = proj_chunk(O_QI, 512, n=16)
            op('dve', lambda e: e.tensor_tensor(out=qib[0:16], in0=pb[0:16, 0:512].rearrange("p (h d) -> p h d", h=8),
                                                in1=wabs[0:16].unsqueeze(2).to_broadcast([16, 8, 64]), op=ALU.mult),
               reads=[pk, 'wabs'], writes=['qib'])
            qib2 = tmpf[0][0:16, :].bitcast(BF16).rearrange("p (h a d) -> p h a d", h=8, a=2)
            for a in range(2):
                op('dve', lambda e, a=a: e.tensor_copy(out=qib2[:, :, a, :], in_=qib[0:16]), reads=['qib'],
                   writes=['tmpf0'])
            transposes(lambda h: qib2[:, h].rearrange("p a d -> p (a d)"), 16, 8, QI, ['QI'], ['tmpf0'])
            pb, pk = proj_chunk(O_GA, 512, n=16)
            op('act', lambda e: e.activation(out=sga[0:16], in_=pb[0:16, 0:512], func=AF.Silu), reads=[pk],
               writes=['sga'])
            if cut <= 31:
                return
            P.nbank = 5
            P.brr = 0
            for nk_, ok_ in (('WinA', 'Win'), ('WinB', 'Win'), ('YBa', 'YB'), ('YBb', 'YB')):
                P.alias(nk_, ok_)
            G32 = Win[:, 0:4, :].rearrange("p a n -> p (a n)").bitcast(F32)
            LG = Win[:, 4:8, :].rearrange("p a n -> p (a n)").bitcast(F32).rearrange("p (r c) -> p r c", c=32)
            TCv = YB[:, 0:8, :].rearrange("p a n -> p (a n)").rearrange("p (r c) -> p r c", c=128)
            Vc = YB[:, 8:16, :].rearrange("p a n -> p (a n)").rearrange("p (r g d) -> p r g d", g=2, d=64)
            scoreM = score[:].rearrange("p (b r q) -> p b r q", b=4, q=4)
            maskM = junkb[:].rearrange("p (b r q) -> p b r q", b=4, q=4)
            Pm = R[:].rearrange("p a n -> p (a n)").rearrange("p (r c) -> p r c", c=32)
            RK = ['R%d' % h for h in range(8)]
            GT = x1[:, 0:128]
            oSn = x1[:, 128:256].rearrange("p (g d) -> p g d", g=2)
            LG2 = x1[:, 256:320].rearrange("p (c k) -> p c k", c=2)
            Bl = x1[:, 320:352]
            sgnB = x1[:, 384:512]
            selt = tmpf[1][:, 0:256]
            WW = osb[:].rearrange("p h d -> p (h d)")[:, 0:NIT * 16].rearrange("p (i c) -> p i c", c=16)
            OA = P.banks[5]
            OAK = 'ps5'
            POs = [P.banks[6], P.banks[7]]
            P.dma('sp', selt, cdr['sel'].ap(), writes=['tmpf1'])
            X = tmpf[0][0:16, 0:128].rearrange("p (t h) -> p t h", t=16)
            op('dve', lambda e: e.tensor_tensor(out=X, in0=C['identf'][0:16, 0:16].unsqueeze(2).to_broadcast([16, 16, 8]),
                                                in1=sgn[0:16].unsqueeze(1).to_broadcast([16, 16, 8]), op=ALU.mult),
               reads=['C_identf', 'sgn'], writes=['tmpf0'])
            pb, pk = P.bank()
            op('pe', lambda e: e.matmul(pb[:, 0:128], lhsT=C['onesf'][0:16, :], rhs=tmpf[0][0:16, 0:128],
                                        start=True, stop=True), reads=['tmpf0', 'C_onesf'], writes=[pk])
            op('act', lambda e: e.copy(out=sgnB, in_=pb[:, 0:128]), reads=[pk], writes=['x1'])
            sgnBv = sgnB.rearrange("p (b q h) -> p b h q", b=4, q=4)
            blrev = tmpf[0][:, 128:160].rearrange("p (h q) -> p h q", h=8)
            P.dma('sp', blrev, bass.AP(scr, 129, [[1, 128], [384, 8], [1, 4]]), reads=['scr'], writes=['tmpf0'])
            pb, pk = P.bank()
            op('pe', lambda e: e.matmul(pb[:, 0:32], lhsT=C['jrev'][:], rhs=tmpf[0][:, 128:160], start=True, stop=True),
               reads=['C_jrev', 'tmpf0'], writes=[pk])
            op('act', lambda e: e.copy(out=Bl, in_=pb[:, 0:32]), reads=[pk], writes=['x1'])
            for t_ in range(4):
                for b2 in range(4):
                    j_ = 4 * b2 + t_
                    P.dma('sp', Bn[j_:j_ + 1], bass.AP(scr, 128 - t_, [[0, 1], [384, 8], [1, 4]]), reads=['scr'],
                          writes=['Bn'])
            if cut <= 32:
                return
            for b_ in range(4):
                P.dma('sp', idxP[:, b_:b_ + 1], bass.AP(pt_s, b_ * 128, [[1, 128], [1, 1]]), writes=['idxP'])
                P.dma('sp', ptl[:, b_:b_ + 1], bass.AP(pt_s, b_ * 128 + 127, [[0, 128], [1, 1]]), writes=['ptl'])
            op('dve', lambda e: e.tensor_copy(out=ptf[:, 0:4], in_=idxP[:]), reads=['idxP'], writes=['ptf'])
            op('dve', lambda e: e.tensor_copy(out=ptf[:, 4:8], in_=ptl[:]), reads=['ptl'], writes=['ptf'])
            for b_ in range(4):
                op('dve', lambda e, b_=b_: e.scalar_tensor_tensor(out=idxF[:, 4 * b_:4 * b_ + 4],
                                                                  in0=ptf[:, b_:b_ + 1].to_broadcast([128, 4]),
                                                                  scalar=4.0, in1=rc4[:], op0=ALU.mult, op1=ALU.add),
                   reads=['ptf', 'rc4'], writes=['idxF'])
                op('dve', lambda e, b_=b_: e.scalar_tensor_tensor(out=idxF[:, 16 + 8 * b_:24 + 8 * b_],
                                                                  in0=ptf[:, b_:b_ + 1].to_broadcast([128, 8]),
                                                                  scalar=8.0, in1=rc8[:], op0=ALU.mult, op1=ALU.add),
                   reads=['ptf', 'rc8'], writes=['idxF'])
            op('dve', lambda e: e.tensor_scalar(out=idxF[:, 48:52], in0=ptf[:, 4:8], scalar1=128.0,
                                                scalar2=iotap[:, 0:1], op0=ALU.mult, op1=ALU.add),
               reads=['ptf', 'iotap'], writes=['idxF'])
            for b_ in range(4):
                for rc in range(4):
                    op('dve', lambda e, b_=b_, rc=rc: e.tensor_copy(out=idxC[b_][rc][:],
                                                                    in_=idxF[:, 4 * b_ + rc:4 * b_ + rc + 1]),
                       reads=['idxF'], writes=['idx'])
                for c_ in range(8):
                    op('dve', lambda e, b_=b_, c_=c_: e.tensor_copy(
                        out=idxK[b_][c_][:], in_=idxF[:, 16 + 8 * b_ + c_:17 + 8 * b_ + c_]),
                       reads=['idxF'], writes=['idx'])
                op('dve', lambda e, b_=b_: e.tensor_copy(out=idxT[b_][:], in_=idxF[:, 48 + b_:49 + b_]),
                   reads=['idxF'], writes=['idx'])
            if cut <= 33:
                return
            cki4 = c_ki.ap().rearrange("n (c w) -> (n c) w", w=2048)
            ckirow = c_ki.ap().rearrange("n (r d) -> (n r) d", d=64)
            ck8 = c_k.ap().rearrange("n (c w) -> (n c) w", w=2048)
            ckrow = c_k.ap().rearrange("n (r d) -> (n r) d", d=128)
            cv8 = c_v.ap().rearrange("n (c w) -> (n c) w", w=2048)
            cvrow = c_v.ap().rearrange("n (r d) -> (n r) d", d=128)
            Gs = [Win[:, s_, :] for s_ in range(4)]
            GK = ['G%d' % s_ for s_ in range(4)]
            for k_ in GK:
                P.alias(k_, 'Win')
            gctr = [0]

            def gather(src, idx_tile, width=2048):
                s_ = gctr[0] % 4
                gctr[0] += 1
                P.dma('pool', Gs[s_][:, 0:width], src, reads=['idx'], writes=[GK[s_]], gather_idx=idx_tile[:, :])
                return Gs[s_], GK[s_]
            TCh = [TCv[:, 0:16, :], TCv[:, 16:32, :]]
            TCK = ['YBa0', 'YBa1']
            for k_ in TCK:
                P.alias(k_, 'YB')
            GTb = x1[:, 0:64].bitcast(BF16)

            def score_from(psv, pkey, npart, ni, dst, b_, scr_ap, scr_key):
                Rs = scr_ap[0:npart, 0:ni * 32].rearrange("p (i h q) -> p i h q", h=8, q=4)
                op('act', lambda e: e.activation(out=Rs, in_=psv, func=AF.Relu), reads=[pkey], writes=[scr_key])
                op('dve', lambda e: e.tensor_tensor(
                    out=Rs, in0=Rs, in1=sgnBv[0:npart, b_].unsqueeze(1).to_broadcast([npart, ni, 8, 4]), op=ALU.mult),
                   reads=[scr_key, 'x1'], writes=[scr_key])
                op('dve', lambda e: e.tensor_reduce(out=dst, in_=Rs.rearrange("p i h q -> p i q h"), axis=AX.X,
                                                    op=ALU.add), reads=[scr_key], writes=['score', 'scoreX'])
            scrE = tmpf[0][:, :]
            scrO = sig[:, 0:512]

            op('pool', lambda e: e.memset(scoreX[:], NEG), writes=['score', 'scoreX'])
            op('pool', lambda e: e.memset(m1[:, 0:16], 0.0), writes=['m1'])
            ev = [0]

            def evac(out_ap, in_ap, rk, wk):
                eng = 'act' if ev[0] % 2 == 0 else 'dve'
                ev[0] += 1
                if eng == 'act':
                    op('act', lambda e: e.copy(out=out_ap, in_=in_ap), reads=rk, writes=wk)
                else:
                    op('dve', lambda e: e.tensor_copy(out=out_ap, in_=in_ap), reads=rk, writes=wk)

            def transpose_chunk(G, gk, npairs, dstT, dstk, kparts=128):
                for h0 in range(0, npairs, 8):
                    nb = min(8, npairs - h0)
                    pbk_, pkk = P.bank()
                    pv_ = pbk_[:].bitcast(BF16)
                    for i in range(nb):
                        op('pe', lambda e, i=i: e.transpose(out=pv_[:, i * 128:(i + 1) * 128],
                                                            in_=G[:, (h0 + i) * 128:(h0 + i + 1) * 128],
                                                            identity=identb[:]), reads=[gk, 'identb'], writes=[pkk])
                    evac(dstT[:, h0:h0 + nb, :], pv_[:, 0:nb * 128].rearrange("p (i c) -> p i c", i=nb), [pkk], [dstk])

            tcc = [0]
            for b_ in range(4):
                qrhs = [QI[0:64, :, 4 * b_:4 * b_ + 4], QI[64:128, :, 4 * b_:4 * b_ + 4]]
                for rc in range(4):
                    G, gk = gather(cki4, idxC[b_][rc])
                    tci = tcc[0] % 2
                    tcc[0] += 1
                    transpose_chunk(G, gk, 16, TCh[tci], TCK[tci])
                    pbe, pke = P.bank()
                    pbo, pko = P.bank()
                    for pr in range(16):
                        op('pe', lambda e, pr=pr: e.matmul(pbe[:, pr * 32:(pr + 1) * 32], lhsT=TCh[tci][0:64, pr, :],
                                                           rhs=qrhs[0], start=True, stop=True),
                           reads=[TCK[tci], 'QI'], writes=[pke])
                        op('pe', lambda e, pr=pr: e.matmul(pbo[:, pr * 32:(pr + 1) * 32], lhsT=TCh[tci][64:128, pr, :],
                                                           rhs=qrhs[1], start=True, stop=True),
                           reads=[TCK[tci], 'QI'], writes=[pko])
                    rows = scoreM[:, b_, rc * 32:(rc + 1) * 32, :].rearrange("p (i two) q -> p i two q", two=2)
                    score_from(pbe[:, 0:512].rearrange("p (i h q) -> p i h q", h=8, q=4), pke, 128, 16,
                               rows[:, :, 0, :], b_, scrE, 'tmpf0')
                    score_from(pbo[:, 0:512].rearrange("p (i h q) -> p i h q", h=8, q=4), pko, 128, 16,
                               rows[:, :, 1, :], b_, scrO, 'sig')
                op('dve', lambda e, b_=b_: e.tensor_reduce(out=m1[:, b_:b_ + 1], in_=scoreM[:, b_], axis=AX.XY,
                                                           op=ALU.max, apply_absolute_value=True),
                   reads=['score'], writes=['m1'])
                op('dve', lambda e, b_=b_: e.tensor_scalar(out=scoreM[:, b_], in0=scoreM[:, b_],
                                                           scalar1=negp[:, 0:1], scalar2=None, op0=ALU.add),
                   reads=['score', 'negp'], writes=['score'])
                P.dma('pool', GTb[:, 0:64], ckirow, reads=['idx'], writes=['x1'], gather_idx=idxT[b_][:, :])
                pbk_, pkk = P.bank()
                pv_ = pbk_[:].bitcast(BF16)
                op('pe', lambda e: e.transpose(out=pv_[0:64, 0:128], in_=GTb[:, 0:64], identity=identb[:]),
                   reads=['x1', 'identb'], writes=[pkk])
                evac(TCt[0:64, :], pv_[0:64, 0:128], [pkk], ['TCt'])
                pbs, pks = P.bank()
                op('pe', lambda e: e.matmul(pbs[:, 0:32], lhsT=TCt[0:64, :], rhs=qrhs[0], start=True, stop=True),
                   reads=['TCt', 'QI'], writes=[pks])
                op('pe', lambda e: e.matmul(pbs[0:16, 32:64], lhsT=kiTs[0:64, 0:16], rhs=qrhs[0],
                                            start=True, stop=True), reads=['kiTs', 'QI'], writes=[pks])
                score_from(pbs[:, 0:32].rearrange("p (i h q) -> p i h q", h=8, q=4), pks, 128, 1,
                           scoreX[:, b_, 0:1, :], b_, scrE, 'tmpf0')
                score_from(pbs[0:16, 32:64].rearrange("p (i h q) -> p i h q", h=8, q=4), pks, 16, 1,
                           scoreX[0:16, b_, 1:2, :], b_, scrE, 'tmpf0')
                op('dve', lambda e, b_=b_: e.tensor_reduce(out=m1[:, 4 + b_:5 + b_], in_=scoreX[:, b_, 0, :], axis=AX.X,
                                                           op=ALU.max, apply_absolute_value=True),
                   reads=['scoreX'], writes=['m1'])
                op('dve', lambda e, b_=b_: e.tensor_reduce(out=m1[0:16, 8 + b_:9 + b_], in_=scoreX[0:16, b_, 1, :],
                                                           axis=AX.X, op=ALU.max, apply_absolute_value=True),
                   reads=['scoreX'], writes=['m1'])
                op('dve', lambda e, b_=b_: e.tensor_tensor(
                    out=scoreX[0:16, b_, 1, :], in0=scoreX[0:16, b_, 1, :],
                    in1=negnew[:].rearrange("p (b q) -> p b q", b=4)[:, b_, :], op=ALU.add),
                   reads=['scoreX', 'negnew'], writes=['scoreX'])
            if cut <= 35:
                return
            op('dve', lambda e: e.tensor_reduce(out=m1[:, 16:20], in_=m1[:, 0:12].rearrange("p (c b) -> p b c", b=4),
                                                axis=AX.X, op=ALU.max), reads=['m1'], writes=['m1'])
            pb, pk = P.bank()
            op('pe', lambda e: e.transpose(out=pb[0:4, 0:128], in_=m1[:, 16:20], identity=C['identf'][:]),
               reads=['m1', 'C_identf'], writes=[pk])
            op('dve', lambda e: e.tensor_reduce(out=mT32[0:4, 0:1], in_=pb[0:4, 0:128], axis=AX.X, op=ALU.max),
               reads=[pk], writes=['mT32'])
            op('dve', lambda e: e.tensor_scalar(out=dg32[0:4, 0:4], in0=C['identf'][0:4, 0:4], scalar1=mT32[0:4, 0:1],
                                                scalar2=None, op0=ALU.mult), reads=['mT32', 'C_identf'], writes=['dg32'])
            pb, pk = P.bank()
            op('pe', lambda e: e.matmul(pb[:, 0:4], lhsT=C['onesf'][0:4, :], rhs=dg32[0:4, 0:4], start=True, stop=True),
               reads=['dg32', 'C_onesf'], writes=[pk])
            lo16 = bq[:, 0:16]
            mid16 = bq[:, 16:32]
            pw16 = bq[:, 32:48]
            cp16 = bq[:, 48:64]
            w016 = bq[:, 64:80]
            bd4 = bq[:, 80:84]
            op('dve', lambda e: e.tensor_scalar(out=bd4, in0=pb[:, 0:4], scalar1=1.0, scalar2=None, op0=ALU.add),
               reads=[pk], writes=['bq'])
            bdb = bd4.unsqueeze(2).to_broadcast([128, 4, 4])
            op('dve', lambda e: e.tensor_scalar(out=lo16.rearrange("p (b q) -> p b q", b=4), in0=bdb, scalar1=-1.0,
                                                scalar2=None, op0=ALU.mult), reads=['bq'], writes=['bq'])
            op('dve', lambda e: e.tensor_scalar(out=w016.rearrange("p (b q) -> p b q", b=4), in0=bdb, scalar1=2.0,
                                                scalar2=None, op0=ALU.mult), reads=['bq'], writes=['bq'])
            op('dve', lambda e: e.tensor_tensor(out=WW, in0=w016.unsqueeze(1).to_broadcast([128, NIT, 16]),
                                                in1=C['pw2'][:, 0:NIT].unsqueeze(2).to_broadcast([128, NIT, 16]), op=ALU.mult),
               reads=['bq', 'C_pw2'], writes=['osb'])
            cx = tmpf[0][:, 0:32].rearrange("p (b c q) -> p b c q", b=4, c=2)
            midp = P.banks[6][:, 0:16]

            def thr_bc(t16, nr):
                return t16.rearrange("p (b q) -> p b q", b=4).unsqueeze(2).to_broadcast([128, 4, nr, 4])
            for it in range(NIT + 1):
                final = (it == NIT)
                if not final:
                    op('dve', lambda e, it=it: e.tensor_tensor(out=midp, in0=lo16, in1=WW[:, it, :], op=ALU.add),
                       reads=['bq', 'osb'], writes=['ps6'])
                thr = lo16 if final else midp
                op('dve', lambda e, thr=thr: e.tensor_tensor(out=maskM, in0=scoreM, in1=thr_bc(thr, 128), op=ALU.is_ge),
                   reads=['score', 'bq', 'ps6'], writes=['junkb'])
                if final:
                    op('dve', lambda e, thr=thr: e.tensor_tensor(out=maskX[:], in0=scoreX[:], in1=thr_bc(thr, 2),
                                                                 op=ALU.is_ge), reads=['scoreX', 'bq'], writes=['maskX'])
                    break
                op('dve', lambda e: e.tensor_reduce(out=cp16.rearrange("p (b q) -> p b q", b=4),
                                                    in_=maskM.rearrange("p b r q -> p b q r"), axis=AX.X, op=ALU.add),
                   reads=['junkb'], writes=['bq'])
                op('dve', lambda e, thr=thr: e.tensor_tensor(out=cx, in0=scoreX[:], in1=thr_bc(thr, 2), op=ALU.is_ge),
                   reads=['scoreX', 'bq', 'ps6'], writes=['tmpf0'])
                for c_ in range(2):
                    op('dve', lambda e, c_=c_: e.tensor_tensor(out=cp16.rearrange("p (b q) -> p b q", b=4),
                                                               in0=cp16.rearrange("p (b q) -> p b q", b=4),
                                                               in1=cx[:, :, c_, :], op=ALU.add),
                       reads=['bq', 'tmpf0'], writes=['bq'])
                pb, pk = P.bank()
                op('pe', lambda e, pb=pb: e.matmul(pb[:, 0:16], lhsT=C['onesf'][:], rhs=cp16, start=True, stop=True),
                   reads=['bq', 'C_onesf'], writes=[pk])
                op('dve', lambda e, pb=pb, it=it: e.scalar_tensor_tensor(out=pw16, in0=pb[:, 0:16], scalar=TOPK - 0.5,
                                                                         in1=WW[:, it, :], op0=ALU.is_ge, op1=ALU.mult),
                   reads=[pk, 'osb'], writes=['bq'])
                op('dve', lambda e: e.tensor_tensor(out=lo16, in0=lo16, in1=pw16, op=ALU.add), reads=['bq'],
                   writes=['bq'])
            if cut <= 37:
                return

            op('pool', lambda e: e.memset(QAz[:], 0.0), writes=['QAz'])
            for b_ in range(4):
                for g in range(2):
                    op('act', lambda e, b_=b_, g=g: e.copy(
                        out=QAz[g * 64:(g + 1) * 64, b_, g * 16:(g + 1) * 16].rearrange("p (f q) -> p f q", f=4),
                        in_=qaT[g * 64:(g + 1) * 64, :, 4 * b_:4 * b_ + 4]), reads=['qaT'], writes=['QAz'])
            first_oa = [True]
            for b_ in range(4):
                qz = QAz[:, b_, :]
                for c_ in range(8):
                    G, gk = gather(ck8, idxK[b_][c_])
                    tci = tcc[0] % 2
                    tcc[0] += 1
                    transpose_chunk(G, gk, 16, TCh[tci], TCK[tci])
                    pbl, pkl = P.bank()
                    for r in range(16):
                        op('pe', lambda e, r=r: e.matmul(pbl[:, r * 32:(r + 1) * 32], lhsT=TCh[tci][:, r, :], rhs=qz,
                                                         start=True, stop=True), reads=[TCK[tci], 'QAz'], writes=[pkl])
                    evac(LG[:, c_ * 16:(c_ + 1) * 16, :], pbl[:, 0:512].rearrange("p (r c) -> p r c", c=32),
                         [pkl], ['WinB'])
                P.dma('pool', GTb, ckrow, reads=['idx'], writes=['x1'], gather_idx=idxT[b_][:, :])
                pbk_, pkk = P.bank()
                pv_ = pbk_[:].bitcast(BF16)
                op('pe', lambda e: e.transpose(out=pv_[:, 0:128], in_=GTb, identity=identb[:]),
                   reads=['x1', 'identb'], writes=[pkk])
                evac(TCt[:, :], pv_[:, 0:128], [pkk], ['TCt'])
                op('pool', lambda e: e.memset(LG2, 0.0), writes=['x1'])
                pbt, pkt = P.bank()
                op('pe', lambda e: e.matmul(pbt[:, 0:32], lhsT=TCt[:, :], rhs=qz, start=True, stop=True),
                   reads=['TCt', 'QAz'], writes=[pkt])
                op('pe', lambda e: e.matmul(pbt[0:16, 32:64], lhsT=kaTs[:, 0:16], rhs=qz, start=True, stop=True),
                   reads=['kaTs', 'QAz'], writes=[pkt])
                op('dve', lambda e: e.tensor_tensor(out=LG2[:, 0, :], in0=pbt[:, 0:32], in1=Bl, op=ALU.add),
                   reads=[pkt, 'x1'], writes=['x1'])
                op('dve', lambda e: e.tensor_tensor(out=LG2[0:16, 1, :], in0=pbt[0:16, 32:64],
                                                    in1=Bn[:].rearrange("p h q -> p (h q)"), op=ALU.add),
                   reads=[pkt, 'Bn', 'x1'], writes=['x1'])
                op('dve', lambda e: e.tensor_reduce(out=m1[:, 32:64], in_=LG.rearrange("p r c -> p c r"), axis=AX.X,
                                                    op=ALU.max), reads=['WinB'], writes=['m1'])
                op('dve', lambda e: e.tensor_tensor(out=m1[:, 32:64], in0=m1[:, 32:64], in1=LG2[:, 0, :], op=ALU.max),
                   reads=['m1', 'x1'], writes=['m1'])
                op('dve', lambda e: e.tensor_tensor(out=m1[:, 32:64], in0=m1[:, 32:64], in1=LG2[:, 1, :], op=ALU.max),
                   reads=['m1', 'x1'], writes=['m1'])
                pb, pk = P.bank()
                op('pe', lambda e, pb=pb: e.transpose(out=pb[0:32, 0:128], in_=m1[:, 32:64], identity=C['identf'][:]),
                   reads=['m1', 'C_identf'], writes=[pk])
                op('dve', lambda e, pb=pb: e.tensor_reduce(out=mT32[:, 1:2], in_=pb[0:32, 0:128], axis=AX.X, op=ALU.max),
                   reads=[pk], writes=['mT32'])
                op('dve', lambda e: e.tensor_scalar(out=dg32[:], in0=C['identf'][0:32, 0:32], scalar1=mT32[:, 1:2],
                                                    scalar2=None, op0=ALU.mult), reads=['mT32', 'C_identf'],
                   writes=['dg32'])
                pb, pk = P.bank()
                op('pe', lambda e, pb=pb: e.matmul(pb[:, 0:32], lhsT=C['onesf'][0:32, :], rhs=dg32[:], start=True,
                                                   stop=True), reads=['dg32', 'C_onesf'], writes=[pk])
                op('dve', lambda e, pb=pb: e.tensor_tensor(
                    out=LG, in0=LG, in1=pb[:, 0:32].unsqueeze(1).to_broadcast([128, 128, 32]), op=ALU.subtract),
                   reads=['WinB', pk], writes=['WinB'])
                op('dve', lambda e, pb=pb: e.tensor_tensor(
                    out=LG2, in0=LG2, in1=pb[:, 0:32].unsqueeze(1).to_broadcast([128, 2, 32]), op=ALU.subtract),
                   reads=['x1', pk], writes=['x1'])
                op('act', lambda e: e.activation(out=Pm, in_=LG, func=AF.Exp, scale=0.125), reads=['WinB'], writes=RK)
                op('act', lambda e: e.activation(out=P2[:], in_=LG2, func=AF.Exp, scale=0.125), reads=['x1'],
                   writes=['P2'])
                op('pool', lambda e, b_=b_: e.tensor_tensor(
                    out=Pm.rearrange("p r (h q) -> p r h q", q=4), in0=Pm.rearrange("p r (h q) -> p r h q", q=4),
                    in1=maskM[:, b_].unsqueeze(2).to_broadcast([128, 128, 8, 4]), op=ALU.mult),
                   reads=RK + ['junkb'], writes=RK)
                op('dve', lambda e, b_=b_: e.tensor_tensor(
                    out=P2[:].rearrange("p c (h q) -> p c h q", q=4), in0=P2[:].rearrange("p c (h q) -> p c h q", q=4),
                    in1=maskX[:, b_].unsqueeze(2).to_broadcast([128, 2, 8, 4]), op=ALU.mult),
                   reads=['P2', 'maskX'], writes=['P2'])
                op('dve', lambda e: e.tensor_reduce(out=m1[:, 32:64], in_=Pm.rearrange("p r c -> p c r"), axis=AX.X,
                                                    op=ALU.add), reads=RK, writes=['m1'])
                for c_ in range(2):
                    op('dve', lambda e, c_=c_: e.tensor_tensor(out=m1[:, 32:64], in0=m1[:, 32:64], in1=P2[:, c_, :],
                                                               op=ALU.add), reads=['m1', 'P2'], writes=['m1'])
                pbd, pkd = P.bank()
                for g in range(2):
                    op('pe', lambda e, g=g, pbd=pbd: e.matmul(pbd[0:16, 2 * g:2 * g + 2], lhsT=m1[:, 32 + g * 16:48 + g * 16],
                                                              rhs=C['onesf'][:, 0:2], start=True, stop=True),
                       reads=['m1', 'C_onesf'], writes=[pkd])
                op('dve', lambda e, pbd=pbd: e.reciprocal(out=den2[:, 0:4], in_=pbd[0:16, 0:4]), reads=[pkd],
                   writes=['den2'])
                for c_ in range(8):
                    G, gk = gather(cv8, idxK[b_][c_])
                    Vs = G[:, 0:2048].rearrange("p (r g d) -> p r g d", g=2, d=64)
                    for g in range(2):
                        for r in range(16):
                            rr = c_ * 16 + r
                            op('pe', lambda e, g=g, r=r, rr=rr, Vs=Vs: e.matmul(
                                POs[g][0:16, 0:64], lhsT=Pm[:, rr, g * 16:(g + 1) * 16], rhs=Vs[:, r, g, :],
                                start=(rr == 0), stop=False), reads=RK + [gk], writes=[POK[g]])
                P.dma('pool', Vt[:].rearrange("p g d -> p (g d)"), cvrow, reads=['idx'], writes=['Vt'],
                      gather_idx=idxT[b_][:, :])
                op('pool', lambda e: e.memset(oSn, 0.0), writes=['x1'])
                for g in range(2):
                    op('pe', lambda e, g=g: e.matmul(POs[g][0:16, 0:64], lhsT=P2[:, 0, g * 16:(g + 1) * 16],
                                                     rhs=Vt[:, g, :], start=False, stop=False),
                       reads=['P2', 'Vt'], writes=[POK[g]])
                    op('pe', lambda e, g=g: e.matmul(POs[g][0:16, 0:64], lhsT=P2[:, 1, g * 16:(g + 1) * 16],
                                                     rhs=vnew[:, g, :], start=False, stop=True),
                       reads=['P2', 'vnew'], writes=[POK[g]])
                    op('dve', lambda e, g=g: e.tensor_scalar(out=oSn[0:16, g, :], in0=POs[g][0:16, 0:64],
                                                             scalar1=den2[:, 2 * g:2 * g + 1], scalar2=None, op0=ALU.mult),
                       reads=[POK[g], 'den2'], writes=['x1'])
                oav = OA[0:16, 0:512].rearrange("p (g h d) -> p g h d", g=2, h=4)
                for h4 in range(4):
                    c0 = (b_ * 4 + h4) * 16
                    for g in range(2):
                        op('pe', lambda e, h4=h4, c0=c0, g=g: e.matmul(
                            oav[:, g, h4, :], lhsT=selt[:, c0:c0 + 16], rhs=oSn[:, g, :], start=first_oa[0],
                            stop=(b_ == 3 and h4 == 3 and g == 1), skip_group_check=True),
                           reads=['tmpf1', 'x1'], writes=[OAK])
                        first_oa[0] = False
                if cut <= 37.6:
                    return
            if cut <= 39:
                return
            op('dve', lambda e: e.tensor_tensor(out=ybb[0:16], in0=OA[0:16, 0:512], in1=sga[0:16], op=ALU.mult),
               reads=[OAK, 'sga'], writes=['ybb'])
            transposes(lambda f: ybb[0:16, f * 128:(f + 1) * 128], 16, 4, yT[:, 0:4, :], ['yTa'], ['ybb'])
            transposes(lambda f: YBs[0:16, f * 128:(f + 1) * 128], 16, 4, yT[:, 4:8, :], ['yTb'], ['YBs'])
            P.dma('sp', x[0:16, :], x_s.ap(), writes=[xk])
            merge(x, xk, pt[0], 'pt0', y_s.ap(), n=16)

        load_win_now(O_QB, 2048)
        n1 = ntiles if stage >= 1 else 0
        if n1 > 0:
            P.ring = [4, 5, 6, 7]
            P.brr = 0
            load_tile(0, False)
            if n1 > 1:
                load_tile(1, False)
            gens = {0: tile_ret(0)}
            next(gens[0], None)
            next(gens[0], None)
            for ti in range(n1):
                g_ = gens.pop(ti)
                if ti + 1 < n1:
                    gens[ti + 1] = tile_ret(ti + 1)
                    next(gens[ti + 1], None)
                next(g_, None)
                if ti + 1 < n1:
                    next(gens[ti + 1], None)
                if ti + 2 < n1:
                    load_tile(ti + 2, False)
                for _ in g_:
                    pass
            P.ring = None
            P.brr = 0
        if with_sample and stage >= 1:
            pass
        stv = st_p.ap().rearrange("(hh hp) k v -> hp k hh v", hp=2)
        for hp in range(2):
            P.dma('sp', stv[hp], S[hp * 64:(hp + 1) * 64, :].rearrange("p (hh v) -> p hh v", hh=4), reads=['S'])
        if with_sample and stage >= 1:
            sample_ret()
        if stage < 2:
            ntiles = 0
        load_win_now(0, O_QB)
        if ntiles > 0:
            load_tile(0, True)

        def do_merge(tj):
            bj = tj % 2
            merge2(xt[bj], 'xt%d' % bj, pt[bj], 'pt%d' % bj, y_p.ap()[tj * 128:(tj + 1) * 128, :])
        if ntiles > 0:
            prenorm(xt[0], 'xt0')
        for ti in range(ntiles):
            gen = tile_dsa(ti)
            next(gen, None)
            if ti > 0 and cut > 18:
                do_merge(ti - 1)
            if ti + 1 < ntiles:
                load_tile(ti + 1, True)
            if WARM > 0 and cut > 18 and P.nbank <= 5:
                est_us = NIT * (1.0 + (ti + 1) * 128 / 960.0) - (12.0 if ti > 0 else 0.0)
                for _d in range(max(0, int(est_us * WARM))):
                    op('pe', lambda e: e.matmul(P.banks[5][:, 0:512], lhsT=identb[:], rhs=BT[:, 0, 0, :],
                                                start=True, stop=True), reads=['identb', 'BT'], writes=['ps5'])
            for st_ in gen:
                if st_ == 'bisected' and ti + 1 < ntiles:
                    prenorm(xt[(ti + 1) % 2], 'xt%d' % ((ti + 1) % 2))
        if ntiles > 0 and cut > 18:
            do_merge(ntiles - 1)
        if with_sample and stage >= 2:
            sample_dsa()
        P.finish()
    return nc, consts


_CACHE = {}


def kernel(x_prompt, x_sample, cache_k, cache_v, cache_kidx, state_ret, page_table, p_prompt, p_sample,
           rel_bias, w_in, w_out, g_pre, g_post, w_ple_up, w_ple_gate):
    if 'nc' not in _CACHE:
        _CACHE['nc'] = build()
    nc, consts = _CACHE['nc']
    f = lambda a: np.ascontiguousarray(np.asarray(a, dtype=np.float32))
    ck = f(cache_k[0]).reshape(NPOOL, 16384)
    cv = f(cache_v[0]).reshape(NPOOL, 16384)
    cki = f(cache_kidx[0]).reshape(NPOOL, 8192)
    shared = {
        'w_in': f(w_in[0]), 'w_out': f(w_out[0]), 'w_gate': f(w_ple_gate[0]), 'w_up': f(w_ple_up[0]),
        'g_pre': f(g_pre), 'g_post': f(g_post), 'rel_bias': f(rel_bias), 'c_k': ck, 'c_v': cv, 'c_ki': cki,
    }
    for k, v in consts.items():
        shared['c_' + k] = v
    in_maps = []
    for c in range(8):
        sl = slice(4 * c, 4 * c + 4)
        m = dict(shared)
        m['x_p'] = f(x_prompt[c])
        m['p_p'] = f(p_prompt[0, c])
        m['x_s'] = f(x_sample[sl]).reshape(16, D)
        m['p_s'] = f(p_sample[0, sl]).reshape(16, 256)
        m['st_in'] = f(state_ret[0, sl])
        m['pt_s'] = np.ascontiguousarray(np.asarray(page_table[sl]).astype(np.int32))
        in_maps.append(m)
    res = run_bass_kernel_spmd(nc, in_maps, core_ids=list(range(8)))
    r = res.results
    cat = lambda k: np.stack([r[c][k] for c in range(8)])
    y_p = cat('y_p')
    k_p = cat('k_p').reshape(1, 8, T, 2, 64)
    v_p = cat('v_p').reshape(1, 8, T, 2, 64)
    ki_p = cat('ki_p').reshape(1, 8, T, 64)
    st_p = cat('st_p').reshape(1, 8, 8, 64, 64)
    y_s = cat('y_s').reshape(32, 4, D)
    k_s = cat('k_s').reshape(1, 32, 4, 2, 64)
    v_s = cat('v_s').reshape(1, 32, 4, 2, 64)
    ki_s = cat('ki_s').reshape(1, 32, 4, 64)
    st_s = cat('st_s').reshape(1, 32, 8, 64, 64)
    return (y_p, y_s, k_p, v_p, ki_p, st_p, k_s, v_s, ki_s, st_s)
```
